# Optimizing a Trainium2 kernel written in Bass

```python
import math
import jax, jax.numpy as jnp
from jax import lax
import numpy as np

D_MODEL = 2048
BATCH = 16
SEQ = 256
DEPTH = 4
DEC_BATCH = 8
DEC_SEQ = 2048
PAST_LEN = 512

GRID_W = 64
HEAD_DIM = 128
ATTN_Q_HEADS = 8
ATTN_KV_HEADS = 2
DIFF_HEADS = 4
DIFF_QK_DIM = 64
DIFF_V_DIM = 128
FOURIER_GROUPS = 4
FOURIER_GROUP_DIM = 128
MIX_WIDTH = ATTN_Q_HEADS * HEAD_DIM + DIFF_HEADS * DIFF_V_DIM + FOURIER_GROUPS * FOURIER_GROUP_DIM
D_FF = 4 * D_MODEL
ROPE_THETA = 10000.0
Q_BLOCK = 128
EPS = 1e-6
N_MOD = 6

W_QA = ATTN_Q_HEADS * HEAD_DIM
W_KA = ATTN_KV_HEADS * HEAD_DIM
W_VA = ATTN_KV_HEADS * HEAD_DIM
W_QB = DIFF_HEADS * 2 * DIFF_QK_DIM
W_KB = DIFF_HEADS * 2 * DIFF_QK_DIM
W_VB = DIFF_HEADS * DIFF_V_DIM
W_F = FOURIER_GROUPS * FOURIER_GROUP_DIM
IN_WIDTH = W_QA + W_KA + W_VA + W_QB + W_KB + W_VB + W_F

kernel_name = "hybrid_diffusion_prefix_trunk_step"


def rms_norm(x, g):
    xf = x.astype(jnp.float32)
    y = xf * lax.rsqrt(jnp.mean(jnp.square(xf), axis=-1, keepdims=True) + EPS)
    return (y * g.astype(jnp.float32)).astype(x.dtype)


def adaln(cond, w_ada_l, b_ada_l):
    m = jax.nn.silu(cond) @ w_ada_l + b_ada_l
    return jnp.split(m[:, None, :], N_MOD, axis=-1)


def axial_rope_tables(n_tokens, rot_dim):
    rows = n_tokens // GRID_W
    row = jnp.repeat(jnp.arange(rows), GRID_W).astype(jnp.float32)
    col = jnp.tile(jnp.arange(GRID_W), rows).astype(jnp.float32)
    nf = rot_dim // 4
    inv = ROPE_THETA ** (-jnp.arange(nf, dtype=jnp.float32) / nf)
    ang = jnp.concatenate([row[:, None] * inv, col[:, None] * inv], axis=-1)
    return jnp.cos(ang), jnp.sin(ang)


def apply_axial_rope(x, cos, sin):
    half = x.shape[-1] // 2
    nf = half // 2
    shape = (1, cos.shape[0]) + (1,) * (x.ndim - 3) + (half,)
    cos = cos.reshape(shape)
    sin = sin.reshape(shape)
    xf = x.astype(jnp.float32)

    def rot(xh, c, s):
        x1, x2 = xh[..., :nf], xh[..., nf:]
        return jnp.concatenate([x1 * c - x2 * s, x2 * c + x1 * s], axis=-1)

    out_r = rot(xf[..., :half], cos[..., :nf], sin[..., :nf])
    out_c = rot(xf[..., half:], cos[..., nf:], sin[..., nf:])
    return jnp.concatenate([out_r, out_c], axis=-1).astype(x.dtype)


def sweep_query_blocks(fn, q):
    b, t = q.shape[:2]
    nb = t // Q_BLOCK
    qb = jnp.moveaxis(q.reshape((b, nb, Q_BLOCK) + q.shape[2:]), 1, 0)
    ob = jnp.moveaxis(lax.map(fn, qb), 0, 1)
    return ob.reshape((b, t) + ob.shape[3:])


def gqa_attention(q, k, v):
    b, t, hq, d = q.shape
    hkv = k.shape[2]
    g = hq // hkv
    scale = d ** -0.5
    qg = q.reshape(b, t, hkv, g, d)

    def blk(qb):
        s = jnp.einsum('bqhgd,bkhd->bhgqk', qb, k).astype(jnp.float32) * scale
        p = jax.nn.softmax(s, axis=-1)
        return jnp.einsum('bhgqk,bkhd->bqhgd', p.astype(v.dtype), v)

    return sweep_query_blocks(blk, qg).reshape(b, t, hq * d)


def diff_attention(q, k, v, lam):
    scale = q.shape[-1] ** -0.5

    def blk(qb):
        s = jnp.einsum('bqhcd,bkhcd->bhcqk', qb, k).astype(jnp.float32) * scale
        p = jax.nn.softmax(s, axis=-1)
        pd = p[:, :, 0] - lam * p[:, :, 1]
        return jnp.einsum('bhqk,bkhd->bqhd', pd.astype(v.dtype), v)

    return sweep_query_blocks(blk, q)


def fourier_mix(f, w_fourier_l):
    b, t, _ = f.shape
    fg = f.reshape(b, t, FOURIER_GROUPS, FOURIER_GROUP_DIM).astype(jnp.float32)
    spec = jnp.fft.fft2(fg, axes=(1, 3), norm="ortho").real.astype(f.dtype)
    out = jnp.einsum('btgc,gcd->btgd', spec, w_fourier_l)
    return out.reshape(b, t, W_F)


def project(h, w_in_l, qn_a, kn_a, qn_b, kn_b):
    b, t, _ = h.shape
    z = h @ w_in_l
    i0 = W_QA
    i1 = i0 + W_KA
    i2 = i1 + W_VA
    i3 = i2 + W_QB
    i4 = i3 + W_KB
    i5 = i4 + W_VB
    qa = z[..., :i0].reshape(b, t, ATTN_Q_HEADS, HEAD_DIM)
    ka = z[..., i0:i1].reshape(b, t, ATTN_KV_HEADS, HEAD_DIM)
    va = z[..., i1:i2].reshape(b, t, ATTN_KV_HEADS, HEAD_DIM)
    qb = z[..., i2:i3].reshape(b, t, DIFF_HEADS, 2, DIFF_QK_DIM)
    kb = z[..., i3:i4].reshape(b, t, DIFF_HEADS, 2, DIFF_QK_DIM)
    vb = z[..., i4:i5].reshape(b, t, DIFF_HEADS, DIFF_V_DIM)
    f = z[..., i5:]
    return (rms_norm(qa, qn_a), rms_norm(ka, kn_a), va,
            rms_norm(qb, qn_b), rms_norm(kb, kn_b), vb, f)


def trunk_layer(x, cond, lambda_init, ctx, rope, lw):
    (w_ada_l, b_ada_l, norm_mix_g_l, norm_mlp_g_l, w_in_l, q_norm_a_l, k_norm_a_l,
     q_norm_b_l, k_norm_b_l, lambda_q1_l, lambda_k1_l, lambda_q2_l, lambda_k2_l,
     subln_g_l, w_fourier_l, w_out_l, w_mlp_in_l, w_mlp_out_l) = lw
    shift1, scale1, gate1, shift2, scale2, gate2 = adaln(cond, w_ada_l, b_ada_l)

    h = rms_norm(x, norm_mix_g_l) * (1.0 + scale1) + shift1
    qa, ka, va, qb, kb, vb, f = project(h, w_in_l, q_norm_a_l, k_norm_a_l, q_norm_b_l, k_norm_b_l)
    ctx_out = (ka, va, kb, vb)

    if ctx is None:
        keys_a, vals_a, keys_b, vals_b = ka, va, kb, vb
    else:
        (cos_a, sin_a), (cos_b, sin_b) = rope
        qa = apply_axial_rope(qa, cos_a, sin_a)
        qb = apply_axial_rope(qb, cos_b, sin_b)
        keys_a = jnp.concatenate([apply_axial_rope(ka, cos_a, sin_a), ctx[0].astype(ka.dtype)], axis=1)
        vals_a = jnp.concatenate([va, ctx[1].astype(va.dtype)], axis=1)
        keys_b = jnp.concatenate([apply_axial_rope(kb, cos_b, sin_b), ctx[2].astype(kb.dtype)], axis=1)
        vals_b = jnp.concatenate([vb, ctx[3].astype(vb.dtype)], axis=1)

    out_a = gqa_attention(qa, keys_a, vals_a)

    lam = (jnp.exp(jnp.sum(lambda_q1_l.astype(jnp.float32) * lambda_k1_l.astype(jnp.float32)))
           - jnp.exp(jnp.sum(lambda_q2_l.astype(jnp.float32) * lambda_k2_l.astype(jnp.float32)))
           + lambda_init)
    ob = diff_attention(qb, keys_b, vals_b, lam)
    ob = rms_norm(ob, subln_g_l) * (1.0 - lambda_init)
    out_b = ob.reshape(ob.shape[0], ob.shape[1], W_VB)

    out_c = fourier_mix(f, w_fourier_l)

    mix = jnp.concatenate([out_a, out_b, out_c], axis=-1) @ w_out_l
    x = x + gate1 * mix

    h2 = rms_norm(x, norm_mlp_g_l) * (1.0 + scale2) + shift2
    mlp = jnp.square(jax.nn.relu(h2 @ w_mlp_in_l)) @ w_mlp_out_l
    x = x + gate2 * mlp
    return x, ctx_out


def setup_inputs(seed: int = 0) -> dict:
    key = jax.random.key(seed)
    ks = jax.random.split(key, 32)
    f32 = jnp.float32
    nrm = lambda k, shape, s: jax.random.normal(k, shape, f32) * s
    gain = lambda k, shape: 1.0 + 0.02 * jax.random.normal(k, shape, f32)
    return {
        "x_prompt": nrm(ks[0], (BATCH, SEQ, D_MODEL), 1.0),
        "x_sample": nrm(ks[1], (DEC_BATCH, DEC_SEQ, D_MODEL), 1.0),
        "cache_attn_k": nrm(ks[2], (DEC_BATCH, DEPTH, PAST_LEN, ATTN_KV_HEADS, HEAD_DIM), 1.0),
        "cache_attn_v": nrm(ks[3], (DEC_BATCH, DEPTH, PAST_LEN, ATTN_KV_HEADS, HEAD_DIM), 1.0),
        "cache_diff_k": nrm(ks[4], (DEC_BATCH, DEPTH, PAST_LEN, DIFF_HEADS, 2, DIFF_QK_DIM), 1.0),
        "cache_diff_v": nrm(ks[5], (DEC_BATCH, DEPTH, PAST_LEN, DIFF_HEADS, DIFF_V_DIM), 1.0),
        "c": nrm(ks[6], (DEC_BATCH, D_MODEL), 1.0),
        "c_ctx": nrm(ks[7], (D_MODEL,), 1.0),
        "w_ada": nrm(ks[8], (DEPTH, D_MODEL, N_MOD * D_MODEL), D_MODEL ** -0.5),
        "b_ada": nrm(ks[9], (DEPTH, N_MOD * D_MODEL), 0.01),
        "norm_mix_g": gain(ks[10], (DEPTH, D_MODEL)),
        "norm_mlp_g": gain(ks[11], (DEPTH, D_MODEL)),
        "w_in": nrm(ks[12], (DEPTH, D_MODEL, IN_WIDTH), D_MODEL ** -0.5),
        "q_norm_a": gain(ks[13], (DEPTH, HEAD_DIM)),
        "k_norm_a": gain(ks[14], (DEPTH, HEAD_DIM)),
        "q_norm_b": gain(ks[15], (DEPTH, DIFF_QK_DIM)),
        "k_norm_b": gain(ks[16], (DEPTH, DIFF_QK_DIM)),
        "lambda_q1": nrm(ks[17], (DEPTH, DIFF_QK_DIM), 0.1),
        "lambda_k1": nrm(ks[18], (DEPTH, DIFF_QK_DIM), 0.1),
        "lambda_q2": nrm(ks[19], (DEPTH, DIFF_QK_DIM), 0.1),
        "lambda_k2": nrm(ks[20], (DEPTH, DIFF_QK_DIM), 0.1),
        "subln_g": gain(ks[21], (DEPTH, DIFF_V_DIM)),
        "w_fourier": nrm(ks[22], (DEPTH, FOURIER_GROUPS, FOURIER_GROUP_DIM, FOURIER_GROUP_DIM), FOURIER_GROUP_DIM ** -0.5),
        "w_out": nrm(ks[23], (DEPTH, MIX_WIDTH, D_MODEL), MIX_WIDTH ** -0.5),
        "w_mlp_in": nrm(ks[24], (DEPTH, D_MODEL, D_FF), D_MODEL ** -0.5),
        "w_mlp_out": nrm(ks[25], (DEPTH, D_FF, D_MODEL), D_FF ** -0.5),
    }


def reference(x_prompt, x_sample, cache_attn_k, cache_attn_v, cache_diff_k, cache_diff_v, c,
              c_ctx, w_ada, b_ada, norm_mix_g, norm_mlp_g, w_in, q_norm_a, k_norm_a, q_norm_b,
              k_norm_b, lambda_q1, lambda_k1, lambda_q2, lambda_k2, subln_g, w_fourier, w_out,
              w_mlp_in, w_mlp_out):
    n_lat = x_sample.shape[1]
    rope = (axial_rope_tables(n_lat, HEAD_DIM), axial_rope_tables(n_lat, DIFF_QK_DIM))
    cond_ctx = c_ctx[None, :]

    xp = x_prompt
    xs = x_sample
    ka_list, va_list, kb_list, vb_list = [], [], [], []
    for l in range(DEPTH):
        lw = (w_ada[l], b_ada[l], norm_mix_g[l], norm_mlp_g[l], w_in[l], q_norm_a[l], k_norm_a[l],
              q_norm_b[l], k_norm_b[l], lambda_q1[l], lambda_k1[l], lambda_q2[l], lambda_k2[l],
              subln_g[l], w_fourier[l], w_out[l], w_mlp_in[l], w_mlp_out[l])
        lambda_init = 0.8 - 0.6 * math.exp(-0.3 * l)
        xp, (ka, va, kb, vb) = trunk_layer(xp, cond_ctx, lambda_init, None, None, lw)
        ka_list.append(ka)
        va_list.append(va)
        kb_list.append(kb)
        vb_list.append(vb)
        ctx = (cache_attn_k[:, l], cache_attn_v[:, l], cache_diff_k[:, l], cache_diff_v[:, l])
        xs, _ = trunk_layer(xs, c, lambda_init, ctx, rope, lw)

    state_attn_k = jnp.stack(ka_list, axis=1)
    state_attn_v = jnp.stack(va_list, axis=1)
    state_diff_k = jnp.stack(kb_list, axis=1)
    state_diff_v = jnp.stack(vb_list, axis=1)
    return (xp, xs, state_attn_k, state_attn_v, state_diff_k, state_diff_v)
```

```python
import math
import numpy as np
import concourse.bass as bass
import concourse.mybir as mybir
from concourse.bass_utils import run_bass_kernel_spmd

F32 = mybir.dt.float32
BF16 = mybir.dt.bfloat16
AF = mybir.ActivationFunctionType
ALU = mybir.AluOpType
AX = mybir.AxisListType
ENGS = ["pe", "act", "dve", "pool", "sp"]

D = 2048
NCH = 16
TS = 2048
TP = 256
NTOK = 2560
DFF = 8192
INW = 3584
EPS = 1e-6
SEM_EPOCH = 30000


class Buf:
    __slots__ = ("name", "w", "r")

    def __init__(self, name):
        self.name = name
        self.w = None
        self.r = []


class Sched:
    def __init__(self):
        self.ops = {e: [] for e in ENGS}
        self.dma_cnt = {}
        self.last_c = {e: None for e in ENGS}

    def op(self, eng, meth, *args, reads=(), writes=(), dma=None, **kw):
        fn = lambda e: getattr(e, meth)(*args, **kw)
        deps = []
        for b in reads:
            if b.w is not None:
                deps.append(b.w)
        for b in writes:
            if b.w is not None:
                deps.append(b.w)
            for t in b.r:
                if t[0] == "c" and t[1] == eng and dma is None:
                    continue
                deps.append(t)
        idx = len(self.ops[eng])
        if dma is None:
            tok = ("c", eng, idx)
            if eng == "pe":
                deps = [t for t in deps if not (t[0] == "c" and t[1] == "pe")]
            self.last_c[eng] = tok
        else:
            n = self.dma_cnt.get(dma, 0) + 1
            self.dma_cnt[dma] = n
            tok = ("d", dma, n)
        self.ops[eng].append(dict(fn=fn, deps=deps, tok=tok, marked=False, dma=dma))
        for b in reads:
            b.r.append(tok)
        for b in writes:
            b.w = tok
            b.r = []
        return tok

    def barrier(self):
        toks = [t for t in self.last_c.values() if t is not None]
        toks += [("d", ch, n) for ch, n in self.dma_cnt.items()]
        for e in ENGS:
            self.ops[e].append(dict(fn=lambda eng: eng.nop(), deps=list(toks), tok=("n", e, len(self.ops[e])), marked=False, dma=None))

    def finalize(self):
        for e in ENGS:
            for o in self.ops[e]:
                for t in o["deps"]:
                    if t[0] == "c":
                        self.ops[t[1]][t[2]]["marked"] = True
        self.cnt = {}
        for e in ENGS:
            c = 0
            arr = []
            for o in self.ops[e]:
                if o["marked"]:
                    c += 1
                arr.append(c)
            self.cnt[e] = arr
        self.sem_keys = set()
        nw = 0
        for e in ENGS:
            known = {}
            for o in self.ops[e]:
                w = {}
                for t in o["deps"]:
                    if t[0] == "c":
                        c = self.cnt[t[1]][t[2]]
                        ep = (c - 1) // SEM_EPOCH
                        key = ("e", t[1], ep)
                        val = c - ep * SEM_EPOCH
                    else:
                        ep = (t[2] - 1) // 1500
                        key = ("d", t[1], ep)
                        val = 16 * (t[2] - ep * 1500)
                    if val > w.get(key, 0):
                        w[key] = val
                ws = []
                for key, val in w.items():
                    if val > known.get(key, 0):
                        known[key] = val
                        ws.append((key, val))
                        self.sem_keys.add(key)
                o["waits"] = ws
                nw += len(ws)
        for e in ENGS:
            for i, o in enumerate(self.ops[e]):
                if o["dma"] is not None:
                    n = o["tok"][2]
                    ep = (n - 1) // 1500
                    o["inc"] = ("d", o["dma"], ep)
                    self.sem_keys.add(o["inc"])
                elif o["marked"]:
                    c = self.cnt[e][i]
                    o["inc"] = ("e", e, (c - 1) // SEM_EPOCH)
                    self.sem_keys.add(o["inc"])
                else:
                    o["inc"] = None
        self.nwaits = nw

    def run(self, nc):
        from contextlib import ExitStack

        with ExitStack() as st:
            sems = {}
            for i, key in enumerate(sorted(self.sem_keys)):
                sems[key] = st.enter_context(nc.semaphore("s%d" % i))
            block = st.enter_context(nc.Block())

            def make(e):
                def body(eng):
                    for o in self.ops[e]:
                        for key, val in o["waits"]:
                            eng.wait_ge(sems[key], val)
                        ins = o["fn"](eng)
                        if o["inc"] is not None:
                            ins.then_inc(sems[o["inc"]], 16 if o["dma"] is not None else 1)

                return body

            block.tensor(make("pe"))
            block.scalar(make("act"))
            block.vector(make("dve"))
            block.gpsimd(make("pool"))
            block.sync(make("sp"))


class Rot:
    def __init__(self, items):
        self.items = items
        self.i = 0

    def next(self):
        it = self.items[self.i % len(self.items)]
        self.i += 1
        return it


SUBT = {0: [(0, 512, 0), (512, 512, 0), (1024, 256, 0)], 1: [(0, 256, 0), (256, 512, 0), (768, 512, 1)]}


class Prog:
    def __init__(self, depth):
        self.depth = depth
        nc = self.nc = bass.Bass("TRN2", target_bir_lowering=False)
        self.S = Sched()
        L = depth

        def din(name, shape, dt=F32):
            return nc.dram_tensor(name, list(shape), dt, kind="ExternalInput").ap()

        def dout(name, shape, dt=F32):
            return nc.dram_tensor(name, list(shape), dt, kind="ExternalOutput").ap()

        self.x_s = din("x_s", [TS, D])
        self.x_p = din("x_p", [2 * TP, D])
        self.cak = din("cak", [4, 512, 256])
        self.cav = din("cav", [4, 512, 256])
        self.cbk = din("cbk", [4, 512, 512])
        self.cbv = din("cbv", [4, 512, 512])
        self.condT = din("condT", [128, 16, 2])
        self.w_ada = din("w_ada", [L, D, 6 * D])
        self.b_adaT = din("b_adaT", [L, 128, 96])
        self.gT = din("gT", [L, 128, 2, 16])
        self.w_in = din("w_in", [L, D, INW])
        self.w_out = din("w_out", [L, D, D])
        self.w1 = din("w1", [L, D, DFF])
        self.w2 = din("w2", [L, DFF, D])
        self.w_f = din("w_f", [L, 4, 128, 128])
        self.smallv = din("smallv", [L, 128, 8])
        self.lamv = din("lamv", [L, 128, 4, 64])
        self.cmat = din("cmat", [128, 9, 128])
        self.rope = din("rope", [4, 128, TS])
        self.dftS = din("dftS", [2, TS, TS])
        self.dftP = din("dftP", [2, TP, TP])
        self.y_s = dout("y_s", [TS, D])
        self.y_p = dout("y_p", [2 * TP, D])
        self.sKA = dout("sKA", [2, L, TP, 256])
        self.sVA = dout("sVA", [2, L, TP, 256])
        self.sKB = dout("sKB", [2, L, TP, 512])
        self.sVB = dout("sVB", [2, L, TP, 512])
        self.xT = nc.dram_tensor("xT_scr", [NCH, 128, NTOK], F32, kind="Internal").ap()
        self.hT = nc.dram_tensor("hT_scr", [NCH, 128, NTOK], BF16, kind="Internal").ap()
        self.mixT = nc.dram_tensor("mixT_scr", [NCH, 128, NTOK], BF16, kind="Internal").ap()

        self.off = 16512

        def sb(name, shape, dt, at=None):
            nbytes = int(np.prod(shape[1:])) * (4 if dt == F32 else 2)
            nbytes = (nbytes + 31) // 32 * 32
            if at is None:
                o = self.off
                self.off += nbytes
            else:
                o = at
            assert o + nbytes <= 229344, (name, o, nbytes)
            return nc.alloc_sbuf_tensor_at(name, list(shape), dt, offset=o)

        self.cf = sb("cf", [128, 3, 128], F32)
        self.cb = sb("cb", [128, 6, 128], BF16)
        self.b_c = Buf("consts")
        self.epsc = sb("epsc", [128, 8], F32)
        self.condS = sb("condS", [128, 16, 2], BF16)
        self.condF = sb("condF", [128, 16, 2], F32)
        self.mod = [sb(f"mod{i}", [128, 96, 2], F32) for i in range(2)]
        self.a12 = [sb(f"a12{i}", [128, 2, 16, 2], F32) for i in range(2)]
        self.b_mod = [Buf("mod0"), Buf("mod1")]
        self.gTs = sb("gTs", [128, 2, 16], F32)
        self.badaS = sb("badaS", [128, 96], F32)
        self.smallS = sb("smallS", [128, 8], F32)
        self.lamS = sb("lamS", [128, 4, 64], F32)
        self.lamT = sb("lamT", [128, 2, 64], F32)
        self.lamR = sb("lamR", [128, 8], F32)
        self.b_small = Buf("small")
        self.wfS = sb("wfS", [128, 4, 128], BF16)
        self.AB = sb("AB", [128, 4, 256], BF16)
        self.b_AB = Buf("AB")
        self.b_wf = Buf("wf")
        self.tF = [sb(f"tF{i}", [128, 512], F32) for i in range(6)]
        self.tB = [sb(f"tB{i}", [128, 512], BF16) for i in range(10)]
        self.rF = Rot([(t, Buf(f"tF{i}")) for i, t in enumerate(self.tF)])
        self.rB = Rot([(t, Buf(f"tB{i}")) for i, t in enumerate(self.tB)])
        self.rOB = Rot([(sb(f"ob{i}", [128, 512], BF16), Buf(f"ob{i}")) for i in range(3)])
        self.rSO = Rot([(sb(f"so{i}", [128, 512], F32), Buf(f"so{i}")) for i in range(2)])
        self.rQT = Rot([(sb(f"qT{i}", [128, 512], BF16), Buf(f"qT{i}")) for i in range(3)])
        self.rRN = Rot([(sb(f"rN{i}", [128, 512], F32), Buf(f"rN{i}")) for i in range(2)])
        slab0 = self.off
        self.off += 4 * 8192
        self.big = [sb(f"big{j}", [128, 16, 512], BF16, at=slab0 + j * 16384) for j in range(2)]
        self.u16 = [sb(f"u16_{i}", [128, 16, 256], BF16, at=slab0 + i * 8192) for i in range(4)]
        self.u2 = [sb(f"u2_{i}", [128, 2, 2048], BF16, at=slab0 + i * 8192) for i in range(4)]
        self.b_unit = [Buf(f"unit{i}") for i in range(4)]
        self.big_i = 0
        self.unit_i = 0
        ov = self.off
        o = ov
        self.hseq = sb("hseq", [128, 16, TS], BF16, at=o); o += 16 * TS * 2
        self.KT = sb("KT", [128, 4, 2560], BF16, at=o)
        self.XAB = sb("XAB", [128, 16, 2, 512], BF16, at=o)
        self.Vs = sb("Vs", [128, 20, 512], BF16, at=o + 4 * 2560 * 2)
        o += 4 * 2560 * 2 + 20 * 512 * 2
        self.ropeT = [sb(f"ropeT{i}", [128, 4, 512], F32, at=o + i * 8192) for i in range(2)]
        o += 16384
        self.cstage = sb("cstage", [128, 4, 512], F32, at=o); o += 8192
        assert o <= 229344, o
        self.b_hseq = Buf("hseq")
        self.b_kt = Buf("kt")
        self.b_v = Buf("v")
        self.b_rope = [Buf("rope0"), Buf("rope1")]
        self.b_cst = Buf("cstage")
        self.rope_i = 0
        o = ov
        self.xs = sb("xs", [128, 16, 1280], F32, at=o); o += 16 * 1280 * 4
        self.h2 = sb("h2", [128, 16, 1280], BF16, at=o); o += 16 * 1280 * 2
        self.tm = [sb(f"tm{i}", [128, D], F32, at=o - 16 * 1280 * 2 + i * 10240) for i in range(2)]
        self.ub = sb("ub", [128, 2, 1280], BF16, at=o); o += 2 * 1280 * 2
        assert o <= 229344, o
        self.b_xs = [Buf(f"xs{c}") for c in range(16)]
        self.b_h2 = [Buf(f"h2{c}") for c in range(16)]
        self.b_u = [Buf("u0"), Buf("u1")]
        self.b_tm = [Buf("tm0"), Buf("tm1")]
        self.ps = [nc.alloc_psum_tensor(f"ps{i}", [128, 512], F32) for i in range(8)]
        self.b_ps = [Buf(f"ps{i}") for i in range(8)]
        self.rP = Rot([0, 1, 2, 3])
        self.rQ = Rot([4, 5])

    def bank(self):
        i = self.rP.next()
        return self.ps[i], self.b_ps[i]

    def bankq(self):
        i = self.rQ.next()
        return self.ps[i], self.b_ps[i]

    def load_big(self, src2d, kch, ncols):
        j = self.big_i % 2
        self.big_i += 1
        t = self.big[j]
        bufs = [self.b_unit[2 * j], self.b_unit[2 * j + 1]]
        self.S.op("pool", "dma_start", out=t[:, 0:kch, 0:ncols], in_=src2d.rearrange("(k p) n -> p k n", p=128),
                  writes=bufs, dma=f"unit{2*j}")
        return t, bufs

    def load_unit(self, src2d, view):
        i = self.unit_i % 4
        self.unit_i += 1
        t = self.u16[i] if view == 16 else self.u2[i]
        self.S.op("pool", "dma_start", out=t[:], in_=src2d.rearrange("(k p) n -> p k n", p=128),
                  writes=[self.b_unit[i]], dma=f"unit{i}")
        return t, [self.b_unit[i]]

    def rsqrt(self, rinv, brv, pq, bq, n):
        S = self.S
        S.op("act", "activation", out=rinv[:, 0:n], in_=pq[:, 0:n], func=AF.Ln, bias=self.epsc[:, 0:1], reads=[bq, self.b_c], writes=[brv])
        S.op("act", "activation", out=rinv[:, 0:n], in_=rinv[:, 0:n], func=AF.Exp, scale=-0.5, reads=[brv], writes=[brv])

    def setup(self):
        S = self.S
        S.op("dve", "memset", self.epsc[:], EPS, writes=[self.b_c])
        S.op("sp", "dma_start", out=self.cf[:], in_=self.cmat[:, 0:3, :], writes=[self.b_c], dma="cc")
        S.op("pool", "dma_start", out=self.cb[:], in_=self.cmat[:, 3:9, :], writes=[self.b_c], dma="ccp")
        S.op("sp", "dma_start", out=self.condF[:], in_=self.condT, writes=[self.b_small], dma="cs")
        S.op("act", "activation", out=self.condS[:], in_=self.condF[:], func=AF.Silu, reads=[self.b_small], writes=[self.b_c])

    def adaln(self, l):
        S = self.S
        par = l % 2
        psA, bA = self.ps[7], self.b_ps[7]
        S.op("sp", "dma_start", out=self.badaS[:], in_=self.b_adaT[l], writes=[self.b_small], dma="cs")
        S.op("sp", "dma_start", out=self.gTs[:], in_=self.gT[l], writes=[self.b_small], dma="cs")
        for sl in range(24):
            big, bb = self.load_big(self.w_ada[l][:, sl * 512:(sl + 1) * 512], 16, 512)
            for jj in range(4):
                j = sl * 4 + jj
                for k in range(16):
                    S.op("pe", "matmul", psA[:, 2 * j:2 * j + 2], big[:, k, jj * 128:(jj + 1) * 128],
                                                                              self.condS[:, k, :], start=(k == 0), stop=(k == 15),
                         reads=bb + [self.b_c], writes=[bA])
        mod = self.mod[par]
        a12 = self.a12[par]
        psv = psA[:, 0:192].rearrange("p (j i) -> p j i", i=2)
        for i in range(2):
            S.op("dve", "tensor_tensor", out=mod[:, :, i], in0=psv[:, :, i], in1=self.badaS[:], op=ALU.add,
                 reads=[bA, self.b_small], writes=[self.b_mod[par]])
        for i in range(2):
            for w, (sc0, gi) in enumerate([(16, 0), (64, 1)]):
                S.op("dve", "scalar_tensor_tensor",
                    out=a12[:, w, :, i], in0=mod[:, sc0:sc0 + 16, i], scalar=1.0, in1=self.gTs[:, gi, :], op0=ALU.add, op1=ALU.mult,
                    reads=[self.b_mod[par], self.b_small], writes=[self.b_mod[par]])

    def mvec(self, l, which, c, ci):
        par = l % 2
        if which == "a1":
            return self.a12[par][:, 0, c, ci:ci + 1]
        if which == "a2":
            return self.a12[par][:, 1, c, ci:ci + 1]
        base = {"shift1": 0, "gate1": 32, "shift2": 48, "gate2": 80}[which]
        return self.mod[par][:, base + c, ci:ci + 1]

    def layer_small(self, l):
        S = self.S
        lam_init = 0.8 - 0.6 * math.exp(-0.3 * l)
        bs = self.b_small
        S.op("sp", "dma_start", out=self.smallS[:], in_=self.smallv[l], writes=[bs], dma="cs")
        S.op("sp", "dma_start", out=self.lamS[:], in_=self.lamv[l], writes=[bs], dma="cs")
        for j in range(2):
            S.op("dve", "tensor_tensor", out=self.lamT[:, j, :], in0=self.lamS[:, 2 * j, :], in1=self.lamS[:, 2 * j + 1, :], op=ALU.mult,
                 reads=[bs], writes=[bs])
            S.op("dve", "reduce_sum", out=self.lamR[:, j:j + 1], in_=self.lamT[:, j, :], axis=AX.X, reads=[bs], writes=[bs])
            S.op("act", "activation", out=self.lamR[:, 2 + j:3 + j], in_=self.lamR[:, j:j + 1], func=AF.Exp, reads=[bs], writes=[bs])
        S.op("dve", "scalar_tensor_tensor", out=self.lamR[:, 4:5], in0=self.lamR[:, 3:4], scalar=-lam_init, in1=self.lamR[:, 2:3],
                                                       op0=ALU.add, op1=ALU.subtract, reads=[bs], writes=[bs])
        S.op("dve", "tensor_scalar", out=self.lamR[:, 5:6], in0=self.smallS[:, 4:5], scalar1=1.0 - lam_init, scalar2=None, op0=ALU.mult,
             reads=[bs], writes=[bs])
        S.op("pool", "dma_start", out=self.wfS[:], in_=self.w_f[l].rearrange("g c d -> c g d"), writes=[self.b_wf], dma="cw")
        for g in range(4):
            for ab in range(2):
                p, bp = self.bankq()
                S.op("pe", "matmul", p[:, 0:128], self.cb[:, 4 + ab, :], self.wfS[:, g, :], start=True, stop=True,
                     reads=[self.b_c, self.b_wf], writes=[bp])
                S.op("act", "activation", out=self.AB[:, g, ab * 128:(ab + 1) * 128], in_=p[:, 0:128], func=AF.Copy,
                                                                     scale=(1.0 if ab == 0 else -1.0), reads=[bp], writes=[self.b_AB])

    def norm(self, st, l, kind):
        S = self.S
        aw, sw = ("a1", "shift1") if kind == 1 else ("a2", "shift2")
        g0 = st * 1280
        for (off, n, ci) in SUBT[st]:
            pq, bq = self.bankq()
            for c in range(16):
                sq, bsq = self.rB.next()
                S.op("act", "activation", out=sq[:, 0:n], in_=self.xs[:, c, off:off + n], func=AF.Square,
                     reads=[self.b_xs[c]], writes=[bsq])
                S.op("pe", "matmul", pq[:, 0:n], self.cb[:, 0, :], sq[:, 0:n], start=(c == 0), stop=(c == 15),
                     reads=[bsq, self.b_c], writes=[bq])
            rinv, brv = self.rRN.next()
            self.rsqrt(rinv, brv, pq, bq, n)
            for c in range(16):
                t, bt = self.rF.next()
                S.op("dve", "scalar_tensor_tensor", out=t[:, 0:n], in0=self.xs[:, c, off:off + n], scalar=self.mvec(l, aw, c, ci),
                                                                                 in1=rinv[:, 0:n], op0=ALU.mult, op1=ALU.mult,
                     reads=[self.b_xs[c], brv, self.b_mod[l % 2]], writes=[bt])
                S.op("act", "activation", out=self.h2[:, c, off:off + n], in_=t[:, 0:n], func=AF.Identity, bias=self.mvec(l, sw, c, ci),
                     reads=[bt, self.b_mod[l % 2]], writes=[self.b_h2[c]])
            if kind == 1:
                S.op("sp", "dma_start", out=self.hT[:, :, g0 + off:g0 + off + n].rearrange("c p n -> p c n"), in_=self.h2[:, :, off:off + n],
                     reads=self.b_h2, dma="h2io")

    def src_rows(self, g, n, sample, prompt):
        return sample[g:g + n, :] if g < TS else prompt[g - TS:g - TS + n, :]

    def stage0(self, st):
        S = self.S
        for b in range(10):
            g = st * 1280 + b * 128
            tm, btm = self.tm[b % 2], self.b_tm[b % 2]
            btmx = [btm] + self.b_h2[4 * (b % 2):4 * (b % 2) + 4]
            S.op("sp", "dma_start", out=tm[:], in_=self.src_rows(g, 128, self.x_s, self.x_p), writes=btmx, dma=f"tm{b%2}")
            for c4 in range(4):
                p, bp = self.bank()
                for cc in range(4):
                    c = c4 * 4 + cc
                    S.op("pe", "transpose", p[:, cc * 128:(cc + 1) * 128], tm[:, c * 128:(c + 1) * 128], self.cf[:, 0, :],
                         reads=btmx + [self.b_c], writes=[bp])
                eng = "act" if c4 % 2 == 0 else "dve"
                if eng == "act":
                    S.op("act", "activation", out=self.xs[:, c4 * 4:c4 * 4 + 4, b * 128:(b + 1) * 128],
                                                                       in_=p[:].rearrange("p (a n) -> p a n", a=4), func=AF.Copy,
                         reads=[bp], writes=self.b_xs[c4 * 4:c4 * 4 + 4])
                else:
                    S.op("dve", "tensor_copy", out=self.xs[:, c4 * 4:c4 * 4 + 4, b * 128:(b + 1) * 128],
                                                                        in_=p[:].rearrange("p (a n) -> p a n", a=4),
                         reads=[bp], writes=self.b_xs[c4 * 4:c4 * 4 + 4])

    def store_xs(self, st):
        g0 = st * 1280
        self.S.op("sp", "dma_start", out=self.xT[:, :, g0:g0 + 1280].rearrange("c p n -> p c n"), in_=self.xs[:], reads=self.b_xs, dma="xsio")

    def final_out(self, st):
        S = self.S
        for b in range(10):
            g = st * 1280 + b * 128
            tm, btm = self.tm[b % 2], self.b_tm[b % 2]
            btmx = [btm] + self.b_h2[4 * (b % 2):4 * (b % 2) + 4]
            for c4 in range(4):
                p, bp = self.bank()
                for cc in range(4):
                    c = c4 * 4 + cc
                    S.op("pe", "transpose", p[:, cc * 128:(cc + 1) * 128], self.xs[:, c, b * 128:(b + 1) * 128], self.cf[:, 0, :],
                         reads=[self.b_xs[c], self.b_c], writes=[bp])
                if c4 % 2 == 0:
                    S.op("act", "activation", out=tm[:, c4 * 512:(c4 + 1) * 512], in_=p[:], func=AF.Copy, reads=[bp], writes=btmx)
                else:
                    S.op("dve", "tensor_copy", out=tm[:, c4 * 512:(c4 + 1) * 512], in_=p[:], reads=[bp], writes=btmx)
            S.op("sp", "dma_start", out=self.src_rows(g, 128, self.y_s, self.y_p), in_=tm[:], reads=btmx, dma=f"tm{b%2}")

    def stage2(self, l, st):
        S = self.S
        g0 = st * 1280
        S.op("sp", "dma_start", out=self.xs[:], in_=self.xT[:, :, g0:g0 + 1280].rearrange("c p n -> p c n"), writes=self.b_xs, dma="xsio")
        S.op("sp", "dma_start", out=self.h2[:], in_=self.mixT[:, :, g0:g0 + 1280].rearrange("c p n -> p c n"), writes=self.b_h2, dma="h2io")
        for sl in range(4):
            big, bb = self.load_big(self.w_out[l][:, sl * 512:(sl + 1) * 512], 16, 512)
            for (off, n, ci) in SUBT[st]:
                for mm in range(4):
                    m = sl * 4 + mm
                    p, bp = self.bank()
                    for k in range(16):
                        S.op("pe", "matmul", p[:, 0:n], big[:, k, mm * 128:(mm + 1) * 128], self.h2[:, k, off:off + n],
                                                                               start=(k == 0), stop=(k == 15),
                             reads=bb + [self.b_h2[k]], writes=[bp])
                    S.op("dve", "scalar_tensor_tensor", out=self.xs[:, m, off:off + n], in0=p[:, 0:n], scalar=self.mvec(l, "gate1", m, ci),
                                                                          in1=self.xs[:, m, off:off + n], op0=ALU.mult, op1=ALU.add,
                         reads=[bp, self.b_xs[m], self.b_mod[l % 2]], writes=[self.b_xs[m]])
        self.norm(st, l, 2)
        for j in range(DFF // 256):
            w1u, b1 = self.load_unit(self.w1[l][:, j * 256:(j + 1) * 256], 16)
            w2u, b2 = self.load_unit(self.w2[l][j * 256:(j + 1) * 256, :], 2)
            for (off, n, ci) in SUBT[st]:
                for hc in range(2):
                    p, bp = self.bank()
                    for k in range(16):
                        S.op("pe", "matmul", p[:, 0:n], w1u[:, k, hc * 128:(hc + 1) * 128], self.h2[:, k, off:off + n],
                                                                               start=(k == 0), stop=(k == 15),
                             reads=b1 + [self.b_h2[k]], writes=[bp])
                    r, br = self.rF.next()
                    S.op("act", "activation", out=r[:, 0:n], in_=p[:, 0:n], func=AF.Relu, reads=[bp], writes=[br])
                    S.op("pool", "tensor_tensor", out=self.ub[:, hc, off:off + n], in0=r[:, 0:n], in1=r[:, 0:n], op=ALU.mult,
                         reads=[br], writes=[self.b_u[hc]])
            for (off, n, ci) in SUBT[st]:
                for m in range(16):
                    p, bp = self.bank()
                    for hc in range(2):
                        S.op("pe", "matmul", p[:, 0:n], w2u[:, hc, m * 128:(m + 1) * 128], self.ub[:, hc, off:off + n],
                                                                               start=(hc == 0), stop=(hc == 1),
                             reads=b2 + [self.b_u[hc]], writes=[bp])
                    S.op("dve", "scalar_tensor_tensor", out=self.xs[:, m, off:off + n], in0=p[:, 0:n], scalar=self.mvec(l, "gate2", m, ci),
                                                                          in1=self.xs[:, m, off:off + n], op0=ALU.mult, op1=ALU.add,
                         reads=[bp, self.b_xs[m], self.b_mod[l % 2]], writes=[self.b_xs[m]])

    def post_qk(self, pz, bpz, n, gcol, ones_idx, rope, out_ap, out_bufs, outf=None):
        S = self.S
        sq, bsq = self.rB.next()
        S.op("act", "activation", out=sq[:, 0:n], in_=pz[:, 0:n], func=AF.Square, reads=[bpz], writes=[bsq])
        pq, bq = self.bankq()
        S.op("pe", "matmul", pq[:, 0:n], self.cb[:, ones_idx, :], sq[:, 0:n], start=True, stop=True, reads=[bsq, self.b_c], writes=[bq])
        rinv, brv = self.rF.next()
        self.rsqrt(rinv, brv, pq, bq, n)
        zg, bzg = self.rF.next()
        S.op("act", "activation", out=zg[:, 0:n], in_=pz[:, 0:n], func=AF.Copy, scale=gcol, reads=[bpz, self.b_small], writes=[bzg])
        if rope is not None:
            ridx, cos, sin, brope = rope
            pr, bpr = self.bankq()
            S.op("pe", "matmul", pr[:, 0:n], self.cf[:, ridx, :], zg[:, 0:n], start=True, stop=True, reads=[bzg, self.b_c], writes=[bpr])
            t1, bt1 = self.rF.next()
            S.op("pool", "tensor_tensor", out=t1[:, 0:n], in0=zg[:, 0:n], in1=cos, op=ALU.mult, reads=[bzg, brope], writes=[bt1])
            t2, bt2 = self.rF.next()
            S.op("dve", "tensor_tensor", out=t2[:, 0:n], in0=pr[:, 0:n], in1=sin, op=ALU.mult, reads=[bpr, brope], writes=[bt2])
            S.op("dve", "tensor_tensor", out=t1[:, 0:n], in0=t1[:, 0:n], in1=t2[:, 0:n], op=ALU.add, reads=[bt1, bt2], writes=[bt1])
            S.op("dve", "tensor_tensor", out=out_ap, in0=t1[:, 0:n], in1=rinv[:, 0:n], op=ALU.mult, reads=[bt1, brv], writes=out_bufs)
        else:
            S.op("dve", "tensor_tensor", out=out_ap, in0=zg[:, 0:n], in1=rinv[:, 0:n], op=ALU.mult, reads=[bzg, brv], writes=out_bufs)
            if outf is not None:
                of, bof = outf
                S.op("dve", "tensor_tensor", out=of[:, 0:n], in0=zg[:, 0:n], in1=rinv[:, 0:n], op=ALU.mult, reads=[bzg, brv], writes=[bof])

    def load_rope(self, t):
        i = self.rope_i % 2
        self.rope_i += 1
        rt, br = self.ropeT[i], self.b_rope[i]
        self.S.op("sp", "dma_start", out=rt[:], in_=self.rope[:, :, t * 512:(t + 1) * 512].rearrange("f p n -> p f n"), writes=[br], dma=f"rope{i}")
        return rt, br

    def attention(self, kind, h, qT, bq, q0, n, kbs, dst_chunk, gcols):
        S = self.S
        psO, bO = self.ps[6], self.b_ps[6]
        psS, bS = self.ps[7], self.b_ps[7]
        ncomp = 1 if kind == "A" else 2
        scale = 128 ** -0.5 if kind == "A" else 64 ** -0.5
        ktc = h // 4 if kind == "A" else h
        vcol = (h // 4) * 128 if kind == "A" else h * 128
        oc = []
        for comp in range(ncomp):
            r0, r1 = (0, 128) if kind == "A" else (comp * 64, comp * 64 + 64)
            for i, kb in enumerate(kbs):
                ps_s, bs = self.bank()
                S.op("pe", "matmul", ps_s[:, 0:n], self.KT[r0:r1, ktc, kb * 128:(kb + 1) * 128], qT[r0:r1, q0:q0 + n],
                                                                              start=True, stop=True, reads=[self.b_kt, bq], writes=[bs])
                pT, bpT = self.rB.next()
                S.op("act", "activation", out=pT[:, 0:n], in_=ps_s[:, 0:n], func=AF.Exp, scale=scale, reads=[bs], writes=[bpT])
                S.op("pe", "matmul", psO[:, 0:n], self.Vs[:, kb, vcol:vcol + 128], pT[:, 0:n], start=(i == 0), stop=(i == len(kbs) - 1),
                     reads=[self.b_v, bpT], writes=[bO])
                S.op("pe", "matmul", psS[:, 0:n], self.cb[:, 3, :], pT[:, 0:n], start=(i == 0), stop=(i == len(kbs) - 1),
                     reads=[self.b_c, bpT], writes=[bS])
            rs, brs = self.rF.next()
            S.op("dve", "reciprocal", out=rs[:, 0:n], in_=psS[:, 0:n], reads=[bS], writes=[brs])
            if kind == "A":
                ob, bob = self.rOB.next()
                S.op("dve", "tensor_tensor", out=ob[:, 0:n], in0=psO[:, 0:n], in1=rs[:, 0:n], op=ALU.mult, reads=[bO, brs], writes=[bob])
                S.op("sp", "dma_start", out=self.mixT[dst_chunk][:, gcols:gcols + n], in_=ob[:, 0:n], reads=[bob], dma=bob.name)
            else:
                o, bo = self.rF.next()
                S.op("dve", "tensor_tensor", out=o[:, 0:n], in0=psO[:, 0:n], in1=rs[:, 0:n], op=ALU.mult, reads=[bO, brs], writes=[bo])
                oc.append((o, bo))
        if kind == "B":
            (o0, b0), (o1, b1) = oc
            od, bod = self.rF.next()
            S.op("dve", "scalar_tensor_tensor", out=od[:, 0:n], in0=o1[:, 0:n], scalar=self.lamR[:, 4:5], in1=o0[:, 0:n], op0=ALU.mult, op1=ALU.add,
                 reads=[b0, b1, self.b_small], writes=[bod])
            sq, bsq = self.rB.next()
            S.op("act", "activation", out=sq[:, 0:n], in_=od[:, 0:n], func=AF.Square, reads=[bod], writes=[bsq])
            pq, bq2 = self.bankq()
            S.op("pe", "matmul", pq[:, 0:n], self.cb[:, 1, :], sq[:, 0:n], start=True, stop=True, reads=[bsq, self.b_c], writes=[bq2])
            rinv, brv = self.rF.next()
            self.rsqrt(rinv, brv, pq, bq2, n)
            ob, bob = self.rOB.next()
            S.op("dve", "scalar_tensor_tensor", out=ob[:, 0:n], in0=od[:, 0:n], scalar=self.lamR[:, 5:6], in1=rinv[:, 0:n], op0=ALU.mult, op1=ALU.mult,
                 reads=[bod, brv, self.b_small], writes=[bob])
            S.op("sp", "dma_start", out=self.mixT[dst_chunk][:, gcols:gcols + n], in_=ob[:, 0:n], reads=[bob], dma=bob.name)

    def stage1(self, l, grp):
        S = self.S
        isS = grp == "S"
        tok0 = 0 if isS else TS
        ntile = 4 if isS else 1
        ntb = 16 if isS else 4
        S.op("sp", "dma_start", out=self.hseq[:, :, 0:ntile * 512], in_=self.hT[:, :, tok0:tok0 + ntile * 512].rearrange("c p n -> p c n"),
             writes=[self.b_hseq], dma="hseq")
        bh = self.b_hseq

        def proj_fm(big, bb, col0, t, n=512):
            p, bp = self.bank()
            for k in range(16):
                S.op("pe", "matmul", p[:, 0:n], big[:, k, col0:col0 + 128], self.hseq[:, k, t * 512:t * 512 + n], start=(k == 0), stop=(k == 15),
                     reads=bb + [bh], writes=[bp])
            return p, bp

        import os
        skip = os.environ.get("K_PSKIP", "") if not isS else ""
        big, bb = self.load_big(self.w_in[l][:, 3072:3584], 16, 512)
        for t in range(ntile if "F" not in skip else 0):
            for g in range(4):
                p, bp = proj_fm(big, bb, g * 128, t)
                ft, bft = self.rB.next()
                S.op("act", "activation", out=ft[:], in_=p[:], func=AF.Copy, reads=[bp], writes=[bft])
                for jb in range(4):
                    px, bpx = self.bankq()
                    S.op("pe", "matmul", px[:, 0:256], ft[:, jb * 128:(jb + 1) * 128], self.AB[:, g, :], start=True, stop=True,
                         reads=[bft, self.b_AB], writes=[bpx])
                    tb = t * 4 + jb
                    S.op("dve", "tensor_copy", out=self.XAB[:, tb, :, g * 128:(g + 1) * 128],
                                                                          in_=px[:, 0:256].rearrange("p (a d) -> p a d", a=2),
                         reads=[bpx], writes=[self.b_kt, self.b_v])
        if "F" in skip:
            pass
        elif isS:
            for t in range(4):
                bc, bbc = self.load_big(self.dftS[0][:, t * 512:(t + 1) * 512], 16, 512)
                bsn, bbs = self.load_big(self.dftS[1][:, t * 512:(t + 1) * 512], 16, 512)
                for gd in range(4):
                    p, bp = self.bank()
                    for tb in range(16):
                        S.op("pe", "matmul", p[:], self.XAB[:, tb, 0, gd * 128:(gd + 1) * 128], bc[:, tb, :], start=(tb == 0), stop=False,
                             reads=bbc + [self.b_kt, self.b_v], writes=[bp])
                    for tb in range(16):
                        S.op("pe", "matmul", p[:], self.XAB[:, tb, 1, gd * 128:(gd + 1) * 128], bsn[:, tb, :], start=False, stop=(tb == 15),
                             reads=bbs + [self.b_kt, self.b_v], writes=[bp])
                    ob, bob = self.rOB.next()
                    S.op("act", "activation", out=ob[:], in_=p[:], func=AF.Copy, reads=[bp], writes=[bob])
                    S.op("sp", "dma_start", out=self.mixT[12 + gd][:, t * 512:(t + 1) * 512], in_=ob[:], reads=[bob], dma=bob.name)
        else:
            bc, bbc = self.load_big(self.dftP[0], 2, 256)
            bsn, bbs = self.load_big(self.dftP[1], 2, 256)
            for s in range(2):
                for gd in range(4):
                    p, bp = self.bank()
                    for b2 in range(2):
                        S.op("pe", "matmul", p[:, 0:256], self.XAB[:, s * 2 + b2, 0, gd * 128:(gd + 1) * 128], bc[:, b2, 0:256],
                                                                             start=(b2 == 0), stop=False, reads=bbc + [self.b_kt, self.b_v], writes=[bp])
                    for b2 in range(2):
                        S.op("pe", "matmul", p[:, 0:256], self.XAB[:, s * 2 + b2, 1, gd * 128:(gd + 1) * 128], bsn[:, b2, 0:256],
                                                                             start=False, stop=(b2 == 1), reads=bbs + [self.b_kt, self.b_v], writes=[bp])
                    ob, bob = self.rOB.next()
                    S.op("act", "activation", out=ob[:, 0:256], in_=p[:, 0:256], func=AF.Copy, reads=[bp], writes=[bob])
                    S.op("sp", "dma_start", out=self.mixT[12 + gd][:, TS + s * 256:TS + (s + 1) * 256], in_=ob[:, 0:256],
                         reads=[bob], dma=bob.name)

        for kind in ("A", "B"):
            if kind in skip:
                continue
            if kind == "A":
                kcol, nkc, vcol, nv, qcol, nq = 1024, 2, 1280, 256, 0, 8
                cK, cV, sK, sV = self.cak, self.cav, self.sKA, self.sVA
                gk, gq, ones_idx, ridx, rc = self.smallS[:, 1:2], self.smallS[:, 0:1], 1, 1, 0
            else:
                kcol, nkc, vcol, nv, qcol, nq = 2048, 4, 2560, 512, 1536, 4
                cK, cV, sK, sV = self.cbk, self.cbv, self.sKB, self.sVB
                gk, gq, ones_idx, ridx, rc = self.smallS[:, 3:4], self.smallS[:, 2:3], 2, 2, 2
            kw = nkc * 128
            if kind == "A":
                bigk, bbk = self.load_big(self.w_in[l][:, 1024:1536], 16, 512)
                bigv, bbv, kc0, vc0 = bigk, bbk, 0, 256
            else:
                bigk, bbk = self.load_big(self.w_in[l][:, 2048:2560], 16, 512)
                bigv, bbv = self.load_big(self.w_in[l][:, 2560:3072], 16, 512)
                kc0, vc0 = 0, 0
            if isS:
                S.op("pool", "dma_start", out=self.Vs[:, 16:20, 0:nv], in_=cV[l].rearrange("(b p) n -> p b n", p=128), writes=[self.b_v], dma="vc")
                S.op("sp", "dma_start", out=self.cstage[:, :, 0:kw], in_=cK[l].rearrange("(b p) n -> p b n", p=128), writes=[self.b_cst], dma="cst")
                for ch in range(nkc):
                    p, bp = self.bank()
                    for b in range(4):
                        S.op("pe", "transpose", p[:, b * 128:(b + 1) * 128], self.cstage[:, b, ch * 128:(ch + 1) * 128], self.cf[:, 0, :],
                             reads=[self.b_cst, self.b_c], writes=[bp])
                    S.op("act", "activation", out=self.KT[:, ch, 2048:2560], in_=p[:], func=AF.Copy, reads=[bp], writes=[self.b_kt])
            for t in range(ntile):
                rope = None
                if isS:
                    rt, brt = self.load_rope(t)
                    rope = (ridx, rt[:, rc, :], rt[:, rc + 1, :], brt)
                for ch in range(nkc):
                    p, bp = proj_fm(bigk, bbk, kc0 + ch * 128, t)
                    outf = None
                    if not isS and "K" not in skip:
                        outf = self.rF.next()
                    self.post_qk(p, bp, 512, gk, ones_idx, rope, self.KT[:, ch, t * 512:(t + 1) * 512], [self.b_kt], outf=outf)
                    if not isS and "K" not in skip:
                        kf, bkf = outf
                        pt, bpt = self.bank()
                        for jb in range(4):
                            S.op("pe", "transpose", pt[:, jb * 128:(jb + 1) * 128], kf[:, jb * 128:(jb + 1) * 128], self.cf[:, 0, :],
                                 reads=[bkf, self.b_c], writes=[bpt])
                        stg, bstg = self.rSO.next()
                        S.op("act", "activation", out=stg[:], in_=pt[:], func=AF.Copy, reads=[bpt], writes=[bstg])
                        for jb in range(4):
                            S.op("sp", "dma_start", out=sK[jb // 2, l, (jb % 2) * 128:(jb % 2) * 128 + 128, ch * 128:(ch + 1) * 128],
                                                                                 in_=stg[:, jb * 128:(jb + 1) * 128], reads=[bstg], dma=bstg.name)
                for jb in range(4):
                    p, bp = self.bank()
                    tok = t * 512 + jb * 128
                    for k in range(16):
                        S.op("pe", "matmul", p[:, 0:nv], self.hseq[:, k, tok:tok + 128], bigv[:, k, vc0:vc0 + nv], start=(k == 0), stop=(k == 15),
                             reads=bbv + [bh], writes=[bp])
                    if isS:
                        S.op("act", "activation", out=self.Vs[:, t * 4 + jb, 0:nv], in_=p[:, 0:nv], func=AF.Copy, reads=[bp], writes=[self.b_v])
                    else:
                        stg, bstg = self.rSO.next()
                        S.op("dve", "tensor_copy", out=stg[:, 0:nv], in_=p[:, 0:nv], reads=[bp], writes=[bstg])
                        S.op("act", "activation", out=self.Vs[:, t * 4 + jb, 0:nv], in_=stg[:, 0:nv], func=AF.Copy, reads=[bstg], writes=[self.b_v])
                        S.op("sp", "dma_start", out=sV[jb // 2, l, (jb % 2) * 128:(jb % 2) * 128 + 128, :], in_=stg[:, 0:nv], reads=[bstg], dma=bstg.name)
            for qs in range(nq // 4):
                bigq, bbq = self.load_big(self.w_in[l][:, qcol + qs * 512:qcol + (qs + 1) * 512], 16, 512)
                for t in range(ntile):
                    rope = None
                    if isS:
                        rt, brt = self.load_rope(t)
                        rope = (ridx, rt[:, rc, :], rt[:, rc + 1, :], brt)
                    for hh in range(4):
                        h = qs * 4 + hh
                        p, bp = proj_fm(bigq, bbq, hh * 128, t)
                        qT, bqT = self.rQT.next()
                        self.post_qk(p, bp, 512, gq, ones_idx, rope, qT[:], [bqT])
                        dst = h if kind == "A" else 8 + h
                        if isS:
                            self.attention(kind, h, qT, bqT, 0, 512, list(range(20)), dst, t * 512)
                        else:
                            for s in range(2):
                                self.attention(kind, h, qT, bqT, s * 256, 256, [2 * s, 2 * s + 1], dst, TS + s * 256)

    def build(self, stop=None):
        import os
        S = self.S
        stop = int(os.environ.get("K_STOP", "999")) if stop is None else stop
        step = [0]

        def go():
            step[0] += 1
            return step[0] <= stop

        def body():
            if not go(): return
            self.setup()
            if not go(): return
            self.adaln(0)
            for st in range(2):
                if not go(): return
                self.stage0(st)
                if not go(): return
                self.norm(st, 0, 1)
                self.store_xs(st)
            S.barrier()
            for l in range(self.depth):
                if not go(): return
                self.layer_small(l)
                for grp in ("S", "P"):
                    if not go(): return
                    self.stage1(l, grp)
                if l + 1 < self.depth:
                    self.adaln(l + 1)
                S.barrier()
                for st in range(2):
                    if not go(): return
                    self.stage2(l, st)
                    if not go(): return
                    if l + 1 < self.depth:
                        self.norm(st, l + 1, 1)
                        self.store_xs(st)
                    else:
                        self.final_out(st)
                S.barrier()

        body()
        S.barrier()
        S.finalize()
        S.run(self.nc)
        return self.nc


def _consts():
    f32 = np.float32
    ident = np.eye(128, dtype=f32)

    def rm(hd):
        R = np.zeros((128, 128), f32)
        q = hd // 4
        for base in range(0, 128, hd):
            for half in range(2):
                b0 = base + half * (hd // 2)
                for i in range(q):
                    R[b0 + q + i, b0 + i] = -1.0
                    R[b0 + i, b0 + q + i] = 1.0
        return R

    onesD = np.full((128, 128), 1.0 / 2048, f32)
    onesA = np.full((128, 128), 1.0 / 128, f32)
    onesB = np.zeros((128, 128), f32)
    onesB[0:64, 0:64] = 1.0 / 64
    onesB[64:128, 64:128] = 1.0 / 64
    ones1 = np.ones((128, 128), f32)
    cd = np.arange(128)
    ang = 2 * np.pi * np.outer(cd, cd) / 128
    Cc = (np.cos(ang) / np.sqrt(128)).astype(f32)
    Sc = (np.sin(ang) / np.sqrt(128)).astype(f32)
    cmat = np.stack([ident, rm(128), rm(64), onesD, onesA, onesB, ones1, Cc, Sc], axis=1)

    def rope_tab(hd):
        t = np.arange(TS)
        row = (t // 64).astype(np.float64)
        col = (t % 64).astype(np.float64)
        nf = hd // 4
        inv = 10000.0 ** (-np.arange(nf, dtype=np.float64) / nf)
        inv = inv.astype(f32).astype(np.float64)
        ang = np.concatenate([row[:, None] * inv, row[:, None] * inv, col[:, None] * inv, col[:, None] * inv], axis=1)
        ang = ang.astype(f32)
        c = np.cos(ang).T.astype(f32)
        s = np.sin(ang).T.astype(f32)
        reps = 128 // hd
        return np.tile(c, (reps, 1)), np.tile(s, (reps, 1))

    cA, sA = rope_tab(128)
    cB, sB = rope_tab(64)
    rope = np.stack([cA, sA, cB, sB], axis=0).astype(f32)

    def dft(T):
        t = np.arange(T, dtype=np.int64)
        m = np.outer(t, t) % T
        a = 2 * np.pi * m / T
        return np.stack([np.cos(a) / np.sqrt(T), np.sin(a) / np.sqrt(T)], axis=0).astype(f32)

    return cmat.astype(f32), rope, dft(TS), dft(TP)


_CACHE = {}


def make_in_maps(inp, ncores=8, depth=4):
    f32 = np.float32
    g = lambda k: np.asarray(inp[k], dtype=f32)
    cmat, rope, dftS, dftP = _consts()
    b_adaT = np.ascontiguousarray(g("b_ada").reshape(4, 96, 128).transpose(0, 2, 1))
    gT = np.ascontiguousarray(np.stack([g("norm_mix_g").reshape(4, 16, 128), g("norm_mlp_g").reshape(4, 16, 128)], axis=1).transpose(0, 3, 1, 2))
    smallv = np.zeros((4, 128, 8), f32)
    smallv[:, :, 0] = g("q_norm_a")
    smallv[:, :, 1] = g("k_norm_a")
    smallv[:, :, 2] = np.tile(g("q_norm_b"), (1, 2))
    smallv[:, :, 3] = np.tile(g("k_norm_b"), (1, 2))
    smallv[:, :, 4] = g("subln_g")
    lam = np.stack([g("lambda_q1"), g("lambda_k1"), g("lambda_q2"), g("lambda_k2")], axis=1)
    lamv = np.ascontiguousarray(np.broadcast_to(lam[:, None], (4, 128, 4, 64)))
    L = depth
    shared = dict(w_ada=g("w_ada")[:L], b_adaT=b_adaT[:L], gT=gT[:L], w_in=g("w_in")[:L], w_out=g("w_out")[:L], w1=g("w_mlp_in")[:L],
                  w2=g("w_mlp_out")[:L], w_f=g("w_fourier")[:L], smallv=smallv[:L], lamv=lamv[:L], cmat=cmat, rope=rope, dftS=dftS, dftP=dftP)
    xs, xp = g("x_sample"), g("x_prompt")
    cak, cav, cbk, cbv = g("cache_attn_k"), g("cache_attn_v"), g("cache_diff_k"), g("cache_diff_v")
    c, cctx = g("c"), g("c_ctx")
    maps = []
    for i in range(ncores):
        cond = np.stack([c[i], cctx], axis=1)
        condT = np.ascontiguousarray(cond.reshape(16, 128, 2).transpose(1, 0, 2))
        m = dict(shared)
        m.update(x_s=np.ascontiguousarray(xs[i]), x_p=np.ascontiguousarray(xp[2 * i:2 * i + 2].reshape(2 * TP, D)),
                 cak=np.ascontiguousarray(cak[i].reshape(4, 512, 256)), cav=np.ascontiguousarray(cav[i].reshape(4, 512, 256)),
                 cbk=np.ascontiguousarray(cbk[i].reshape(4, 512, 512)), cbv=np.ascontiguousarray(cbv[i].reshape(4, 512, 512)),
                 condT=condT)
        maps.append(m)
    return maps


def assemble(results, ncores=8, depth=4):
    f32 = np.float32
    y_p = np.zeros((2 * ncores, TP, D), f32)
    y_s = np.zeros((ncores, TS, D), f32)
    sKA = np.zeros((2 * ncores, depth, TP, 2, 128), f32)
    sVA = np.zeros((2 * ncores, depth, TP, 2, 128), f32)
    sKB = np.zeros((2 * ncores, depth, TP, 4, 2, 64), f32)
    sVB = np.zeros((2 * ncores, depth, TP, 4, 128), f32)
    for i, r in enumerate(results):
        y_s[i] = r["y_s"]
        y_p[2 * i:2 * i + 2] = r["y_p"].reshape(2, TP, D)
        sKA[2 * i:2 * i + 2] = r["sKA"].reshape(2, depth, TP, 2, 128)
        sVA[2 * i:2 * i + 2] = r["sVA"].reshape(2, depth, TP, 2, 128)
        sKB[2 * i:2 * i + 2] = r["sKB"].reshape(2, depth, TP, 4, 2, 64)
        sVB[2 * i:2 * i + 2] = r["sVB"].reshape(2, depth, TP, 4, 128)
    return (y_p, y_s, sKA, sVA, sKB, sVB)


def kernel(**inputs):
    ncores, depth = 8, 4
    if "nc" not in _CACHE:
        _CACHE["nc"] = Prog(depth).build()
    nc = _CACHE["nc"]
    maps = make_in_maps(inputs, ncores, depth)
    res = run_bass_kernel_spmd(nc, maps, core_ids=list(range(ncores)))
    return assemble(res.results, ncores, depth)
```

```python
import math
import numpy as np
import concourse.bass as bass
import concourse.mybir as mybir
from concourse.bass_utils import run_bass_kernel_spmd

F32 = mybir.dt.float32
BF16 = mybir.dt.bfloat16
AF = mybir.ActivationFunctionType
ALU = mybir.AluOpType
AX = mybir.AxisListType
ENGS = ["pe", "act", "dve", "pool", "sp"]

D = 2048
NCH = 16
TS = 2048
TP = 256
NTOK = 2560
DFF = 8192
INW = 3584
EPS = 1e-6
SEM_EPOCH = 30000


class Buf:
    __slots__ = ("name", "w", "r")

    def __init__(self, name):
        self.name = name
        self.w = None
        self.r = []


class Sched:
    def __init__(self):
        self.ops = {e: [] for e in ENGS}
        self.dma_cnt = {}
        self.last_c = {e: None for e in ENGS}

    def op(self, eng, meth, *args, reads=(), writes=(), dma=None, **kw):
        fn = lambda e: getattr(e, meth)(*args, **kw)
        deps = []
        for b in reads:
            if b.w is not None:
                deps.append(b.w)
        for b in writes:
            if b.w is not None:
                deps.append(b.w)
            for t in b.r:
                if t[0] == "c" and t[1] == eng and dma is None and eng == "pe":
                    continue
                deps.append(t)
        idx = len(self.ops[eng])
        if dma is None:
            tok = ("c", eng, idx)
            if eng == "pe":
                deps = [t for t in deps if not (t[0] == "c" and t[1] == "pe")]
            self.last_c[eng] = tok
        else:
            n = self.dma_cnt.get(dma, 0) + 1
            self.dma_cnt[dma] = n
            tok = ("d", dma, n)
        self.ops[eng].append(dict(fn=fn, deps=deps, tok=tok, marked=False, dma=dma))
        for b in reads:
            b.r.append(tok)
        for b in writes:
            b.w = tok
            b.r = []
        return tok

    def barrier(self):
        toks = [t for t in self.last_c.values() if t is not None]
        toks += [("d", ch, n) for ch, n in self.dma_cnt.items()]
        for e in ENGS:
            self.ops[e].append(dict(fn=lambda eng: eng.nop(), deps=list(toks), tok=("n", e, len(self.ops[e])), marked=False, dma=None))

    def finalize(self):
        for e in ENGS:
            for o in self.ops[e]:
                for t in o["deps"]:
                    if t[0] == "c":
                        self.ops[t[1]][t[2]]["marked"] = True
        self.cnt = {}
        for e in ENGS:
            c = 0
            arr = []
            for o in self.ops[e]:
                if o["marked"]:
                    c += 1
                arr.append(c)
            self.cnt[e] = arr
        self.sem_keys = set()
        nw = 0
        for e in ENGS:
            known = {}
            for o in self.ops[e]:
                w = {}
                for t in o["deps"]:
                    if t[0] == "c":
                        c = self.cnt[t[1]][t[2]]
                        ep = (c - 1) // SEM_EPOCH
                        key = ("e", t[1], ep)
                        val = c - ep * SEM_EPOCH
                    else:
                        ep = (t[2] - 1) // 1500
                        key = ("d", t[1], ep)
                        val = 16 * (t[2] - ep * 1500)
                    if val > w.get(key, 0):
                        w[key] = val
                ws = []
                for key, val in w.items():
                    if val > known.get(key, 0):
                        known[key] = val
                        ws.append((key, val))
                        self.sem_keys.add(key)
                o["waits"] = ws
                nw += len(ws)
        for e in ENGS:
            for i, o in enumerate(self.ops[e]):
                if o["dma"] is not None:
                    n = o["tok"][2]
                    ep = (n - 1) // 1500
                    o["inc"] = ("d", o["dma"], ep)
                    self.sem_keys.add(o["inc"])
                elif o["marked"]:
                    c = self.cnt[e][i]
                    o["inc"] = ("e", e, (c - 1) // SEM_EPOCH)
                    self.sem_keys.add(o["inc"])
                else:
                    o["inc"] = None
        self.nwaits = nw

    def run(self, nc):
        from contextlib import ExitStack

        with ExitStack() as st:
            sems = {}
            for i, key in enumerate(sorted(self.sem_keys)):
                sems[key] = st.enter_context(nc.semaphore("s%d" % i))
            block = st.enter_context(nc.Block())

            def make(e):
                def body(eng):
                    for o in self.ops[e]:
                        for key, val in o["waits"]:
                            eng.wait_ge(sems[key], val)
                        ins = o["fn"](eng)
                        if o["inc"] is not None:
                            ins.then_inc(sems[o["inc"]], 16 if o["dma"] is not None else 1)

                return body

            block.tensor(make("pe"))
            block.scalar(make("act"))
            block.vector(make("dve"))
            block.gpsimd(make("pool"))
            block.sync(make("sp"))


class Rot:
    def __init__(self, items):
        self.items = items
        self.i = 0

    def next(self):
        it = self.items[self.i % len(self.items)]
        self.i += 1
        return it


SUBT = {0: [(0, 512, 0), (512, 512, 0), (1024, 256, 0)], 1: [(0, 256, 0), (256, 512, 0), (768, 512, 1)]}


class Prog:
    def __init__(self, depth):
        self.depth = depth
        nc = self.nc = bass.Bass("TRN2", target_bir_lowering=False)
        self.S = Sched()
        L = depth

        def din(name, shape, dt=F32):
            return nc.dram_tensor(name, list(shape), dt, kind="ExternalInput").ap()

        def dout(name, shape, dt=F32):
            return nc.dram_tensor(name, list(shape), dt, kind="ExternalOutput").ap()

        self.x_s = din("x_s", [TS, D])
        self.x_p = din("x_p", [2 * TP, D])
        self.cak = din("cak", [4, 512, 256])
        self.cav = din("cav", [4, 512, 256])
        self.cbk = din("cbk", [4, 512, 512])
        self.cbv = din("cbv", [4, 512, 512])
        self.condT = din("condT", [128, 16, 2])
        self.w_ada = din("w_ada", [L, D, 6 * D])
        self.b_adaT = din("b_adaT", [L, 128, 96])
        self.gT = din("gT", [L, 128, 2, 16])
        self.w_in = din("w_in", [L, D, INW])
        self.w_out = din("w_out", [L, D, D])
        self.w1 = din("w1", [L, D, DFF])
        self.w2 = din("w2", [L, DFF, D])
        self.w_f = din("w_f", [L, 4, 128, 128])
        self.smallv = din("smallv", [L, 128, 8])
        self.lamv = din("lamv", [L, 128, 4, 64])
        self.cmat = din("cmat", [128, 9, 128])
        self.rope = din("rope", [4, 128, TS])
        self.dftS = din("dftS", [2, TS, TS])
        self.dftP = din("dftP", [2, TP, TP])
        self.y_s = dout("y_s", [TS, D])
        self.y_p = dout("y_p", [2 * TP, D])
        self.sKA = dout("sKA", [2, L, TP, 256])
        self.sVA = dout("sVA", [2, L, TP, 256])
        self.sKB = dout("sKB", [2, L, TP, 512])
        self.sVB = dout("sVB", [2, L, TP, 512])
        self.xT = nc.dram_tensor("xT_scr", [NCH, 128, NTOK], F32, kind="Internal").ap()
        self.hT = nc.dram_tensor("hT_scr", [NCH, 128, NTOK], BF16, kind="Internal").ap()
        self.mixT = nc.dram_tensor("mixT_scr", [NCH, 128, NTOK], BF16, kind="Internal").ap()

        self.off = 16512

        def sb(name, shape, dt, at=None):
            nbytes = int(np.prod(shape[1:])) * (4 if dt == F32 else 2)
            nbytes = (nbytes + 31) // 32 * 32
            if at is None:
                o = self.off
                self.off += nbytes
            else:
                o = at
            assert o + nbytes <= 229344, (name, o, nbytes)
            return nc.alloc_sbuf_tensor_at(name, list(shape), dt, offset=o)

        self.cf = sb("cf", [128, 3, 128], F32)
        self.cb = sb("cb", [128, 6, 128], BF16)
        self.b_c = Buf("consts")
        self.epsc = sb("epsc", [128, 8], F32)
        self.condS = sb("condS", [128, 16, 2], BF16)
        self.condF = sb("condF", [128, 16, 2], F32)
        self.mod = [sb(f"mod{i}", [128, 96, 2], F32) for i in range(2)]
        self.a12 = [sb(f"a12{i}", [128, 2, 16, 2], F32) for i in range(2)]
        self.b_mod = [Buf("mod0"), Buf("mod1")]
        self.gTs = sb("gTs", [128, 2, 16], F32)
        self.badaS = sb("badaS", [128, 96], F32)
        self.smallS = sb("smallS", [128, 8], F32)
        self.lamS = sb("lamS", [128, 4, 64], F32)
        self.lamT = sb("lamT", [128, 2, 64], F32)
        self.lamR = sb("lamR", [128, 8], F32)
        self.b_small = Buf("small")
        self.wfS = sb("wfS", [128, 4, 128], BF16)
        self.AB = sb("AB", [128, 4, 256], BF16)
        self.b_AB = Buf("AB")
        self.b_wf = Buf("wf")
        self.tF = [sb(f"tF{i}", [128, 512], F32) for i in range(6)]
        self.tB = [sb(f"tB{i}", [128, 512], BF16) for i in range(10)]
        self.rF = Rot([(t, Buf(f"tF{i}")) for i, t in enumerate(self.tF)])
        self.rB = Rot([(t, Buf(f"tB{i}")) for i, t in enumerate(self.tB)])
        self.rOB = Rot([(sb(f"ob{i}", [128, 512], BF16), Buf(f"ob{i}")) for i in range(3)])
        self.rSO = Rot([(sb(f"so{i}", [128, 512], F32), Buf(f"so{i}")) for i in range(2)])
        self.rQT = Rot([(sb(f"qT{i}", [128, 512], BF16), Buf(f"qT{i}")) for i in range(3)])
        self.rRN = Rot([(sb(f"rN{i}", [128, 512], F32), Buf(f"rN{i}")) for i in range(2)])
        slab0 = self.off
        self.off += 4 * 8192
        self.big = [sb(f"big{j}", [128, 16, 512], BF16, at=slab0 + j * 16384) for j in range(2)]
        self.u16 = [sb(f"u16_{i}", [128, 16, 256], BF16, at=slab0 + i * 8192) for i in range(4)]
        self.u2 = [sb(f"u2_{i}", [128, 2, 2048], BF16, at=slab0 + i * 8192) for i in range(4)]
        self.b_unit = [Buf(f"unit{i}") for i in range(4)]
        self.big_i = 0
        self.unit_i = 0
        ov = self.off
        o = ov
        self.hseq = sb("hseq", [128, 16, TS], BF16, at=o); o += 16 * TS * 2
        self.KT = sb("KT", [128, 4, 2560], BF16, at=o)
        self.XAB = sb("XAB", [128, 16, 2, 512], BF16, at=o)
        self.Vs = sb("Vs", [128, 20, 512], BF16, at=o + 4 * 2560 * 2)
        o += 4 * 2560 * 2 + 20 * 512 * 2
        self.ropeT = [sb(f"ropeT{i}", [128, 4, 512], F32, at=o + i * 8192) for i in range(2)]
        o += 16384
        self.cstage = sb("cstage", [128, 4, 512], F32, at=o); o += 8192
        assert o <= 229344, o
        self.b_hseq = Buf("hseq")
        self.b_kt = Buf("kt")
        self.b_v = Buf("v")
        self.b_rope = [Buf("rope0"), Buf("rope1")]
        self.b_cst = Buf("cstage")
        self.rope_i = 0
        o = ov
        self.xs = sb("xs", [128, 16, 1280], F32, at=o); o += 16 * 1280 * 4
        self.h2 = sb("h2", [128, 16, 1280], BF16, at=o); o += 16 * 1280 * 2
        self.tm = [sb(f"tm{i}", [128, D], F32, at=o - 16 * 1280 * 2 + i * 10240) for i in range(2)]
        self.ub = sb("ub", [128, 2, 1280], BF16, at=o); o += 2 * 1280 * 2
        assert o <= 229344, o
        self.b_xs = [Buf(f"xs{c}") for c in range(16)]
        self.b_h2 = [Buf(f"h2{c}") for c in range(16)]
        self.b_u = [Buf("u0"), Buf("u1")]
        self.b_tm = [Buf("tm0"), Buf("tm1")]
        self.ps = [nc.alloc_psum_tensor(f"ps{i}", [128, 512], F32) for i in range(8)]
        self.b_ps = [Buf(f"ps{i}") for i in range(8)]
        self.rP = Rot([0, 1, 2])
        self.rQ = Rot([3])
        self.acc_i = 0

    def bank(self):
        i = self.rP.next()
        return self.ps[i], self.b_ps[i]

    def bankq(self):
        i = self.rQ.next()
        return self.ps[i], self.b_ps[i]

    def load_big(self, src2d, kch, ncols):
        j = self.big_i % 2
        self.big_i += 1
        t = self.big[j]
        bufs = [self.b_unit[2 * j], self.b_unit[2 * j + 1]]
        self.S.op("pool", "dma_start", out=t[:, 0:kch, 0:ncols], in_=src2d.rearrange("(k p) n -> p k n", p=128),
                  writes=bufs, dma=f"unit{2*j}")
        return t, bufs

    def load_unit(self, src2d, view):
        i = self.unit_i % 4
        self.unit_i += 1
        t = self.u16[i] if view == 16 else self.u2[i]
        self.S.op("pool", "dma_start", out=t[:], in_=src2d.rearrange("(k p) n -> p k n", p=128),
                  writes=[self.b_unit[i]], dma=f"unit{i}")
        return t, [self.b_unit[i]]

    def rsqrt(self, rinv, brv, pq, bq, n):
        S = self.S
        S.op("act", "activation", out=rinv[:, 0:n], in_=pq[:, 0:n], func=AF.Ln, bias=self.epsc[:, 0:1], reads=[bq, self.b_c], writes=[brv])
        S.op("act", "activation", out=rinv[:, 0:n], in_=rinv[:, 0:n], func=AF.Exp, scale=-0.5, reads=[brv], writes=[brv])

    def setup(self):
        S = self.S
        S.op("dve", "memset", self.epsc[:], EPS, writes=[self.b_c])
        S.op("sp", "dma_start", out=self.cf[:], in_=self.cmat[:, 0:3, :], writes=[self.b_c], dma="cc")
        S.op("pool", "dma_start", out=self.cb[:], in_=self.cmat[:, 3:9, :], writes=[self.b_c], dma="ccp")
        S.op("sp", "dma_start", out=self.condF[:], in_=self.condT, writes=[self.b_small], dma="cs")
        S.op("act", "activation", out=self.condS[:], in_=self.condF[:], func=AF.Silu, reads=[self.b_small], writes=[self.b_c])

    def adaln(self, l):
        S = self.S
        par = l % 2
        psA, bA = self.ps[7], self.b_ps[7]
        S.op("sp", "dma_start", out=self.badaS[:], in_=self.b_adaT[l], writes=[self.b_small], dma="cs")
        S.op("sp", "dma_start", out=self.gTs[:], in_=self.gT[l], writes=[self.b_small], dma="cs")
        for sl in range(24):
            big, bb = self.load_big(self.w_ada[l][:, sl * 512:(sl + 1) * 512], 16, 512)
            for jj in range(4):
                j = sl * 4 + jj
                for k in range(16):
                    S.op("pe", "matmul", psA[:, 2 * j:2 * j + 2], big[:, k, jj * 128:(jj + 1) * 128],
                                                                              self.condS[:, k, :], start=(k == 0), stop=(k == 15),
                         reads=bb + [self.b_c], writes=[bA])
        mod = self.mod[par]
        a12 = self.a12[par]
        psv = psA[:, 0:192].rearrange("p (j i) -> p j i", i=2)
        for i in range(2):
            S.op("dve", "tensor_tensor", out=mod[:, :, i], in0=psv[:, :, i], in1=self.badaS[:], op=ALU.add,
                 reads=[bA, self.b_small], writes=[self.b_mod[par]])
        for i in range(2):
            for w, (sc0, gi) in enumerate([(16, 0), (64, 1)]):
                S.op("dve", "scalar_tensor_tensor",
                    out=a12[:, w, :, i], in0=mod[:, sc0:sc0 + 16, i], scalar=1.0, in1=self.gTs[:, gi, :], op0=ALU.add, op1=ALU.mult,
                    reads=[self.b_mod[par], self.b_small], writes=[self.b_mod[par]])

    def mvec(self, l, which, c, ci):
        par = l % 2
        if which == "a1":
            return self.a12[par][:, 0, c, ci:ci + 1]
        if which == "a2":
            return self.a12[par][:, 1, c, ci:ci + 1]
        base = {"shift1": 0, "gate1": 32, "shift2": 48, "gate2": 80}[which]
        return self.mod[par][:, base + c, ci:ci + 1]

    def layer_small(self, l):
        S = self.S
        lam_init = 0.8 - 0.6 * math.exp(-0.3 * l)
        bs = self.b_small
        S.op("sp", "dma_start", out=self.smallS[:], in_=self.smallv[l], writes=[bs], dma="cs")
        S.op("sp", "dma_start", out=self.lamS[:], in_=self.lamv[l], writes=[bs], dma="cs")
        for j in range(2):
            S.op("dve", "tensor_tensor", out=self.lamT[:, j, :], in0=self.lamS[:, 2 * j, :], in1=self.lamS[:, 2 * j + 1, :], op=ALU.mult,
                 reads=[bs], writes=[bs])
            S.op("dve", "reduce_sum", out=self.lamR[:, j:j + 1], in_=self.lamT[:, j, :], axis=AX.X, reads=[bs], writes=[bs])
            S.op("act", "activation", out=self.lamR[:, 2 + j:3 + j], in_=self.lamR[:, j:j + 1], func=AF.Exp, reads=[bs], writes=[bs])
        S.op("dve", "scalar_tensor_tensor", out=self.lamR[:, 4:5], in0=self.lamR[:, 3:4], scalar=-lam_init, in1=self.lamR[:, 2:3],
                                                       op0=ALU.add, op1=ALU.subtract, reads=[bs], writes=[bs])
        S.op("dve", "tensor_scalar", out=self.lamR[:, 5:6], in0=self.smallS[:, 4:5], scalar1=1.0 - lam_init, scalar2=None, op0=ALU.mult,
             reads=[bs], writes=[bs])
        S.op("pool", "dma_start", out=self.wfS[:], in_=self.w_f[l].rearrange("g c d -> c g d"), writes=[self.b_wf], dma="cw")
        for g in range(4):
            for ab in range(2):
                p, bp = self.bankq()
                S.op("pe", "matmul", p[:, 0:128], self.cb[:, 4 + ab, :], self.wfS[:, g, :], start=True, stop=True,
                     reads=[self.b_c, self.b_wf], writes=[bp])
                S.op("act", "activation", out=self.AB[:, g, ab * 128:(ab + 1) * 128], in_=p[:, 0:128], func=AF.Copy,
                                                                     scale=(1.0 if ab == 0 else -1.0), reads=[bp], writes=[self.b_AB])

    def norm(self, st, l, kind):
        S = self.S
        aw, sw = ("a1", "shift1") if kind == 1 else ("a2", "shift2")
        g0 = st * 1280
        for (off, n, ci) in SUBT[st]:
            pq, bq = self.bankq()
            for c in range(16):
                sq, bsq = self.rB.next()
                S.op("act", "activation", out=sq[:, 0:n], in_=self.xs[:, c, off:off + n], func=AF.Square,
                     reads=[self.b_xs[c]], writes=[bsq])
                S.op("pe", "matmul", pq[:, 0:n], self.cb[:, 0, :], sq[:, 0:n], start=(c == 0), stop=(c == 15),
                     reads=[bsq, self.b_c], writes=[bq])
            rinv, brv = self.rRN.next()
            self.rsqrt(rinv, brv, pq, bq, n)
            for c in range(16):
                t, bt = self.rF.next()
                S.op("dve", "scalar_tensor_tensor", out=t[:, 0:n], in0=self.xs[:, c, off:off + n], scalar=self.mvec(l, aw, c, ci),
                                                                                 in1=rinv[:, 0:n], op0=ALU.mult, op1=ALU.mult,
                     reads=[self.b_xs[c], brv, self.b_mod[l % 2]], writes=[bt])
                S.op("act", "activation", out=self.h2[:, c, off:off + n], in_=t[:, 0:n], func=AF.Identity, bias=self.mvec(l, sw, c, ci),
                     reads=[bt, self.b_mod[l % 2]], writes=[self.b_h2[c]])
            if kind == 1:
                S.op("sp", "dma_start", out=self.hT[:, :, g0 + off:g0 + off + n].rearrange("c p n -> p c n"), in_=self.h2[:, :, off:off + n],
                     reads=self.b_h2, dma="h2io")

    def src_rows(self, g, n, sample, prompt):
        return sample[g:g + n, :] if g < TS else prompt[g - TS:g - TS + n, :]

    def stage0(self, st):
        S = self.S
        for b in range(10):
            g = st * 1280 + b * 128
            tm, btm = self.tm[b % 2], self.b_tm[b % 2]
            btmx = [btm] + self.b_h2[4 * (b % 2):4 * (b % 2) + 4]
            S.op("sp", "dma_start", out=tm[:], in_=self.src_rows(g, 128, self.x_s, self.x_p), writes=btmx, dma=f"tm{b%2}")
            for c4 in range(4):
                p, bp = self.bank()
                for cc in range(4):
                    c = c4 * 4 + cc
                    S.op("pe", "transpose", p[:, cc * 128:(cc + 1) * 128], tm[:, c * 128:(c + 1) * 128], self.cf[:, 0, :],
                         reads=btmx + [self.b_c], writes=[bp])
                eng = "act" if c4 % 2 == 0 else "dve"
                if eng == "act":
                    S.op("act", "activation", out=self.xs[:, c4 * 4:c4 * 4 + 4, b * 128:(b + 1) * 128],
                                                                       in_=p[:].rearrange("p (a n) -> p a n", a=4), func=AF.Copy,
                         reads=[bp], writes=self.b_xs[c4 * 4:c4 * 4 + 4])
                else:
                    S.op("dve", "tensor_copy", out=self.xs[:, c4 * 4:c4 * 4 + 4, b * 128:(b + 1) * 128],
                                                                        in_=p[:].rearrange("p (a n) -> p a n", a=4),
                         reads=[bp], writes=self.b_xs[c4 * 4:c4 * 4 + 4])

    def store_xs(self, st):
        g0 = st * 1280
        self.S.op("sp", "dma_start", out=self.xT[:, :, g0:g0 + 1280].rearrange("c p n -> p c n"), in_=self.xs[:], reads=self.b_xs, dma="xsio")

    def final_out(self, st):
        S = self.S
        for b in range(10):
            g = st * 1280 + b * 128
            tm, btm = self.tm[b % 2], self.b_tm[b % 2]
            btmx = [btm] + self.b_h2[4 * (b % 2):4 * (b % 2) + 4]
            for c4 in range(4):
                p, bp = self.bank()
                for cc in range(4):
                    c = c4 * 4 + cc
                    S.op("pe", "transpose", p[:, cc * 128:(cc + 1) * 128], self.xs[:, c, b * 128:(b + 1) * 128], self.cf[:, 0, :],
                         reads=[self.b_xs[c], self.b_c], writes=[bp])
                if c4 % 2 == 0:
                    S.op("act", "activation", out=tm[:, c4 * 512:(c4 + 1) * 512], in_=p[:], func=AF.Copy, reads=[bp], writes=btmx)
                else:
                    S.op("dve", "tensor_copy", out=tm[:, c4 * 512:(c4 + 1) * 512], in_=p[:], reads=[bp], writes=btmx)
            S.op("sp", "dma_start", out=self.src_rows(g, 128, self.y_s, self.y_p), in_=tm[:], reads=btmx, dma=f"tm{b%2}")

    def stage2(self, l, st):
        S = self.S
        g0 = st * 1280
        S.op("sp", "dma_start", out=self.xs[:], in_=self.xT[:, :, g0:g0 + 1280].rearrange("c p n -> p c n"), writes=self.b_xs, dma="xsio")
        S.op("sp", "dma_start", out=self.h2[:], in_=self.mixT[:, :, g0:g0 + 1280].rearrange("c p n -> p c n"), writes=self.b_h2, dma="h2io")
        for sl in range(4):
            big, bb = self.load_big(self.w_out[l][:, sl * 512:(sl + 1) * 512], 16, 512)
            for (off, n, ci) in SUBT[st]:
                for mm in range(4):
                    m = sl * 4 + mm
                    p, bp = self.bank()
                    for k in range(16):
                        S.op("pe", "matmul", p[:, 0:n], big[:, k, mm * 128:(mm + 1) * 128], self.h2[:, k, off:off + n],
                                                                               start=(k == 0), stop=(k == 15),
                             reads=bb + [self.b_h2[k]], writes=[bp])
                    S.op("dve", "scalar_tensor_tensor", out=self.xs[:, m, off:off + n], in0=p[:, 0:n], scalar=self.mvec(l, "gate1", m, ci),
                                                                          in1=self.xs[:, m, off:off + n], op0=ALU.mult, op1=ALU.add,
                         reads=[bp, self.b_xs[m], self.b_mod[l % 2]], writes=[self.b_xs[m]])
        self.norm(st, l, 2)
        for j in range(DFF // 256):
            w1u, b1 = self.load_unit(self.w1[l][:, j * 256:(j + 1) * 256], 16)
            w2u, b2 = self.load_unit(self.w2[l][j * 256:(j + 1) * 256, :], 2)
            for (off, n, ci) in SUBT[st]:
                for hc in range(2):
                    p, bp = self.bank()
                    for k in range(16):
                        S.op("pe", "matmul", p[:, 0:n], w1u[:, k, hc * 128:(hc + 1) * 128], self.h2[:, k, off:off + n],
                                                                               start=(k == 0), stop=(k == 15),
                             reads=b1 + [self.b_h2[k]], writes=[bp])
                    r, br = self.rF.next()
                    S.op("act", "activation", out=r[:, 0:n], in_=p[:, 0:n], func=AF.Relu, reads=[bp], writes=[br])
                    S.op("pool", "tensor_tensor", out=self.ub[:, hc, off:off + n], in0=r[:, 0:n], in1=r[:, 0:n], op=ALU.mult,
                         reads=[br], writes=[self.b_u[hc]])
            for (off, n, ci) in SUBT[st]:
                for m in range(16):
                    p, bp = self.bank()
                    for hc in range(2):
                        S.op("pe", "matmul", p[:, 0:n], w2u[:, hc, m * 128:(m + 1) * 128], self.ub[:, hc, off:off + n],
                                                                               start=(hc == 0), stop=(hc == 1),
                             reads=b2 + [self.b_u[hc]], writes=[bp])
                    S.op("dve", "scalar_tensor_tensor", out=self.xs[:, m, off:off + n], in0=p[:, 0:n], scalar=self.mvec(l, "gate2", m, ci),
                                                                          in1=self.xs[:, m, off:off + n], op0=ALU.mult, op1=ALU.add,
                         reads=[bp, self.b_xs[m], self.b_mod[l % 2]], writes=[self.b_xs[m]])

    def post_qk(self, pz, bpz, n, gcol, ones_idx, rope, out_ap, out_bufs, outf=None):
        for _ in self.post_qk_gen(pz, bpz, n, gcol, ones_idx, rope, out_ap, out_bufs, outf):
            pass

    def post_qk_gen(self, pz, bpz, n, gcol, ones_idx, rope, out_ap, out_bufs, outf=None):
        S = self.S
        sq, bsq = self.rB.next()
        S.op("act", "activation", out=sq[:, 0:n], in_=pz[:, 0:n], func=AF.Square, reads=[bpz], writes=[bsq])
        zg, bzg = self.rF.next()
        S.op("act", "activation", out=zg[:, 0:n], in_=pz[:, 0:n], func=AF.Copy, scale=gcol, reads=[bpz, self.b_small], writes=[bzg])
        yield
        pq, bq = self.bankq()
        S.op("pe", "matmul", pq[:, 0:n], self.cb[:, ones_idx, :], sq[:, 0:n], start=True, stop=True, reads=[bsq, self.b_c], writes=[bq])
        rinv, brv = self.rF.next()
        self.rsqrt(rinv, brv, pq, bq, n)
        if rope is not None:
            ridx, cos, sin, brope = rope
            t1, bt1 = self.rF.next()
            S.op("pool", "tensor_tensor", out=t1[:, 0:n], in0=zg[:, 0:n], in1=cos, op=ALU.mult, reads=[bzg, brope], writes=[bt1])
            yield
            pr, bpr = self.bankq()
            S.op("pe", "matmul", pr[:, 0:n], self.cf[:, ridx, :], zg[:, 0:n], start=True, stop=True, reads=[bzg, self.b_c], writes=[bpr])
            t2, bt2 = self.rF.next()
            S.op("dve", "tensor_tensor", out=t2[:, 0:n], in0=pr[:, 0:n], in1=sin, op=ALU.mult, reads=[bpr, brope], writes=[bt2])
            S.op("dve", "tensor_tensor", out=t1[:, 0:n], in0=t1[:, 0:n], in1=t2[:, 0:n], op=ALU.add, reads=[bt1, bt2], writes=[bt1])
            S.op("dve", "tensor_tensor", out=out_ap, in0=t1[:, 0:n], in1=rinv[:, 0:n], op=ALU.mult, reads=[bt1, brv], writes=out_bufs)
        else:
            S.op("dve", "tensor_tensor", out=out_ap, in0=zg[:, 0:n], in1=rinv[:, 0:n], op=ALU.mult, reads=[bzg, brv], writes=out_bufs)
            if outf is not None:
                of, bof = outf
                S.op("dve", "tensor_tensor", out=of[:, 0:n], in0=zg[:, 0:n], in1=rinv[:, 0:n], op=ALU.mult, reads=[bzg, brv], writes=[bof])
        yield

    def load_rope(self, t):
        i = self.rope_i % 2
        self.rope_i += 1
        rt, br = self.ropeT[i], self.b_rope[i]
        self.S.op("sp", "dma_start", out=rt[:], in_=self.rope[:, :, t * 512:(t + 1) * 512].rearrange("f p n -> p f n"), writes=[br], dma=f"rope{i}")
        return rt, br

    def attention(self, kind, h, qT, bq, q0, n, kbs, dst_chunk, gcols, hook=None):
        S = self.S
        ncomp = 1 if kind == "A" else 2
        scale = 128 ** -0.5 if kind == "A" else 64 ** -0.5
        ktc = h // 4 if kind == "A" else h
        vcol = (h // 4) * 128 if kind == "A" else h * 128
        oc = []
        nk = len(kbs)
        for comp in range(ncomp):
            ai = self.acc_i % 2
            self.acc_i += 1
            psO, bO = self.ps[4 + 2 * ai], self.b_ps[4 + 2 * ai]
            psS, bS = self.ps[5 + 2 * ai], self.b_ps[5 + 2 * ai]
            r0, r1 = (0, 128) if kind == "A" else (comp * 64, comp * 64 + 64)

            def pv(item):
                pT, bpT, kb, i = item
                S.op("pe", "matmul", psO[:, 0:n], self.Vs[:, kb, vcol:vcol + 128], pT[:, 0:n], start=(i == 0), stop=(i == nk - 1),
                     reads=[self.b_v, bpT], writes=[bO])
                S.op("pe", "matmul", psS[:, 0:n], self.cb[:, 3, :], pT[:, 0:n], start=(i == 0), stop=(i == nk - 1),
                     reads=[self.b_c, bpT], writes=[bS])

            prev = None
            for i, kb in enumerate(kbs):
                ps_s, bs = self.bank()
                S.op("pe", "matmul", ps_s[:, 0:n], self.KT[r0:r1, ktc, kb * 128:(kb + 1) * 128], qT[r0:r1, q0:q0 + n], start=True, stop=True,
                     reads=[self.b_kt, bq], writes=[bs])
                pT, bpT = self.rB.next()
                S.op("act", "activation", out=pT[:, 0:n], in_=ps_s[:, 0:n], func=AF.Exp, scale=scale, reads=[bs], writes=[bpT])
                if prev is not None:
                    pv(prev)
                prev = (pT, bpT, kb, i)
                if hook is not None and comp == 0 and i in (1, 5, 9):
                    next(hook, None)
            pv(prev)
            rs, brs = self.rF.next()
            S.op("dve", "reciprocal", out=rs[:, 0:n], in_=psS[:, 0:n], reads=[bS], writes=[brs])
            if kind == "A":
                ob, bob = self.rOB.next()
                S.op("dve", "tensor_tensor", out=ob[:, 0:n], in0=psO[:, 0:n], in1=rs[:, 0:n], op=ALU.mult, reads=[bO, brs], writes=[bob])
                S.op("sp", "dma_start", out=self.mixT[dst_chunk][:, gcols:gcols + n], in_=ob[:, 0:n], reads=[bob], dma=bob.name)
            else:
                o, bo = self.rF.next()
                S.op("dve", "tensor_tensor", out=o[:, 0:n], in0=psO[:, 0:n], in1=rs[:, 0:n], op=ALU.mult, reads=[bO, brs], writes=[bo])
                oc.append((o, bo))
        if kind == "B":
            (o0, b0), (o1, b1) = oc
            od, bod = self.rF.next()
            S.op("dve", "scalar_tensor_tensor", out=od[:, 0:n], in0=o1[:, 0:n], scalar=self.lamR[:, 4:5], in1=o0[:, 0:n], op0=ALU.mult, op1=ALU.add,
                 reads=[b0, b1, self.b_small], writes=[bod])
            sq, bsq = self.rB.next()
            S.op("act", "activation", out=sq[:, 0:n], in_=od[:, 0:n], func=AF.Square, reads=[bod], writes=[bsq])
            pq, bq2 = self.bankq()
            S.op("pe", "matmul", pq[:, 0:n], self.cb[:, 1, :], sq[:, 0:n], start=True, stop=True, reads=[bsq, self.b_c], writes=[bq2])
            rinv, brv = self.rF.next()
            self.rsqrt(rinv, brv, pq, bq2, n)
            ob, bob = self.rOB.next()
            S.op("dve", "scalar_tensor_tensor", out=ob[:, 0:n], in0=od[:, 0:n], scalar=self.lamR[:, 5:6], in1=rinv[:, 0:n], op0=ALU.mult, op1=ALU.mult,
                 reads=[bod, brv, self.b_small], writes=[bob])
            S.op("sp", "dma_start", out=self.mixT[dst_chunk][:, gcols:gcols + n], in_=ob[:, 0:n], reads=[bob], dma=bob.name)

    def stage1(self, l, grp):
        S = self.S
        isS = grp == "S"
        tok0 = 0 if isS else TS
        ntile = 4 if isS else 1
        ntb = 16 if isS else 4
        S.op("sp", "dma_start", out=self.hseq[:, :, 0:ntile * 512], in_=self.hT[:, :, tok0:tok0 + ntile * 512].rearrange("c p n -> p c n"),
             writes=[self.b_hseq], dma="hseq")
        bh = self.b_hseq

        def proj_fm(big, bb, col0, t, n=512):
            p, bp = self.bank()
            for k in range(16):
                S.op("pe", "matmul", p[:, 0:n], big[:, k, col0:col0 + 128], self.hseq[:, k, t * 512:t * 512 + n], start=(k == 0), stop=(k == 15),
                     reads=bb + [bh], writes=[bp])
            return p, bp

        import os
        skip = os.environ.get("K_PSKIP", "") if not isS else ""
        big, bb = self.load_big(self.w_in[l][:, 3072:3584], 16, 512)
        for t in range(ntile if "F" not in skip else 0):
            for g in range(4):
                p, bp = proj_fm(big, bb, g * 128, t)
                ft, bft = self.rB.next()
                S.op("act", "activation", out=ft[:], in_=p[:], func=AF.Copy, reads=[bp], writes=[bft])
                for jb in range(4):
                    px, bpx = self.bank()
                    S.op("pe", "matmul", px[:, 0:256], ft[:, jb * 128:(jb + 1) * 128], self.AB[:, g, :], start=True, stop=True,
                         reads=[bft, self.b_AB], writes=[bpx])
                    tb = t * 4 + jb
                    S.op("dve", "tensor_copy", out=self.XAB[:, tb, :, g * 128:(g + 1) * 128],
                                                                          in_=px[:, 0:256].rearrange("p (a d) -> p a d", a=2),
                         reads=[bpx], writes=[self.b_kt, self.b_v])
        if "F" in skip:
            pass
        elif isS:
            for t in range(4):
                bc, bbc = self.load_big(self.dftS[0][:, t * 512:(t + 1) * 512], 16, 512)
                bsn, bbs = self.load_big(self.dftS[1][:, t * 512:(t + 1) * 512], 16, 512)
                for gd in range(4):
                    p, bp = self.bank()
                    for tb in range(16):
                        S.op("pe", "matmul", p[:], self.XAB[:, tb, 0, gd * 128:(gd + 1) * 128], bc[:, tb, :], start=(tb == 0), stop=False,
                             reads=bbc + [self.b_kt, self.b_v], writes=[bp])
                    for tb in range(16):
                        S.op("pe", "matmul", p[:], self.XAB[:, tb, 1, gd * 128:(gd + 1) * 128], bsn[:, tb, :], start=False, stop=(tb == 15),
                             reads=bbs + [self.b_kt, self.b_v], writes=[bp])
                    ob, bob = self.rOB.next()
                    S.op("act", "activation", out=ob[:], in_=p[:], func=AF.Copy, reads=[bp], writes=[bob])
                    S.op("sp", "dma_start", out=self.mixT[12 + gd][:, t * 512:(t + 1) * 512], in_=ob[:], reads=[bob], dma=bob.name)
        else:
            bc, bbc = self.load_big(self.dftP[0], 2, 256)
            bsn, bbs = self.load_big(self.dftP[1], 2, 256)
            for s in range(2):
                for gd in range(4):
                    p, bp = self.bank()
                    for b2 in range(2):
                        S.op("pe", "matmul", p[:, 0:256], self.XAB[:, s * 2 + b2, 0, gd * 128:(gd + 1) * 128], bc[:, b2, 0:256],
                                                                             start=(b2 == 0), stop=False, reads=bbc + [self.b_kt, self.b_v], writes=[bp])
                    for b2 in range(2):
                        S.op("pe", "matmul", p[:, 0:256], self.XAB[:, s * 2 + b2, 1, gd * 128:(gd + 1) * 128], bsn[:, b2, 0:256],
                                                                             start=False, stop=(b2 == 1), reads=bbs + [self.b_kt, self.b_v], writes=[bp])
                    ob, bob = self.rOB.next()
                    S.op("act", "activation", out=ob[:, 0:256], in_=p[:, 0:256], func=AF.Copy, reads=[bp], writes=[bob])
                    S.op("sp", "dma_start", out=self.mixT[12 + gd][:, TS + s * 256:TS + (s + 1) * 256], in_=ob[:, 0:256],
                         reads=[bob], dma=bob.name)

        for kind in ("A", "B"):
            if kind in skip:
                continue
            if kind == "A":
                kcol, nkc, vcol, nv, qcol, nq = 1024, 2, 1280, 256, 0, 8
                cK, cV, sK, sV = self.cak, self.cav, self.sKA, self.sVA
                gk, gq, ones_idx, ridx, rc = self.smallS[:, 1:2], self.smallS[:, 0:1], 1, 1, 0
            else:
                kcol, nkc, vcol, nv, qcol, nq = 2048, 4, 2560, 512, 1536, 4
                cK, cV, sK, sV = self.cbk, self.cbv, self.sKB, self.sVB
                gk, gq, ones_idx, ridx, rc = self.smallS[:, 3:4], self.smallS[:, 2:3], 2, 2, 2
            kw = nkc * 128
            if kind == "A":
                bigk, bbk = self.load_big(self.w_in[l][:, 1024:1536], 16, 512)
                bigv, bbv, kc0, vc0 = bigk, bbk, 0, 256
            else:
                bigk, bbk = self.load_big(self.w_in[l][:, 2048:2560], 16, 512)
                bigv, bbv = self.load_big(self.w_in[l][:, 2560:3072], 16, 512)
                kc0, vc0 = 0, 0
            if isS:
                S.op("pool", "dma_start", out=self.Vs[:, 16:20, 0:nv], in_=cV[l].rearrange("(b p) n -> p b n", p=128), writes=[self.b_v], dma="vc")
                S.op("sp", "dma_start", out=self.cstage[:, :, 0:kw], in_=cK[l].rearrange("(b p) n -> p b n", p=128), writes=[self.b_cst], dma="cst")
                for ch in range(nkc):
                    p, bp = self.bank()
                    for b in range(4):
                        S.op("pe", "transpose", p[:, b * 128:(b + 1) * 128], self.cstage[:, b, ch * 128:(ch + 1) * 128], self.cf[:, 0, :],
                             reads=[self.b_cst, self.b_c], writes=[bp])
                    S.op("act", "activation", out=self.KT[:, ch, 2048:2560], in_=p[:], func=AF.Copy, reads=[bp], writes=[self.b_kt])
            for t in range(ntile):
                rope = None
                if isS:
                    rt, brt = self.load_rope(t)
                    rope = (ridx, rt[:, rc, :], rt[:, rc + 1, :], brt)
                for ch in range(nkc):
                    p, bp = proj_fm(bigk, bbk, kc0 + ch * 128, t)
                    outf = None
                    if not isS and "K" not in skip:
                        outf = self.rF.next()
                    self.post_qk(p, bp, 512, gk, ones_idx, rope, self.KT[:, ch, t * 512:(t + 1) * 512], [self.b_kt], outf=outf)
                    if not isS and "K" not in skip:
                        kf, bkf = outf
                        pt, bpt = self.bank()
                        for jb in range(4):
                            S.op("pe", "transpose", pt[:, jb * 128:(jb + 1) * 128], kf[:, jb * 128:(jb + 1) * 128], self.cf[:, 0, :],
                                 reads=[bkf, self.b_c], writes=[bpt])
                        stg, bstg = self.rSO.next()
                        S.op("act", "activation", out=stg[:], in_=pt[:], func=AF.Copy, reads=[bpt], writes=[bstg])
                        for jb in range(4):
                            S.op("sp", "dma_start", out=sK[jb // 2, l, (jb % 2) * 128:(jb % 2) * 128 + 128, ch * 128:(ch + 1) * 128],
                                                                                 in_=stg[:, jb * 128:(jb + 1) * 128], reads=[bstg], dma=bstg.name)
                for jb in range(4):
                    p, bp = self.bank()
                    tok = t * 512 + jb * 128
                    for k in range(16):
                        S.op("pe", "matmul", p[:, 0:nv], self.hseq[:, k, tok:tok + 128], bigv[:, k, vc0:vc0 + nv], start=(k == 0), stop=(k == 15),
                             reads=bbv + [bh], writes=[bp])
                    if isS:
                        S.op("act", "activation", out=self.Vs[:, t * 4 + jb, 0:nv], in_=p[:, 0:nv], func=AF.Copy, reads=[bp], writes=[self.b_v])
                    else:
                        stg, bstg = self.rSO.next()
                        S.op("dve", "tensor_copy", out=stg[:, 0:nv], in_=p[:, 0:nv], reads=[bp], writes=[bstg])
                        S.op("act", "activation", out=self.Vs[:, t * 4 + jb, 0:nv], in_=stg[:, 0:nv], func=AF.Copy, reads=[bstg], writes=[self.b_v])
                        S.op("sp", "dma_start", out=sV[jb // 2, l, (jb % 2) * 128:(jb % 2) * 128 + 128, :], in_=stg[:, 0:nv], reads=[bstg], dma=bstg.name)
            for qs in range(nq // 4):
                bigq, bbq = self.load_big(self.w_in[l][:, qcol + qs * 512:qcol + (qs + 1) * 512], 16, 512)
                ropes = {}

                def unit_gen(t, hh, holder):
                    if isS and t not in ropes:
                        rt, brt = self.load_rope(t)
                        ropes[t] = (ridx, rt[:, rc, :], rt[:, rc + 1, :], brt)
                    p, bp = proj_fm(bigq, bbq, hh * 128, t)
                    qT, bqT = self.rQT.next()
                    holder["q"] = (qT, bqT)
                    yield from self.post_qk_gen(p, bp, 512, gq, ones_idx, ropes.get(t), qT[:], [bqT])

                units = [(t, hh) for t in range(ntile) for hh in range(4)]
                hold = {}
                cur = unit_gen(units[0][0], units[0][1], hold)
                for _ in cur:
                    pass
                for ui, (t, hh) in enumerate(units):
                    h = qs * 4 + hh
                    qT, bqT = hold["q"]
                    nhold = {}
                    nxt = unit_gen(units[ui + 1][0], units[ui + 1][1], nhold) if ui + 1 < len(units) else None
                    dst = h if kind == "A" else 8 + h
                    if isS:
                        self.attention(kind, h, qT, bqT, 0, 512, list(range(20)), dst, t * 512, hook=nxt)
                    else:
                        for s in range(2):
                            self.attention(kind, h, qT, bqT, s * 256, 256, [2 * s, 2 * s + 1], dst, TS + s * 256, hook=None)
                    if nxt is not None:
                        for _ in nxt:
                            pass
                    hold = nhold

    def build(self, stop=None):
        import os
        S = self.S
        stop = int(os.environ.get("K_STOP", "999")) if stop is None else stop
        step = [0]

        def go():
            step[0] += 1
            return step[0] <= stop

        def body():
            if not go(): return
            self.setup()
            if not go(): return
            self.adaln(0)
            for st in range(2):
                if not go(): return
                self.stage0(st)
                if not go(): return
                self.norm(st, 0, 1)
                self.store_xs(st)
            S.barrier()
            for l in range(self.depth):
                if not go(): return
                self.layer_small(l)
                for grp in ("S", "P"):
                    if not go(): return
                    self.stage1(l, grp)
                if l + 1 < self.depth:
                    self.adaln(l + 1)
                S.barrier()
                for st in range(2):
                    if not go(): return
                    self.stage2(l, st)
                    if not go(): return
                    if l + 1 < self.depth:
                        self.norm(st, l + 1, 1)
                        self.store_xs(st)
                    else:
                        self.final_out(st)
                S.barrier()

        body()
        S.barrier()
        S.finalize()
        S.run(self.nc)
        return self.nc


def _consts():
    f32 = np.float32
    ident = np.eye(128, dtype=f32)

    def rm(hd):
        R = np.zeros((128, 128), f32)
        q = hd // 4
        for base in range(0, 128, hd):
            for half in range(2):
                b0 = base + half * (hd // 2)
                for i in range(q):
                    R[b0 + q + i, b0 + i] = -1.0
                    R[b0 + i, b0 + q + i] = 1.0
        return R

    onesD = np.full((128, 128), 1.0 / 2048, f32)
    onesA = np.full((128, 128), 1.0 / 128, f32)
    onesB = np.zeros((128, 128), f32)
    onesB[0:64, 0:64] = 1.0 / 64
    onesB[64:128, 64:128] = 1.0 / 64
    ones1 = np.ones((128, 128), f32)
    cd = np.arange(128)
    ang = 2 * np.pi * np.outer(cd, cd) / 128
    Cc = (np.cos(ang) / np.sqrt(128)).astype(f32)
    Sc = (np.sin(ang) / np.sqrt(128)).astype(f32)
    cmat = np.stack([ident, rm(128), rm(64), onesD, onesA, onesB, ones1, Cc, Sc], axis=1)

    def rope_tab(hd):
        t = np.arange(TS)
        row = (t // 64).astype(np.float64)
        col = (t % 64).astype(np.float64)
        nf = hd // 4
        inv = 10000.0 ** (-np.arange(nf, dtype=np.float64) / nf)
        inv = inv.astype(f32).astype(np.float64)
        ang = np.concatenate([row[:, None] * inv, row[:, None] * inv, col[:, None] * inv, col[:, None] * inv], axis=1)
        ang = ang.astype(f32)
        c = np.cos(ang).T.astype(f32)
        s = np.sin(ang).T.astype(f32)
        reps = 128 // hd
        return np.tile(c, (reps, 1)), np.tile(s, (reps, 1))

    cA, sA = rope_tab(128)
    cB, sB = rope_tab(64)
    rope = np.stack([cA, sA, cB, sB], axis=0).astype(f32)

    def dft(T):
        t = np.arange(T, dtype=np.int64)
        m = np.outer(t, t) % T
        a = 2 * np.pi * m / T
        return np.stack([np.cos(a) / np.sqrt(T), np.sin(a) / np.sqrt(T)], axis=0).astype(f32)

    return cmat.astype(f32), rope, dft(TS), dft(TP)


_CACHE = {}


def make_in_maps(inp, ncores=8, depth=4):
    f32 = np.float32
    g = lambda k: np.asarray(inp[k], dtype=f32)
    cmat, rope, dftS, dftP = _consts()
    b_adaT = np.ascontiguousarray(g("b_ada").reshape(4, 96, 128).transpose(0, 2, 1))
    gT = np.ascontiguousarray(np.stack([g("norm_mix_g").reshape(4, 16, 128), g("norm_mlp_g").reshape(4, 16, 128)], axis=1).transpose(0, 3, 1, 2))
    smallv = np.zeros((4, 128, 8), f32)
    smallv[:, :, 0] = g("q_norm_a")
    smallv[:, :, 1] = g("k_norm_a")
    smallv[:, :, 2] = np.tile(g("q_norm_b"), (1, 2))
    smallv[:, :, 3] = np.tile(g("k_norm_b"), (1, 2))
    smallv[:, :, 4] = g("subln_g")
    lam = np.stack([g("lambda_q1"), g("lambda_k1"), g("lambda_q2"), g("lambda_k2")], axis=1)
    lamv = np.ascontiguousarray(np.broadcast_to(lam[:, None], (4, 128, 4, 64)))
    L = depth
    shared = dict(w_ada=g("w_ada")[:L], b_adaT=b_adaT[:L], gT=gT[:L], w_in=g("w_in")[:L], w_out=g("w_out")[:L], w1=g("w_mlp_in")[:L],
                  w2=g("w_mlp_out")[:L], w_f=g("w_fourier")[:L], smallv=smallv[:L], lamv=lamv[:L], cmat=cmat, rope=rope, dftS=dftS, dftP=dftP)
    xs, xp = g("x_sample"), g("x_prompt")
    cak, cav, cbk, cbv = g("cache_attn_k"), g("cache_attn_v"), g("cache_diff_k"), g("cache_diff_v")
    c, cctx = g("c"), g("c_ctx")
    maps = []
    for i in range(ncores):
        cond = np.stack([c[i], cctx], axis=1)
        condT = np.ascontiguousarray(cond.reshape(16, 128, 2).transpose(1, 0, 2))
        m = dict(shared)
        m.update(x_s=np.ascontiguousarray(xs[i]), x_p=np.ascontiguousarray(xp[2 * i:2 * i + 2].reshape(2 * TP, D)),
                 cak=np.ascontiguousarray(cak[i].reshape(4, 512, 256)), cav=np.ascontiguousarray(cav[i].reshape(4, 512, 256)),
                 cbk=np.ascontiguousarray(cbk[i].reshape(4, 512, 512)), cbv=np.ascontiguousarray(cbv[i].reshape(4, 512, 512)),
                 condT=condT)
        maps.append(m)
    return maps


def assemble(results, ncores=8, depth=4):
    f32 = np.float32
    y_p = np.zeros((2 * ncores, TP, D), f32)
    y_s = np.zeros((ncores, TS, D), f32)
    sKA = np.zeros((2 * ncores, depth, TP, 2, 128), f32)
    sVA = np.zeros((2 * ncores, depth, TP, 2, 128), f32)
    sKB = np.zeros((2 * ncores, depth, TP, 4, 2, 64), f32)
    sVB = np.zeros((2 * ncores, depth, TP, 4, 128), f32)
    for i, r in enumerate(results):
        y_s[i] = r["y_s"]
        y_p[2 * i:2 * i + 2] = r["y_p"].reshape(2, TP, D)
        sKA[2 * i:2 * i + 2] = r["sKA"].reshape(2, depth, TP, 2, 128)
        sVA[2 * i:2 * i + 2] = r["sVA"].reshape(2, depth, TP, 2, 128)
        sKB[2 * i:2 * i + 2] = r["sKB"].reshape(2, depth, TP, 4, 2, 64)
        sVB[2 * i:2 * i + 2] = r["sVB"].reshape(2, depth, TP, 4, 128)
    return (y_p, y_s, sKA, sVA, sKB, sVB)


def kernel(**inputs):
    ncores, depth = 8, 4
    if "nc" not in _CACHE:
        _CACHE["nc"] = Prog(depth).build()
    nc = _CACHE["nc"]
    maps = make_in_maps(inputs, ncores, depth)
    res = run_bass_kernel_spmd(nc, maps, core_ids=list(range(ncores)))
    return assemble(res.results, ncores, depth)
```

```python
import math
import numpy as np
import concourse.bass as bass
import concourse.mybir as mybir
from concourse.bass_utils import run_bass_kernel_spmd

F32 = mybir.dt.float32
BF16 = mybir.dt.bfloat16
AF = mybir.ActivationFunctionType
ALU = mybir.AluOpType
AX = mybir.AxisListType
ENGS = ["pe", "act", "dve", "pool", "sp"]

D = 2048
NCH = 16
TS = 2048
TP = 256
NTOK = 2560
DFF = 8192
INW = 3584
EPS = 1e-6
SEM_EPOCH = 30000


class Buf:
    __slots__ = ("name", "w", "r")

    def __init__(self, name):
        self.name = name
        self.w = None
        self.r = []


class Sched:
    def __init__(self):
        self.ops = {e: [] for e in ENGS}
        self.dma_cnt = {}
        self.last_c = {e: None for e in ENGS}

    def op(self, eng, meth, *args, reads=(), writes=(), dma=None, **kw):
        fn = lambda e: getattr(e, meth)(*args, **kw)
        deps = []
        for b in reads:
            if b.w is not None:
                deps.append(b.w)
        for b in writes:
            if b.w is not None:
                deps.append(b.w)
            for t in b.r:
                if t[0] == "c" and t[1] == eng and dma is None and eng == "pe":
                    continue
                deps.append(t)
        idx = len(self.ops[eng])
        if dma is None:
            tok = ("c", eng, idx)
            if eng == "pe":
                deps = [t for t in deps if not (t[0] == "c" and t[1] == "pe")]
            self.last_c[eng] = tok
        else:
            n = self.dma_cnt.get(dma, 0) + 1
            self.dma_cnt[dma] = n
            tok = ("d", dma, n)
        self.ops[eng].append(dict(fn=fn, deps=deps, tok=tok, marked=False, dma=dma))
        for b in reads:
            b.r.append(tok)
        for b in writes:
            b.w = tok
            b.r = []
        return tok

    def barrier(self):
        toks = [t for t in self.last_c.values() if t is not None]
        toks += [("d", ch, n) for ch, n in self.dma_cnt.items()]
        for e in ENGS:
            self.ops[e].append(dict(fn=lambda eng: eng.nop(), deps=list(toks), tok=("n", e, len(self.ops[e])), marked=False, dma=None))

    def finalize(self):
        for e in ENGS:
            for o in self.ops[e]:
                for t in o["deps"]:
                    if t[0] == "c":
                        self.ops[t[1]][t[2]]["marked"] = True
        self.cnt = {}
        for e in ENGS:
            c = 0
            arr = []
            for o in self.ops[e]:
                if o["marked"]:
                    c += 1
                arr.append(c)
            self.cnt[e] = arr
        self.sem_keys = set()
        nw = 0
        for e in ENGS:
            known = {}
            for o in self.ops[e]:
                w = {}
                for t in o["deps"]:
                    if t[0] == "c":
                        c = self.cnt[t[1]][t[2]]
                        ep = (c - 1) // SEM_EPOCH
                        key = ("e", t[1], ep)
                        val = c - ep * SEM_EPOCH
                    else:
                        ep = (t[2] - 1) // 1500
                        key = ("d", t[1], ep)
                        val = 16 * (t[2] - ep * 1500)
                    if val > w.get(key, 0):
                        w[key] = val
                ws = []
                for key, val in w.items():
                    if val > known.get(key, 0):
                        known[key] = val
                        ws.append((key, val))
                        self.sem_keys.add(key)
                o["waits"] = ws
                nw += len(ws)
        for e in ENGS:
            for i, o in enumerate(self.ops[e]):
                if o["dma"] is not None:
                    n = o["tok"][2]
                    ep = (n - 1) // 1500
                    o["inc"] = ("d", o["dma"], ep)
                    self.sem_keys.add(o["inc"])
                elif o["marked"]:
                    c = self.cnt[e][i]
                    o["inc"] = ("e", e, (c - 1) // SEM_EPOCH)
                    self.sem_keys.add(o["inc"])
                else:
                    o["inc"] = None
        self.nwaits = nw

    def run(self, nc):
        from contextlib import ExitStack

        with ExitStack() as st:
            sems = {}
            for i, key in enumerate(sorted(self.sem_keys)):
                sems[key] = st.enter_context(nc.semaphore("s%d" % i))
            block = st.enter_context(nc.Block())

            def make(e):
                def body(eng):
                    for o in self.ops[e]:
                        for key, val in o["waits"]:
                            eng.wait_ge(sems[key], val)
                        ins = o["fn"](eng)
                        if o["inc"] is not None:
                            ins.then_inc(sems[o["inc"]], 16 if o["dma"] is not None else 1)

                return body

            block.tensor(make("pe"))
            block.scalar(make("act"))
            block.vector(make("dve"))
            block.gpsimd(make("pool"))
            block.sync(make("sp"))


class Rot:
    def __init__(self, items):
        self.items = items
        self.i = 0

    def next(self):
        it = self.items[self.i % len(self.items)]
        self.i += 1
        return it


SUBT = {0: [(0, 512, 0), (512, 512, 0), (1024, 256, 0)], 1: [(0, 256, 0), (256, 512, 0), (768, 512, 1)]}


class Prog:
    def __init__(self, depth):
        self.depth = depth
        nc = self.nc = bass.Bass("TRN2", target_bir_lowering=False)
        self.S = Sched()
        L = depth

        def din(name, shape, dt=F32):
            return nc.dram_tensor(name, list(shape), dt, kind="ExternalInput").ap()

        def dout(name, shape, dt=F32):
            return nc.dram_tensor(name, list(shape), dt, kind="ExternalOutput").ap()

        self.x_s = din("x_s", [TS, D])
        self.x_p = din("x_p", [2 * TP, D])
        self.cak = din("cak", [4, 512, 256])
        self.cav = din("cav", [4, 512, 256])
        self.cbk = din("cbk", [4, 512, 512])
        self.cbv = din("cbv", [4, 512, 512])
        self.condT = din("condT", [128, 16, 2])
        self.w_ada = din("w_ada", [L, D, 6 * D])
        self.b_adaT = din("b_adaT", [L, 128, 96])
        self.gT = din("gT", [L, 128, 2, 16])
        self.w_in = din("w_in", [L, D, INW])
        self.w_out = din("w_out", [L, D, D])
        self.w1 = din("w1", [L, D, DFF])
        self.w2 = din("w2", [L, DFF, D])
        self.w_f = din("w_f", [L, 4, 128, 128])
        self.smallv = din("smallv", [L, 128, 8])
        self.lamv = din("lamv", [L, 128, 4, 64])
        self.cmat = din("cmat", [128, 9, 128])
        self.rope = din("rope", [4, 128, TS])
        self.dftS = din("dftS", [2, TS, TS])
        self.dftP = din("dftP", [2, TP, TP])
        self.y_s = dout("y_s", [TS, D])
        self.y_p = dout("y_p", [2 * TP, D])
        self.sKA = dout("sKA", [2, L, TP, 256])
        self.sVA = dout("sVA", [2, L, TP, 256])
        self.sKB = dout("sKB", [2, L, TP, 512])
        self.sVB = dout("sVB", [2, L, TP, 512])
        self.xT = nc.dram_tensor("xT_scr", [NCH, 128, NTOK], F32, kind="Internal").ap()
        self.hT = nc.dram_tensor("hT_scr", [NCH, 128, NTOK], BF16, kind="Internal").ap()
        self.mixT = nc.dram_tensor("mixT_scr", [NCH, 128, NTOK], BF16, kind="Internal").ap()

        self.off = 16512

        def sb(name, shape, dt, at=None):
            nbytes = int(np.prod(shape[1:])) * (4 if dt == F32 else 2)
            nbytes = (nbytes + 31) // 32 * 32
            if at is None:
                o = self.off
                self.off += nbytes
            else:
                o = at
            assert o + nbytes <= 229344, (name, o, nbytes)
            return nc.alloc_sbuf_tensor_at(name, list(shape), dt, offset=o)

        self.cf = sb("cf", [128, 3, 128], F32)
        self.cb = sb("cb", [128, 6, 128], BF16)
        self.b_c = Buf("consts")
        self.epsc = sb("epsc", [128, 8], F32)
        self.condS = sb("condS", [128, 16, 2], BF16)
        self.condF = sb("condF", [128, 16, 2], F32)
        self.mod = [sb(f"mod{i}", [128, 96, 2], F32) for i in range(2)]
        self.a12 = [sb(f"a12{i}", [128, 2, 16, 2], F32) for i in range(2)]
        self.b_mod = [Buf("mod0"), Buf("mod1")]
        self.gTs = sb("gTs", [128, 2, 16], F32)
        self.badaS = sb("badaS", [128, 96], F32)
        self.smallS = sb("smallS", [128, 8], F32)
        self.lamS = sb("lamS", [128, 4, 64], F32)
        self.lamT = sb("lamT", [128, 2, 64], F32)
        self.lamR = sb("lamR", [128, 8], F32)
        self.b_small = Buf("small")
        self.b_ada = Buf("ada")
        self.adab = [sb(f"adab{i}", [128, 16, 128], BF16) for i in range(2)]
        self.b_adab = [Buf("adab0"), Buf("adab1")]
        self.ada_i = 0
        self.wfS = sb("wfS", [128, 4, 128], BF16)
        self.AB = sb("AB", [128, 4, 256], BF16)
        self.b_AB = Buf("AB")
        self.b_wf = Buf("wf")
        self.tF = [sb(f"tF{i}", [128, 512], F32) for i in range(6)]
        self.tB = [sb(f"tB{i}", [128, 512], BF16) for i in range(8)]
        self.rF = Rot([(t, Buf(f"tF{i}")) for i, t in enumerate(self.tF)])
        self.rB = Rot([(t, Buf(f"tB{i}")) for i, t in enumerate(self.tB)])
        s1 = self.off
        self.rOB = Rot([(sb(f"ob{i}", [128, 512], BF16), Buf(f"ob{i}")) for i in range(3)])
        self.rSO = Rot([(sb(f"so{i}", [128, 512], F32), Buf(f"so{i}")) for i in range(2)])
        self.rQT = Rot([(sb(f"qT{i}", [128, 512], BF16), Buf(f"qT{i}")) for i in range(3)])
        assert self.off - s1 >= 4096 + 5120
        self.rRN = Rot([(sb(f"rN{i}", [128, 512], F32, at=s1 + i * 2048), Buf(f"rN{i}")) for i in range(2)])
        self.ub1_at = s1 + 4096
        slab0 = self.off
        self.off += 4 * 8192
        self.big = [sb(f"big{j}", [128, 16, 512], BF16, at=slab0 + j * 16384) for j in range(2)]
        self.u16 = [sb(f"u16_{i}", [128, 16, 256], BF16, at=slab0 + i * 8192) for i in range(4)]
        self.u2 = [sb(f"u2_{i}", [128, 2, 2048], BF16, at=slab0 + i * 8192) for i in range(4)]
        self.b_unit = [Buf(f"unit{i}") for i in range(4)]
        self.big_i = 0
        self.unit_i = 0
        ov = self.off
        o = ov
        self.hseq = sb("hseq", [128, 16, TS], BF16, at=o); o += 16 * TS * 2
        self.KT = sb("KT", [128, 4, 2560], BF16, at=o)
        self.XAB = sb("XAB", [128, 16, 2, 512], BF16, at=o)
        self.Vs = sb("Vs", [128, 20, 512], BF16, at=o + 4 * 2560 * 2)
        o += 4 * 2560 * 2 + 20 * 512 * 2
        self.ropeT = [sb(f"ropeT{i}", [128, 4, 512], F32, at=o + i * 8192) for i in range(2)]
        o += 16384
        self.cstage = sb("cstage", [128, 4, 512], F32, at=o - 8192)
        assert o <= 229344, o
        self.b_hseq = Buf("hseq")
        self.b_kt = Buf("kt")
        self.b_v = Buf("v")
        self.b_rope = [Buf("rope0"), Buf("rope1")]
        self.b_cst = self.b_rope[1]
        self.rope_i = 0
        o = ov
        self.xs = sb("xs", [128, 16, 1280], F32, at=o); o += 16 * 1280 * 4
        self.h2 = sb("h2", [128, 16, 1280], BF16, at=o); o += 16 * 1280 * 2
        self.tm = [sb(f"tm{i}", [128, D], F32, at=o - 16 * 1280 * 2 + i * 10240) for i in range(2)]
        self.ub = [sb("ub0", [128, 2, 1280], BF16, at=o), sb("ub1", [128, 2, 1280], BF16, at=self.ub1_at)]; o += 2 * 1280 * 2
        assert o <= 229344, o
        self.b_xs = [Buf(f"xs{c}") for c in range(16)]
        self.b_h2 = [Buf(f"h2{c}") for c in range(16)]
        self.b_u = [[Buf("u00"), Buf("u01")], [Buf("u10"), Buf("u11")]]
        self.b_tm = [Buf("tm0"), Buf("tm1")]
        self.ps = [nc.alloc_psum_tensor(f"ps{i}", [128, 512], F32) for i in range(8)]
        self.b_ps = [Buf(f"ps{i}") for i in range(8)]
        self.rP = Rot([0, 1, 2])
        self.rQ = Rot([3])
        self.acc_i = 0
        self.rP2 = Rot([0, 1, 2, 4, 5, 6, 7])

    def bank(self):
        i = self.rP.next()
        return self.ps[i], self.b_ps[i]

    def bank2(self):
        i = self.rP2.next()
        return self.ps[i], self.b_ps[i]

    def bankq(self):
        i = self.rQ.next()
        return self.ps[i], self.b_ps[i]

    def load_big(self, src2d, kch, ncols):
        j = self.big_i % 2
        self.big_i += 1
        t = self.big[j]
        bufs = [self.b_unit[2 * j], self.b_unit[2 * j + 1]]
        self.S.op("pool", "dma_start", out=t[:, 0:kch, 0:ncols], in_=src2d.rearrange("(k p) n -> p k n", p=128),
                  writes=bufs, dma=f"unit{2*j}")
        return t, bufs

    def load_unit(self, src2d, view):
        i = self.unit_i % 4
        self.unit_i += 1
        t = self.u16[i] if view == 16 else self.u2[i]
        self.S.op("pool", "dma_start", out=t[:], in_=src2d.rearrange("(k p) n -> p k n", p=128),
                  writes=[self.b_unit[i]], dma=f"unit{i}")
        return t, [self.b_unit[i]]

    def rsqrt(self, rinv, brv, pq, bq, n):
        S = self.S
        S.op("act", "activation", out=rinv[:, 0:n], in_=pq[:, 0:n], func=AF.Ln, bias=self.epsc[:, 0:1], reads=[bq, self.b_c], writes=[brv])
        S.op("act", "activation", out=rinv[:, 0:n], in_=rinv[:, 0:n], func=AF.Exp, scale=-0.5, reads=[brv], writes=[brv])

    def setup(self):
        S = self.S
        S.op("dve", "memset", self.epsc[:], EPS, writes=[self.b_c])
        S.op("sp", "dma_start", out=self.cf[:], in_=self.cmat[:, 0:3, :], writes=[self.b_c], dma="cc")
        S.op("pool", "dma_start", out=self.cb[:], in_=self.cmat[:, 3:9, :], writes=[self.b_c], dma="ccp")
        S.op("sp", "dma_start", out=self.condF[:], in_=self.condT, writes=[self.b_small], dma="cs")
        S.op("act", "activation", out=self.condS[:], in_=self.condF[:], func=AF.Silu, reads=[self.b_small], writes=[self.b_c])

    def adaln(self, l):
        for _ in self.adaln_gen(l):
            pass

    def adaln_gen(self, l):
        S = self.S
        par = l % 2
        mod = self.mod[par]
        a12 = self.a12[par]
        S.op("sp", "dma_start", out=self.badaS[:], in_=self.b_adaT[l], writes=[self.b_ada], dma="cad")
        S.op("sp", "dma_start", out=self.gTs[:], in_=self.gT[l], writes=[self.b_ada], dma="cad")
        for j in range(96):
            i = self.ada_i % 2
            self.ada_i += 1
            ad, bad = self.adab[i], self.b_adab[i]
            S.op("pool", "dma_start", out=ad[:], in_=self.w_ada[l][:, j * 128:(j + 1) * 128].rearrange("(k p) n -> p k n", p=128),
                 writes=[bad], dma=f"adab{i}")
            p, bp = self.bankq()
            for k in range(16):
                S.op("pe", "matmul", p[:, 0:2], ad[:, k, :], self.condS[:, k, :], start=(k == 0), stop=(k == 15), reads=[bad, self.b_c], writes=[bp])
            S.op("dve", "tensor_scalar", out=mod[:, j, :], in0=p[:, 0:2], scalar1=self.badaS[:, j:j + 1], scalar2=None, op0=ALU.add,
                 reads=[bp, self.b_ada], writes=[self.b_mod[par]])
            yield
        for i in range(2):
            for w, (sc0, gi) in enumerate([(16, 0), (64, 1)]):
                S.op("dve", "scalar_tensor_tensor", out=a12[:, w, :, i], in0=mod[:, sc0:sc0 + 16, i], scalar=1.0, in1=self.gTs[:, gi, :],
                     op0=ALU.add, op1=ALU.mult, reads=[self.b_mod[par], self.b_ada], writes=[self.b_mod[par]])
        yield

    def mvec(self, l, which, c, ci):
        par = l % 2
        if which == "a1":
            return self.a12[par][:, 0, c, ci:ci + 1]
        if which == "a2":
            return self.a12[par][:, 1, c, ci:ci + 1]
        base = {"shift1": 0, "gate1": 32, "shift2": 48, "gate2": 80}[which]
        return self.mod[par][:, base + c, ci:ci + 1]

    def layer_small(self, l):
        S = self.S
        lam_init = 0.8 - 0.6 * math.exp(-0.3 * l)
        bs = self.b_small
        S.op("sp", "dma_start", out=self.smallS[:], in_=self.smallv[l], writes=[bs], dma="cs")
        S.op("sp", "dma_start", out=self.lamS[:], in_=self.lamv[l], writes=[bs], dma="cs")
        for j in range(2):
            S.op("dve", "tensor_tensor", out=self.lamT[:, j, :], in0=self.lamS[:, 2 * j, :], in1=self.lamS[:, 2 * j + 1, :], op=ALU.mult,
                 reads=[bs], writes=[bs])
            S.op("dve", "reduce_sum", out=self.lamR[:, j:j + 1], in_=self.lamT[:, j, :], axis=AX.X, reads=[bs], writes=[bs])
            S.op("act", "activation", out=self.lamR[:, 2 + j:3 + j], in_=self.lamR[:, j:j + 1], func=AF.Exp, reads=[bs], writes=[bs])
        S.op("dve", "scalar_tensor_tensor", out=self.lamR[:, 4:5], in0=self.lamR[:, 3:4], scalar=-lam_init, in1=self.lamR[:, 2:3],
                                                       op0=ALU.add, op1=ALU.subtract, reads=[bs], writes=[bs])
        S.op("dve", "tensor_scalar", out=self.lamR[:, 5:6], in0=self.smallS[:, 4:5], scalar1=1.0 - lam_init, scalar2=None, op0=ALU.mult,
             reads=[bs], writes=[bs])
        S.op("pool", "dma_start", out=self.wfS[:], in_=self.w_f[l].rearrange("g c d -> c g d"), writes=[self.b_wf], dma="cw")
        for g in range(4):
            for ab in range(2):
                p, bp = self.bankq()
                S.op("pe", "matmul", p[:, 0:128], self.cb[:, 4 + ab, :], self.wfS[:, g, :], start=True, stop=True,
                     reads=[self.b_c, self.b_wf], writes=[bp])
                S.op("act", "activation", out=self.AB[:, g, ab * 128:(ab + 1) * 128], in_=p[:, 0:128], func=AF.Copy,
                                                                     scale=(1.0 if ab == 0 else -1.0), reads=[bp], writes=[self.b_AB])

    def norm(self, st, l, kind):
        S = self.S
        aw, sw = ("a1", "shift1") if kind == 1 else ("a2", "shift2")
        g0 = st * 1280
        for (off, n, ci) in SUBT[st]:
            pq, bq = self.bankq()
            for c in range(16):
                sq, bsq = self.rB.next()
                S.op("act", "activation", out=sq[:, 0:n], in_=self.xs[:, c, off:off + n], func=AF.Square,
                     reads=[self.b_xs[c]], writes=[bsq])
                S.op("pe", "matmul", pq[:, 0:n], self.cb[:, 0, :], sq[:, 0:n], start=(c == 0), stop=(c == 15),
                     reads=[bsq, self.b_c], writes=[bq])
            rinv, brv = self.rRN.next()
            self.rsqrt(rinv, brv, pq, bq, n)
            for c in range(16):
                t, bt = self.rF.next()
                S.op("dve", "scalar_tensor_tensor", out=t[:, 0:n], in0=self.xs[:, c, off:off + n], scalar=self.mvec(l, aw, c, ci),
                                                                                 in1=rinv[:, 0:n], op0=ALU.mult, op1=ALU.mult,
                     reads=[self.b_xs[c], brv, self.b_mod[l % 2]], writes=[bt])
                S.op("act", "activation", out=self.h2[:, c, off:off + n], in_=t[:, 0:n], func=AF.Identity, bias=self.mvec(l, sw, c, ci),
                     reads=[bt, self.b_mod[l % 2]], writes=[self.b_h2[c]])
            if kind == 1:
                S.op("sp", "dma_start", out=self.hT[:, :, g0 + off:g0 + off + n].rearrange("c p n -> p c n"), in_=self.h2[:, :, off:off + n],
                     reads=self.b_h2, dma="h2io")

    def src_rows(self, g, n, sample, prompt):
        return sample[g:g + n, :] if g < TS else prompt[g - TS:g - TS + n, :]

    def stage0(self, st):
        S = self.S
        for b in range(10):
            g = st * 1280 + b * 128
            tm, btm = self.tm[b % 2], self.b_tm[b % 2]
            btmx = [btm] + self.b_h2[4 * (b % 2):4 * (b % 2) + 4]
            S.op("sp", "dma_start", out=tm[:], in_=self.src_rows(g, 128, self.x_s, self.x_p), writes=btmx, dma=f"tm{b%2}")
            for c4 in range(4):
                p, bp = self.bank()
                for cc in range(4):
                    c = c4 * 4 + cc
                    S.op("pe", "transpose", p[:, cc * 128:(cc + 1) * 128], tm[:, c * 128:(c + 1) * 128], self.cf[:, 0, :],
                         reads=btmx + [self.b_c], writes=[bp])
                eng = "act" if c4 % 2 == 0 else "dve"
                if eng == "act":
                    S.op("act", "activation", out=self.xs[:, c4 * 4:c4 * 4 + 4, b * 128:(b + 1) * 128],
                                                                       in_=p[:].rearrange("p (a n) -> p a n", a=4), func=AF.Copy,
                         reads=[bp], writes=self.b_xs[c4 * 4:c4 * 4 + 4])
                else:
                    S.op("dve", "tensor_copy", out=self.xs[:, c4 * 4:c4 * 4 + 4, b * 128:(b + 1) * 128],
                                                                        in_=p[:].rearrange("p (a n) -> p a n", a=4),
                         reads=[bp], writes=self.b_xs[c4 * 4:c4 * 4 + 4])

    def store_xs(self, st):
        g0 = st * 1280
        self.S.op("sp", "dma_start", out=self.xT[:, :, g0:g0 + 1280].rearrange("c p n -> p c n"), in_=self.xs[:], reads=self.b_xs, dma="xsio")

    def final_out(self, st):
        S = self.S
        for b in range(10):
            g = st * 1280 + b * 128
            tm, btm = self.tm[b % 2], self.b_tm[b % 2]
            btmx = [btm] + self.b_h2[4 * (b % 2):4 * (b % 2) + 4]
            for c4 in range(4):
                p, bp = self.bank()
                for cc in range(4):
                    c = c4 * 4 + cc
                    S.op("pe", "transpose", p[:, cc * 128:(cc + 1) * 128], self.xs[:, c, b * 128:(b + 1) * 128], self.cf[:, 0, :],
                         reads=[self.b_xs[c], self.b_c], writes=[bp])
                if c4 % 2 == 0:
                    S.op("act", "activation", out=tm[:, c4 * 512:(c4 + 1) * 512], in_=p[:], func=AF.Copy, reads=[bp], writes=btmx)
                else:
                    S.op("dve", "tensor_copy", out=tm[:, c4 * 512:(c4 + 1) * 512], in_=p[:], reads=[bp], writes=btmx)
            S.op("sp", "dma_start", out=self.src_rows(g, 128, self.y_s, self.y_p), in_=tm[:], reads=btmx, dma=f"tm{b%2}")

    def stage2(self, l, st):
        S = self.S
        g0 = st * 1280
        S.op("sp", "dma_start", out=self.xs[:], in_=self.xT[:, :, g0:g0 + 1280].rearrange("c p n -> p c n"), writes=self.b_xs, dma="xsio")
        S.op("sp", "dma_start", out=self.h2[:], in_=self.mixT[:, :, g0:g0 + 1280].rearrange("c p n -> p c n"), writes=self.b_h2, dma="h2io")
        for sl in range(4):
            big, bb = self.load_big(self.w_out[l][:, sl * 512:(sl + 1) * 512], 16, 512)
            for (off, n, ci) in SUBT[st]:
                for mm in range(4):
                    m = sl * 4 + mm
                    p, bp = self.bank()
                    for k in range(16):
                        S.op("pe", "matmul", p[:, 0:n], big[:, k, mm * 128:(mm + 1) * 128], self.h2[:, k, off:off + n],
                                                                               start=(k == 0), stop=(k == 15),
                             reads=bb + [self.b_h2[k]], writes=[bp])
                    S.op("dve", "scalar_tensor_tensor", out=self.xs[:, m, off:off + n], in0=p[:, 0:n], scalar=self.mvec(l, "gate1", m, ci),
                                                                          in1=self.xs[:, m, off:off + n], op0=ALU.mult, op1=ALU.add,
                         reads=[bp, self.b_xs[m], self.b_mod[l % 2]], writes=[self.b_xs[m]])
        self.norm(st, l, 2)
        nsl = DFF // 256
        subt = SUBT[st]
        wts = {}

        def load1(j):
            wts[j] = self.load_unit(self.w1[l][:, j * 256:(j + 1) * 256], 16)

        def load2(j):
            wts[j] = wts[j] + self.load_unit(self.w2[l][j * 256:(j + 1) * 256, :], 2)

        def load(j):
            load1(j)
            load2(j)

        def w1_group(j, off, n, hc):
            w1u, b1 = wts[j][0], wts[j][1]
            par = j % 2
            p, bp = self.bank2()
            for k in range(16):
                S.op("pe", "matmul", p[:, 0:n], w1u[:, k, hc * 128:(hc + 1) * 128], self.h2[:, k, off:off + n], start=(k == 0), stop=(k == 15),
                     reads=b1 + [self.b_h2[k]], writes=[bp])
            r, br = self.rF.next()
            S.op("act", "activation", out=r[:, 0:n], in_=p[:, 0:n], func=AF.Relu, reads=[bp], writes=[br])
            S.op("act", "activation", out=self.ub[par][:, hc, off:off + n], in_=r[:, 0:n], func=AF.Square, reads=[br], writes=[self.b_u[par][hc]])

        def w2_group(j, off, n, ci, m):
            _, _, w2u, b2 = wts[j]
            par = j % 2
            p, bp = self.bank2()
            for hc in range(2):
                S.op("pe", "matmul", p[:, 0:n], w2u[:, hc, m * 128:(m + 1) * 128], self.ub[par][:, hc, off:off + n], start=(hc == 0), stop=(hc == 1),
                     reads=b2 + [self.b_u[par][hc]], writes=[bp])
            S.op("dve", "scalar_tensor_tensor", out=self.xs[:, m, off:off + n], in0=p[:, 0:n], scalar=self.mvec(l, "gate2", m, ci),
                 in1=self.xs[:, m, off:off + n], op0=ALU.mult, op1=ALU.add, reads=[bp, self.b_xs[m], self.b_mod[l % 2]], writes=[self.b_xs[m]])

        g1 = [(off, n, hc) for (off, n, ci) in subt for hc in range(2)]
        g2 = [(off, n, ci, m) for (off, n, ci) in subt for m in range(16)]
        load(0)
        load(1)
        for a in g1:
            w1_group(0, *a)
        for j in range(nsl):
            if j + 2 < nsl:
                load1(j + 2)
            per = len(g2) // len(g1)
            for gi, a in enumerate(g1):
                if j + 1 < nsl:
                    w1_group(j + 1, *a)
                for b in g2[gi * per:(gi + 1) * per]:
                    w2_group(j, *b)
            wts.pop(j)
            if j + 2 < nsl:
                load2(j + 2)

    def post_qk(self, pz, bpz, n, gcol, ones_idx, rope, out_ap, out_bufs, outf=None):
        for _ in self.post_qk_gen(pz, bpz, n, gcol, ones_idx, rope, out_ap, out_bufs, outf):
            pass

    def post_qk_gen(self, pz, bpz, n, gcol, ones_idx, rope, out_ap, out_bufs, outf=None):
        S = self.S
        sq, bsq = self.rB.next()
        S.op("act", "activation", out=sq[:, 0:n], in_=pz[:, 0:n], func=AF.Square, reads=[bpz], writes=[bsq])
        zg, bzg = self.rF.next()
        S.op("act", "activation", out=zg[:, 0:n], in_=pz[:, 0:n], func=AF.Copy, scale=gcol, reads=[bpz, self.b_small], writes=[bzg])
        yield
        pq, bq = self.bankq()
        S.op("pe", "matmul", pq[:, 0:n], self.cb[:, ones_idx, :], sq[:, 0:n], start=True, stop=True, reads=[bsq, self.b_c], writes=[bq])
        rinv, brv = self.rF.next()
        self.rsqrt(rinv, brv, pq, bq, n)
        if rope is not None:
            ridx, cos, sin, brope = rope
            t1, bt1 = self.rF.next()
            S.op("pool", "tensor_tensor", out=t1[:, 0:n], in0=zg[:, 0:n], in1=cos, op=ALU.mult, reads=[bzg, brope], writes=[bt1])
            yield
            pr, bpr = self.bankq()
            S.op("pe", "matmul", pr[:, 0:n], self.cf[:, ridx, :], zg[:, 0:n], start=True, stop=True, reads=[bzg, self.b_c], writes=[bpr])
            t2, bt2 = self.rF.next()
            S.op("dve", "tensor_tensor", out=t2[:, 0:n], in0=pr[:, 0:n], in1=sin, op=ALU.mult, reads=[bpr, brope], writes=[bt2])
            S.op("dve", "tensor_tensor", out=t1[:, 0:n], in0=t1[:, 0:n], in1=t2[:, 0:n], op=ALU.add, reads=[bt1, bt2], writes=[bt1])
            S.op("dve", "tensor_tensor", out=out_ap, in0=t1[:, 0:n], in1=rinv[:, 0:n], op=ALU.mult, reads=[bt1, brv], writes=out_bufs)
        else:
            S.op("dve", "tensor_tensor", out=out_ap, in0=zg[:, 0:n], in1=rinv[:, 0:n], op=ALU.mult, reads=[bzg, brv], writes=out_bufs)
            if outf is not None:
                of, bof = outf
                S.op("dve", "tensor_tensor", out=of[:, 0:n], in0=zg[:, 0:n], in1=rinv[:, 0:n], op=ALU.mult, reads=[bzg, brv], writes=[bof])
        yield

    def load_rope(self, t):
        i = self.rope_i % 2
        self.rope_i += 1
        rt, br = self.ropeT[i], self.b_rope[i]
        self.S.op("sp", "dma_start", out=rt[:], in_=self.rope[:, :, t * 512:(t + 1) * 512].rearrange("f p n -> p f n"), writes=[br], dma=f"rope{i}")
        return rt, br

    def attention(self, kind, h, qT, bq, q0, n, kbs, dst_chunk, gcols, hook=None, bg=None):
        S = self.S
        ncomp = 1 if kind == "A" else 2
        scale = 128 ** -0.5 if kind == "A" else 64 ** -0.5
        ktc = h // 4 if kind == "A" else h
        vcol = (h // 4) * 128 if kind == "A" else h * 128
        oc = []
        nk = len(kbs)
        for comp in range(ncomp):
            ai = self.acc_i % 2
            self.acc_i += 1
            psO, bO = self.ps[4 + 2 * ai], self.b_ps[4 + 2 * ai]
            psS, bS = self.ps[5 + 2 * ai], self.b_ps[5 + 2 * ai]
            r0, r1 = (0, 128) if kind == "A" else (comp * 64, comp * 64 + 64)

            def pv(item):
                pT, bpT, kb, i = item
                S.op("pe", "matmul", psO[:, 0:n], self.Vs[:, kb, vcol:vcol + 128], pT[:, 0:n], start=(i == 0), stop=(i == nk - 1),
                     reads=[self.b_v, bpT], writes=[bO])
                S.op("pe", "matmul", psS[:, 0:n], self.cb[:, 3, :], pT[:, 0:n], start=(i == 0), stop=(i == nk - 1),
                     reads=[self.b_c, bpT], writes=[bS])

            prev = None
            for i, kb in enumerate(kbs):
                ps_s, bs = self.bank()
                S.op("pe", "matmul", ps_s[:, 0:n], self.KT[r0:r1, ktc, kb * 128:(kb + 1) * 128], qT[r0:r1, q0:q0 + n], start=True, stop=True,
                     reads=[self.b_kt, bq], writes=[bs])
                pT, bpT = self.rB.next()
                S.op("act", "activation", out=pT[:, 0:n], in_=ps_s[:, 0:n], func=AF.Exp, scale=scale, reads=[bs], writes=[bpT])
                if prev is not None:
                    pv(prev)
                prev = (pT, bpT, kb, i)
                if hook is not None and comp == 0 and i in (1, 5, 9):
                    next(hook, None)
                if bg is not None and comp == 0 and i in (12, 16):
                    next(bg, None)
            pv(prev)
            rs, brs = self.rF.next()
            S.op("dve", "reciprocal", out=rs[:, 0:n], in_=psS[:, 0:n], reads=[bS], writes=[brs])
            if kind == "A":
                ob, bob = self.rOB.next()
                S.op("dve", "tensor_tensor", out=ob[:, 0:n], in0=psO[:, 0:n], in1=rs[:, 0:n], op=ALU.mult, reads=[bO, brs], writes=[bob])
                S.op("sp", "dma_start", out=self.mixT[dst_chunk][:, gcols:gcols + n], in_=ob[:, 0:n], reads=[bob], dma=bob.name)
            else:
                o, bo = self.rF.next()
                S.op("dve", "tensor_tensor", out=o[:, 0:n], in0=psO[:, 0:n], in1=rs[:, 0:n], op=ALU.mult, reads=[bO, brs], writes=[bo])
                oc.append((o, bo))
        if kind == "B":
            (o0, b0), (o1, b1) = oc
            od, bod = self.rF.next()
            S.op("dve", "scalar_tensor_tensor", out=od[:, 0:n], in0=o1[:, 0:n], scalar=self.lamR[:, 4:5], in1=o0[:, 0:n], op0=ALU.mult, op1=ALU.add,
                 reads=[b0, b1, self.b_small], writes=[bod])
            sq, bsq = self.rB.next()
            S.op("act", "activation", out=sq[:, 0:n], in_=od[:, 0:n], func=AF.Square, reads=[bod], writes=[bsq])
            pq, bq2 = self.bankq()
            S.op("pe", "matmul", pq[:, 0:n], self.cb[:, 1, :], sq[:, 0:n], start=True, stop=True, reads=[bsq, self.b_c], writes=[bq2])
            rinv, brv = self.rF.next()
            self.rsqrt(rinv, brv, pq, bq2, n)
            ob, bob = self.rOB.next()
            S.op("dve", "scalar_tensor_tensor", out=ob[:, 0:n], in0=od[:, 0:n], scalar=self.lamR[:, 5:6], in1=rinv[:, 0:n], op0=ALU.mult, op1=ALU.mult,
                 reads=[bod, brv, self.b_small], writes=[bob])
            S.op("sp", "dma_start", out=self.mixT[dst_chunk][:, gcols:gcols + n], in_=ob[:, 0:n], reads=[bob], dma=bob.name)

    def stage1(self, l, grp, bg=None):
        S = self.S
        isS = grp == "S"
        tok0 = 0 if isS else TS
        ntile = 4 if isS else 1
        ntb = 16 if isS else 4
        S.op("sp", "dma_start", out=self.hseq[:, :, 0:ntile * 512], in_=self.hT[:, :, tok0:tok0 + ntile * 512].rearrange("c p n -> p c n"),
             writes=[self.b_hseq], dma="hseq")
        bh = self.b_hseq

        def proj_fm(big, bb, col0, t, n=512):
            p, bp = self.bank()
            for k in range(16):
                S.op("pe", "matmul", p[:, 0:n], big[:, k, col0:col0 + 128], self.hseq[:, k, t * 512:t * 512 + n], start=(k == 0), stop=(k == 15),
                     reads=bb + [bh], writes=[bp])
            return p, bp

        import os
        skip = os.environ.get("K_PSKIP", "") if not isS else ""
        big, bb = self.load_big(self.w_in[l][:, 3072:3584], 16, 512)
        for t in range(ntile if "F" not in skip else 0):
            for g in range(4):
                p, bp = proj_fm(big, bb, g * 128, t)
                ft, bft = self.rB.next()
                S.op("act", "activation", out=ft[:], in_=p[:], func=AF.Copy, reads=[bp], writes=[bft])
                for jb in range(4):
                    px, bpx = self.bank()
                    S.op("pe", "matmul", px[:, 0:256], ft[:, jb * 128:(jb + 1) * 128], self.AB[:, g, :], start=True, stop=True,
                         reads=[bft, self.b_AB], writes=[bpx])
                    tb = t * 4 + jb
                    S.op("dve", "tensor_copy", out=self.XAB[:, tb, :, g * 128:(g + 1) * 128],
                                                                          in_=px[:, 0:256].rearrange("p (a d) -> p a d", a=2),
                         reads=[bpx], writes=[self.b_kt, self.b_v])
        if "F" in skip:
            pass
        elif isS:
            for t in range(4):
                bc, bbc = self.load_big(self.dftS[0][:, t * 512:(t + 1) * 512], 16, 512)
                bsn, bbs = self.load_big(self.dftS[1][:, t * 512:(t + 1) * 512], 16, 512)
                for gd in range(4):
                    p, bp = self.bank()
                    for tb in range(16):
                        S.op("pe", "matmul", p[:], self.XAB[:, tb, 0, gd * 128:(gd + 1) * 128], bc[:, tb, :], start=(tb == 0), stop=False,
                             reads=bbc + [self.b_kt, self.b_v], writes=[bp])
                    for tb in range(16):
                        S.op("pe", "matmul", p[:], self.XAB[:, tb, 1, gd * 128:(gd + 1) * 128], bsn[:, tb, :], start=False, stop=(tb == 15),
                             reads=bbs + [self.b_kt, self.b_v], writes=[bp])
                    ob, bob = self.rOB.next()
                    S.op("act", "activation", out=ob[:], in_=p[:], func=AF.Copy, reads=[bp], writes=[bob])
                    S.op("sp", "dma_start", out=self.mixT[12 + gd][:, t * 512:(t + 1) * 512], in_=ob[:], reads=[bob], dma=bob.name)
        else:
            bc, bbc = self.load_big(self.dftP[0], 2, 256)
            bsn, bbs = self.load_big(self.dftP[1], 2, 256)
            for s in range(2):
                for gd in range(4):
                    p, bp = self.bank()
                    for b2 in range(2):
                        S.op("pe", "matmul", p[:, 0:256], self.XAB[:, s * 2 + b2, 0, gd * 128:(gd + 1) * 128], bc[:, b2, 0:256],
                                                                             start=(b2 == 0), stop=False, reads=bbc + [self.b_kt, self.b_v], writes=[bp])
                    for b2 in range(2):
                        S.op("pe", "matmul", p[:, 0:256], self.XAB[:, s * 2 + b2, 1, gd * 128:(gd + 1) * 128], bsn[:, b2, 0:256],
                                                                             start=False, stop=(b2 == 1), reads=bbs + [self.b_kt, self.b_v], writes=[bp])
                    ob, bob = self.rOB.next()
                    S.op("act", "activation", out=ob[:, 0:256], in_=p[:, 0:256], func=AF.Copy, reads=[bp], writes=[bob])
                    S.op("sp", "dma_start", out=self.mixT[12 + gd][:, TS + s * 256:TS + (s + 1) * 256], in_=ob[:, 0:256],
                         reads=[bob], dma=bob.name)

        for kind in ("A", "B"):
            if kind in skip:
                continue
            if kind == "A":
                kcol, nkc, vcol, nv, qcol, nq = 1024, 2, 1280, 256, 0, 8
                cK, cV, sK, sV = self.cak, self.cav, self.sKA, self.sVA
                gk, gq, ones_idx, ridx, rc = self.smallS[:, 1:2], self.smallS[:, 0:1], 1, 1, 0
            else:
                kcol, nkc, vcol, nv, qcol, nq = 2048, 4, 2560, 512, 1536, 4
                cK, cV, sK, sV = self.cbk, self.cbv, self.sKB, self.sVB
                gk, gq, ones_idx, ridx, rc = self.smallS[:, 3:4], self.smallS[:, 2:3], 2, 2, 2
            kw = nkc * 128
            if kind == "A":
                bigk, bbk = self.load_big(self.w_in[l][:, 1024:1536], 16, 512)
                bigv, bbv, kc0, vc0 = bigk, bbk, 0, 256
            else:
                bigk, bbk = self.load_big(self.w_in[l][:, 2048:2560], 16, 512)
                bigv, bbv = self.load_big(self.w_in[l][:, 2560:3072], 16, 512)
                kc0, vc0 = 0, 0
            if isS:
                S.op("pool", "dma_start", out=self.Vs[:, 16:20, 0:nv], in_=cV[l].rearrange("(b p) n -> p b n", p=128), writes=[self.b_v], dma="vc")
                S.op("sp", "dma_start", out=self.cstage[:, :, 0:kw], in_=cK[l].rearrange("(b p) n -> p b n", p=128), writes=[self.b_cst], dma="rope1")
                for ch in range(nkc):
                    p, bp = self.bank()
                    for b in range(4):
                        S.op("pe", "transpose", p[:, b * 128:(b + 1) * 128], self.cstage[:, b, ch * 128:(ch + 1) * 128], self.cf[:, 0, :],
                             reads=[self.b_cst, self.b_c], writes=[bp])
                    S.op("act", "activation", out=self.KT[:, ch, 2048:2560], in_=p[:], func=AF.Copy, reads=[bp], writes=[self.b_kt])
            for t in range(ntile):
                rope = None
                if isS:
                    rt, brt = self.load_rope(t)
                    rope = (ridx, rt[:, rc, :], rt[:, rc + 1, :], brt)
                for ch in range(nkc):
                    p, bp = proj_fm(bigk, bbk, kc0 + ch * 128, t)
                    outf = None
                    if not isS and "K" not in skip:
                        outf = self.rF.next()
                    self.post_qk(p, bp, 512, gk, ones_idx, rope, self.KT[:, ch, t * 512:(t + 1) * 512], [self.b_kt], outf=outf)
                    if not isS and "K" not in skip:
                        kf, bkf = outf
                        pt, bpt = self.bank()
                        for jb in range(4):
                            S.op("pe", "transpose", pt[:, jb * 128:(jb + 1) * 128], kf[:, jb * 128:(jb + 1) * 128], self.cf[:, 0, :],
                                 reads=[bkf, self.b_c], writes=[bpt])
                        stg, bstg = self.rSO.next()
                        S.op("act", "activation", out=stg[:], in_=pt[:], func=AF.Copy, reads=[bpt], writes=[bstg])
                        for jb in range(4):
                            S.op("sp", "dma_start", out=sK[jb // 2, l, (jb % 2) * 128:(jb % 2) * 128 + 128, ch * 128:(ch + 1) * 128],
                                                                                 in_=stg[:, jb * 128:(jb + 1) * 128], reads=[bstg], dma=bstg.name)
                for jb in range(4):
                    p, bp = self.bank()
                    tok = t * 512 + jb * 128
                    for k in range(16):
                        S.op("pe", "matmul", p[:, 0:nv], self.hseq[:, k, tok:tok + 128], bigv[:, k, vc0:vc0 + nv], start=(k == 0), stop=(k == 15),
                             reads=bbv + [bh], writes=[bp])
                    if isS:
                        S.op("act", "activation", out=self.Vs[:, t * 4 + jb, 0:nv], in_=p[:, 0:nv], func=AF.Copy, reads=[bp], writes=[self.b_v])
                    else:
                        stg, bstg = self.rSO.next()
                        S.op("dve", "tensor_copy", out=stg[:, 0:nv], in_=p[:, 0:nv], reads=[bp], writes=[bstg])
                        S.op("act", "activation", out=self.Vs[:, t * 4 + jb, 0:nv], in_=stg[:, 0:nv], func=AF.Copy, reads=[bstg], writes=[self.b_v])
                        S.op("sp", "dma_start", out=sV[jb // 2, l, (jb % 2) * 128:(jb % 2) * 128 + 128, :], in_=stg[:, 0:nv], reads=[bstg], dma=bstg.name)
            for qs in range(nq // 4):
                bigq, bbq = self.load_big(self.w_in[l][:, qcol + qs * 512:qcol + (qs + 1) * 512], 16, 512)
                ropes = {}

                def unit_gen(t, hh, holder):
                    if isS and t not in ropes:
                        rt, brt = self.load_rope(t)
                        ropes[t] = (ridx, rt[:, rc, :], rt[:, rc + 1, :], brt)
                    p, bp = proj_fm(bigq, bbq, hh * 128, t)
                    qT, bqT = self.rQT.next()
                    holder["q"] = (qT, bqT)
                    yield from self.post_qk_gen(p, bp, 512, gq, ones_idx, ropes.get(t), qT[:], [bqT])

                units = [(t, hh) for t in range(ntile) for hh in range(4)]
                hold = {}
                cur = unit_gen(units[0][0], units[0][1], hold)
                for _ in cur:
                    pass
                for ui, (t, hh) in enumerate(units):
                    h = qs * 4 + hh
                    qT, bqT = hold["q"]
                    nhold = {}
                    nxt = unit_gen(units[ui + 1][0], units[ui + 1][1], nhold) if ui + 1 < len(units) else None
                    dst = h if kind == "A" else 8 + h
                    if isS:
                        self.attention(kind, h, qT, bqT, 0, 512, list(range(20)), dst, t * 512, hook=nxt, bg=bg)
                    else:
                        for s in range(2):
                            self.attention(kind, h, qT, bqT, s * 256, 256, [2 * s, 2 * s + 1], dst, TS + s * 256, hook=None)
                    if nxt is not None:
                        for _ in nxt:
                            pass
                    hold = nhold

    def build(self, stop=None):
        import os
        S = self.S
        stop = int(os.environ.get("K_STOP", "999")) if stop is None else stop
        step = [0]

        def go():
            step[0] += 1
            return step[0] <= stop

        def body():
            if not go(): return
            self.setup()
            if not go(): return
            self.adaln(0)
            for st in range(2):
                if not go(): return
                self.stage0(st)
                if not go(): return
                self.norm(st, 0, 1)
                self.store_xs(st)
            S.barrier()
            for l in range(self.depth):
                if not go(): return
                self.layer_small(l)
                bg = self.adaln_gen(l + 1) if l + 1 < self.depth else None
                for grp in ("S", "P"):
                    if not go(): return
                    self.stage1(l, grp, bg=bg if grp == "S" else None)
                    if bg is not None:
                        for _ in bg:
                            pass
                S.barrier()
                for st in range(2):
                    if not go(): return
                    self.stage2(l, st)
                    if not go(): return
                    if l + 1 < self.depth:
                        self.norm(st, l + 1, 1)
                        self.store_xs(st)
                    else:
                        self.final_out(st)
                S.barrier()

        body()
        S.barrier()
        S.finalize()
        S.run(self.nc)
        return self.nc


def _consts():
    f32 = np.float32
    ident = np.eye(128, dtype=f32)

    def rm(hd):
        R = np.zeros((128, 128), f32)
        q = hd // 4
        for base in range(0, 128, hd):
            for half in range(2):
                b0 = base + half * (hd // 2)
                for i in range(q):
                    R[b0 + q + i, b0 + i] = -1.0
                    R[b0 + i, b0 + q + i] = 1.0
        return R

    onesD = np.full((128, 128), 1.0 / 2048, f32)
    onesA = np.full((128, 128), 1.0 / 128, f32)
    onesB = np.zeros((128, 128), f32)
    onesB[0:64, 0:64] = 1.0 / 64
    onesB[64:128, 64:128] = 1.0 / 64
    ones1 = np.ones((128, 128), f32)
    cd = np.arange(128)
    ang = 2 * np.pi * np.outer(cd, cd) / 128
    Cc = (np.cos(ang) / np.sqrt(128)).astype(f32)
    Sc = (np.sin(ang) / np.sqrt(128)).astype(f32)
    cmat = np.stack([ident, rm(128), rm(64), onesD, onesA, onesB, ones1, Cc, Sc], axis=1)

    def rope_tab(hd):
        t = np.arange(TS)
        row = (t // 64).astype(np.float64)
        col = (t % 64).astype(np.float64)
        nf = hd // 4
        inv = 10000.0 ** (-np.arange(nf, dtype=np.float64) / nf)
        inv = inv.astype(f32).astype(np.float64)
        ang = np.concatenate([row[:, None] * inv, row[:, None] * inv, col[:, None] * inv, col[:, None] * inv], axis=1)
        ang = ang.astype(f32)
        c = np.cos(ang).T.astype(f32)
        s = np.sin(ang).T.astype(f32)
        reps = 128 // hd
        return np.tile(c, (reps, 1)), np.tile(s, (reps, 1))

    cA, sA = rope_tab(128)
    cB, sB = rope_tab(64)
    rope = np.stack([cA, sA, cB, sB], axis=0).astype(f32)

    def dft(T):
        t = np.arange(T, dtype=np.int64)
        m = np.outer(t, t) % T
        a = 2 * np.pi * m / T
        return np.stack([np.cos(a) / np.sqrt(T), np.sin(a) / np.sqrt(T)], axis=0).astype(f32)

    return cmat.astype(f32), rope, dft(TS), dft(TP)


_CACHE = {}


def make_in_maps(inp, ncores=8, depth=4):
    f32 = np.float32
    g = lambda k: np.asarray(inp[k], dtype=f32)
    cmat, rope, dftS, dftP = _consts()
    b_adaT = np.ascontiguousarray(g("b_ada").reshape(4, 96, 128).transpose(0, 2, 1))
    gT = np.ascontiguousarray(np.stack([g("norm_mix_g").reshape(4, 16, 128), g("norm_mlp_g").reshape(4, 16, 128)], axis=1).transpose(0, 3, 1, 2))
    smallv = np.zeros((4, 128, 8), f32)
    smallv[:, :, 0] = g("q_norm_a")
    smallv[:, :, 1] = g("k_norm_a")
    smallv[:, :, 2] = np.tile(g("q_norm_b"), (1, 2))
    smallv[:, :, 3] = np.tile(g("k_norm_b"), (1, 2))
    smallv[:, :, 4] = g("subln_g")
    lam = np.stack([g("lambda_q1"), g("lambda_k1"), g("lambda_q2"), g("lambda_k2")], axis=1)
    lamv = np.ascontiguousarray(np.broadcast_to(lam[:, None], (4, 128, 4, 64)))
    L = depth
    shared = dict(w_ada=g("w_ada")[:L], b_adaT=b_adaT[:L], gT=gT[:L], w_in=g("w_in")[:L], w_out=g("w_out")[:L], w1=g("w_mlp_in")[:L],
                  w2=g("w_mlp_out")[:L], w_f=g("w_fourier")[:L], smallv=smallv[:L], lamv=lamv[:L], cmat=cmat, rope=rope, dftS=dftS, dftP=dftP)
    xs, xp = g("x_sample"), g("x_prompt")
    cak, cav, cbk, cbv = g("cache_attn_k"), g("cache_attn_v"), g("cache_diff_k"), g("cache_diff_v")
    c, cctx = g("c"), g("c_ctx")
    maps = []
    for i in range(ncores):
        cond = np.stack([c[i], cctx], axis=1)
        condT = np.ascontiguousarray(cond.reshape(16, 128, 2).transpose(1, 0, 2))
        m = dict(shared)
        m.update(x_s=np.ascontiguousarray(xs[i]), x_p=np.ascontiguousarray(xp[2 * i:2 * i + 2].reshape(2 * TP, D)),
                 cak=np.ascontiguousarray(cak[i].reshape(4, 512, 256)), cav=np.ascontiguousarray(cav[i].reshape(4, 512, 256)),
                 cbk=np.ascontiguousarray(cbk[i].reshape(4, 512, 512)), cbv=np.ascontiguousarray(cbv[i].reshape(4, 512, 512)),
                 condT=condT)
        maps.append(m)
    return maps


def assemble(results, ncores=8, depth=4):
    f32 = np.float32
    y_p = np.zeros((2 * ncores, TP, D), f32)
    y_s = np.zeros((ncores, TS, D), f32)
    sKA = np.zeros((2 * ncores, depth, TP, 2, 128), f32)
    sVA = np.zeros((2 * ncores, depth, TP, 2, 128), f32)
    sKB = np.zeros((2 * ncores, depth, TP, 4, 2, 64), f32)
    sVB = np.zeros((2 * ncores, depth, TP, 4, 128), f32)
    for i, r in enumerate(results):
        y_s[i] = r["y_s"]
        y_p[2 * i:2 * i + 2] = r["y_p"].reshape(2, TP, D)
        sKA[2 * i:2 * i + 2] = r["sKA"].reshape(2, depth, TP, 2, 128)
        sVA[2 * i:2 * i + 2] = r["sVA"].reshape(2, depth, TP, 2, 128)
        sKB[2 * i:2 * i + 2] = r["sKB"].reshape(2, depth, TP, 4, 2, 64)
        sVB[2 * i:2 * i + 2] = r["sVB"].reshape(2, depth, TP, 4, 128)
    return (y_p, y_s, sKA, sVA, sKB, sVB)


def kernel(**inputs):
    ncores, depth = 8, 4
    if "nc" not in _CACHE:
        _CACHE["nc"] = Prog(depth).build()
    nc = _CACHE["nc"]
    maps = make_in_maps(inputs, ncores, depth)
    res = run_bass_kernel_spmd(nc, maps, core_ids=list(range(ncores)))
    return assemble(res.results, ncores, depth)
```

```python
import math
import numpy as np
import concourse.bass as bass
import concourse.mybir as mybir
from concourse.bass_utils import run_bass_kernel_spmd

F32 = mybir.dt.float32
BF16 = mybir.dt.bfloat16
AF = mybir.ActivationFunctionType
ALU = mybir.AluOpType
AX = mybir.AxisListType
ENGS = ["pe", "act", "dve", "pool", "sp"]

D = 2048
NCH = 16
TS = 2048
TP = 256
NTOK = 2560
DFF = 8192
INW = 3584
EPS = 1e-6
SEM_EPOCH = 30000


class Buf:
    __slots__ = ("name", "w", "r")

    def __init__(self, name):
        self.name = name
        self.w = None
        self.r = []


class Sched:
    def __init__(self):
        self.ops = {e: [] for e in ENGS}
        self.dma_cnt = {}
        self.last_c = {e: None for e in ENGS}

    def op(self, eng, meth, *args, reads=(), writes=(), dma=None, **kw):
        fn = lambda e: getattr(e, meth)(*args, **kw)
        deps = []
        for b in reads:
            if b.w is not None:
                deps.append(b.w)
        for b in writes:
            if b.w is not None:
                deps.append(b.w)
            for t in b.r:
                if t[0] == "c" and t[1] == eng and dma is None and eng == "pe":
                    continue
                deps.append(t)
        idx = len(self.ops[eng])
        if dma is None:
            tok = ("c", eng, idx)
            if eng == "pe":
                deps = [t for t in deps if not (t[0] == "c" and t[1] == "pe")]
            self.last_c[eng] = tok
        else:
            n = self.dma_cnt.get(dma, 0) + 1
            self.dma_cnt[dma] = n
            tok = ("d", dma, n)
        self.ops[eng].append(dict(fn=fn, deps=deps, tok=tok, marked=False, dma=dma))
        for b in reads:
            b.r.append(tok)
        for b in writes:
            b.w = tok
            b.r = []
        return tok

    def barrier(self):
        toks = [t for t in self.last_c.values() if t is not None]
        toks += [("d", ch, n) for ch, n in self.dma_cnt.items()]
        for e in ENGS:
            self.ops[e].append(dict(fn=lambda eng: eng.nop(), deps=list(toks), tok=("n", e, len(self.ops[e])), marked=False, dma=None))

    def finalize(self):
        for e in ENGS:
            for o in self.ops[e]:
                for t in o["deps"]:
                    if t[0] == "c":
                        self.ops[t[1]][t[2]]["marked"] = True
        self.cnt = {}
        for e in ENGS:
            c = 0
            arr = []
            for o in self.ops[e]:
                if o["marked"]:
                    c += 1
                arr.append(c)
            self.cnt[e] = arr
        self.sem_keys = set()
        nw = 0
        for e in ENGS:
            known = {}
            for o in self.ops[e]:
                w = {}
                for t in o["deps"]:
                    if t[0] == "c":
                        c = self.cnt[t[1]][t[2]]
                        ep = (c - 1) // SEM_EPOCH
                        key = ("e", t[1], ep)
                        val = c - ep * SEM_EPOCH
                    else:
                        ep = (t[2] - 1) // 1500
                        key = ("d", t[1], ep)
                        val = 16 * (t[2] - ep * 1500)
                    if val > w.get(key, 0):
                        w[key] = val
                ws = []
                for key, val in w.items():
                    if val > known.get(key, 0):
                        known[key] = val
                        ws.append((key, val))
                        self.sem_keys.add(key)
                o["waits"] = ws
                nw += len(ws)
        for e in ENGS:
            for i, o in enumerate(self.ops[e]):
                if o["dma"] is not None:
                    n = o["tok"][2]
                    ep = (n - 1) // 1500
                    o["inc"] = ("d", o["dma"], ep)
                    self.sem_keys.add(o["inc"])
                elif o["marked"]:
                    c = self.cnt[e][i]
                    o["inc"] = ("e", e, (c - 1) // SEM_EPOCH)
                    self.sem_keys.add(o["inc"])
                else:
                    o["inc"] = None
        self.nwaits = nw

    def run(self, nc):
        from contextlib import ExitStack

        with ExitStack() as st:
            sems = {}
            for i, key in enumerate(sorted(self.sem_keys)):
                sems[key] = st.enter_context(nc.semaphore("s%d" % i))
            block = st.enter_context(nc.Block())

            def make(e):
                def body(eng):
                    for o in self.ops[e]:
                        for key, val in o["waits"]:
                            eng.wait_ge(sems[key], val)
                        ins = o["fn"](eng)
                        if o["inc"] is not None:
                            ins.then_inc(sems[o["inc"]], 16 if o["dma"] is not None else 1)

                return body

            block.tensor(make("pe"))
            block.scalar(make("act"))
            block.vector(make("dve"))
            block.gpsimd(make("pool"))
            block.sync(make("sp"))


class Rot:
    def __init__(self, items):
        self.items = items
        self.i = 0

    def next(self):
        it = self.items[self.i % len(self.items)]
        self.i += 1
        return it


SUBT = {0: [(0, 512, 0), (512, 512, 0), (1024, 256, 0)], 1: [(0, 256, 0), (256, 512, 0), (768, 512, 1)]}


class Prog:
    def __init__(self, depth):
        self.depth = depth
        nc = self.nc = bass.Bass("TRN2", target_bir_lowering=False)
        self.S = Sched()
        L = depth

        def din(name, shape, dt=F32):
            return nc.dram_tensor(name, list(shape), dt, kind="ExternalInput").ap()

        def dout(name, shape, dt=F32):
            return nc.dram_tensor(name, list(shape), dt, kind="ExternalOutput").ap()

        self.x_s = din("x_s", [TS, D])
        self.x_p = din("x_p", [2 * TP, D])
        self.cak = din("cak", [4, 512, 256])
        self.cav = din("cav", [4, 512, 256])
        self.cbk = din("cbk", [4, 512, 512])
        self.cbv = din("cbv", [4, 512, 512])
        self.condT = din("condT", [128, 16, 2])
        self.w_ada = din("w_ada", [L, D, 6 * D])
        self.b_adaT = din("b_adaT", [L, 128, 96])
        self.gT = din("gT", [L, 128, 2, 16])
        self.w_in = din("w_in", [L, D, INW])
        self.w_out = din("w_out", [L, D, D])
        self.w1 = din("w1", [L, D, DFF])
        self.w2 = din("w2", [L, DFF, D])
        self.w_f = din("w_f", [L, 4, 128, 128])
        self.smallv = din("smallv", [L, 128, 8])
        self.lamv = din("lamv", [L, 128, 4, 64])
        self.cmat = din("cmat", [128, 9, 128])
        self.rope = din("rope", [4, 128, TS])
        self.dftS = din("dftS", [2, TS, TS])
        self.dftP = din("dftP", [2, TP, TP])
        self.y_s = dout("y_s", [TS, D])
        self.y_p = dout("y_p", [2 * TP, D])
        self.sKA = dout("sKA", [2, L, TP, 256])
        self.sVA = dout("sVA", [2, L, TP, 256])
        self.sKB = dout("sKB", [2, L, TP, 512])
        self.sVB = dout("sVB", [2, L, TP, 512])
        self.xT = nc.dram_tensor("xT_scr", [NCH, 128, NTOK], F32, kind="Internal").ap()
        self.hT = nc.dram_tensor("hT_scr", [NCH, 128, NTOK], BF16, kind="Internal").ap()
        self.mixT = nc.dram_tensor("mixT_scr", [NCH, 128, NTOK], BF16, kind="Internal").ap()

        self.off = 16512

        def sb(name, shape, dt, at=None):
            nbytes = int(np.prod(shape[1:])) * (4 if dt == F32 else 2)
            nbytes = (nbytes + 31) // 32 * 32
            if at is None:
                o = self.off
                self.off += nbytes
            else:
                o = at
            assert o + nbytes <= 229344, (name, o, nbytes)
            return nc.alloc_sbuf_tensor_at(name, list(shape), dt, offset=o)

        self.cf = sb("cf", [128, 3, 128], F32)
        self.cb = sb("cb", [128, 6, 128], BF16)
        self.b_c = Buf("consts")
        self.epsc = sb("epsc", [128, 8], F32)
        self.condS = sb("condS", [128, 16, 2], BF16)
        self.condF = sb("condF", [128, 16, 2], F32)
        self.mod = [sb(f"mod{i}", [128, 96, 2], F32) for i in range(2)]
        self.a12 = [sb(f"a12{i}", [128, 2, 16, 2], F32) for i in range(2)]
        self.b_mod = [Buf("mod0"), Buf("mod1")]
        self.gTs = sb("gTs", [128, 2, 16], F32)
        self.badaS = sb("badaS", [128, 96], F32)
        self.smallS = sb("smallS", [128, 8], F32)
        self.lamS = sb("lamS", [128, 4, 64], F32)
        self.lamT = sb("lamT", [128, 2, 64], F32)
        self.lamR = sb("lamR", [128, 8], F32)
        self.b_small = Buf("small")
        self.b_ada = Buf("ada")
        self.adab = [sb(f"adab{i}", [128, 16, 128], BF16) for i in range(2)]
        self.b_adab = [Buf("adab0"), Buf("adab1")]
        self.ada_i = 0
        self.wfS = sb("wfS", [128, 4, 128], BF16)
        self.AB = sb("AB", [128, 4, 256], BF16)
        self.b_AB = Buf("AB")
        self.b_wf = Buf("wf")
        self.tF = [sb(f"tF{i}", [128, 512], F32) for i in range(6)]
        self.tB = [sb(f"tB{i}", [128, 512], BF16) for i in range(8)]
        self.rF = Rot([(t, Buf(f"tF{i}")) for i, t in enumerate(self.tF)])
        self.rB = Rot([(t, Buf(f"tB{i}")) for i, t in enumerate(self.tB)])
        s1 = self.off
        self.rOB = Rot([(sb(f"ob{i}", [128, 512], BF16), Buf(f"ob{i}")) for i in range(3)])
        self.rSO = Rot([(sb(f"so{i}", [128, 512], F32), Buf(f"so{i}")) for i in range(2)])
        self.rQT = Rot([(sb(f"qT{i}", [128, 512], BF16), Buf(f"qT{i}")) for i in range(3)])
        assert self.off - s1 >= 4096 + 5120
        self.rRN = Rot([(sb(f"rN{i}", [128, 512], F32, at=s1 + i * 2048), Buf(f"rN{i}")) for i in range(2)])
        self.ub1_at = s1 + 4096
        slab0 = self.off
        self.off += 4 * 8192
        self.big = [sb(f"big{j}", [128, 16, 512], BF16, at=slab0 + j * 16384) for j in range(2)]
        self.u16 = [sb(f"u16_{i}", [128, 16, 256], BF16, at=slab0 + i * 8192) for i in range(4)]
        self.u2 = [sb(f"u2_{i}", [128, 2, 2048], BF16, at=slab0 + i * 8192) for i in range(4)]
        self.b_unit = [Buf(f"unit{i}") for i in range(4)]
        self.big_i = 0
        self.unit_i = 0
        ov = self.off
        o = ov
        self.hseq = sb("hseq", [128, 16, TS], BF16, at=o); o += 16 * TS * 2
        self.KT = sb("KT", [128, 4, 2560], BF16, at=o)
        self.XAB = sb("XAB", [128, 16, 2, 512], BF16, at=o)
        self.Vs = sb("Vs", [128, 20, 512], BF16, at=o + 4 * 2560 * 2)
        o += 4 * 2560 * 2 + 20 * 512 * 2
        self.ropeT = [sb(f"ropeT{i}", [128, 4, 512], F32, at=o + i * 8192) for i in range(2)]
        o += 16384
        self.cstage = sb("cstage", [128, 4, 512], F32, at=o - 8192)
        assert o <= 229344, o
        self.b_hseq = Buf("hseq")
        self.b_kt = Buf("kt")
        self.b_v = Buf("v")
        self.b_rope = [Buf("rope0"), Buf("rope1")]
        self.b_cst = self.b_rope[1]
        self.rope_i = 0
        o = ov
        self.xs = sb("xs", [128, 16, 1280], F32, at=o); o += 16 * 1280 * 4
        self.h2 = sb("h2", [128, 16, 1280], BF16, at=o); o += 16 * 1280 * 2
        self.tm = [sb(f"tm{i}", [128, D], F32, at=o - 16 * 1280 * 2 + i * 10240) for i in range(2)]
        self.ub = [sb("ub0", [128, 2, 1280], BF16, at=o), sb("ub1", [128, 2, 1280], BF16, at=self.ub1_at)]; o += 2 * 1280 * 2
        assert o <= 229344, o
        self.b_xs = [Buf(f"xs{c}") for c in range(16)]
        self.b_h2 = [Buf(f"h2{c}") for c in range(16)]
        self.b_u = [[Buf("u00"), Buf("u01")], [Buf("u10"), Buf("u11")]]
        self.b_tm = [Buf("tm0"), Buf("tm1")]
        self.ps = [nc.alloc_psum_tensor(f"ps{i}", [128, 512], F32) for i in range(8)]
        self.b_ps = [Buf(f"ps{i}") for i in range(8)]
        self.rP = Rot([0, 1, 2])
        self.rQ = Rot([3])
        self.acc_i = 0
        self.rP2 = Rot([0, 1, 2, 4, 5, 6, 7])

    def bank(self):
        i = self.rP.next()
        return self.ps[i], self.b_ps[i]

    def bank2(self):
        i = self.rP2.next()
        return self.ps[i], self.b_ps[i]

    def bankq(self):
        i = self.rQ.next()
        return self.ps[i], self.b_ps[i]

    def load_big(self, src2d, kch, ncols):
        j = self.big_i % 2
        self.big_i += 1
        t = self.big[j]
        bufs = [self.b_unit[2 * j], self.b_unit[2 * j + 1]]
        self.S.op("pool", "dma_start", out=t[:, 0:kch, 0:ncols], in_=src2d.rearrange("(k p) n -> p k n", p=128),
                  writes=bufs, dma=f"unit{2*j}")
        return t, bufs

    def load_unit(self, src2d, view):
        i = self.unit_i % 4
        self.unit_i += 1
        t = self.u16[i] if view == 16 else self.u2[i]
        self.S.op("pool", "dma_start", out=t[:], in_=src2d.rearrange("(k p) n -> p k n", p=128),
                  writes=[self.b_unit[i]], dma=f"unit{i}")
        return t, [self.b_unit[i]]

    def rsqrt(self, rinv, brv, pq, bq, n):
        S = self.S
        S.op("act", "activation", out=rinv[:, 0:n], in_=pq[:, 0:n], func=AF.Ln, bias=self.epsc[:, 0:1], reads=[bq, self.b_c], writes=[brv])
        S.op("act", "activation", out=rinv[:, 0:n], in_=rinv[:, 0:n], func=AF.Exp, scale=-0.5, reads=[brv], writes=[brv])

    def setup(self):
        S = self.S
        S.op("dve", "memset", self.epsc[:], EPS, writes=[self.b_c])
        S.op("sp", "dma_start", out=self.cf[:], in_=self.cmat[:, 0:3, :], writes=[self.b_c], dma="cc")
        S.op("pool", "dma_start", out=self.cb[:], in_=self.cmat[:, 3:9, :], writes=[self.b_c], dma="ccp")
        S.op("sp", "dma_start", out=self.condF[:], in_=self.condT, writes=[self.b_small], dma="cs")
        S.op("act", "activation", out=self.condS[:], in_=self.condF[:], func=AF.Silu, reads=[self.b_small], writes=[self.b_c])

    def adaln(self, l):
        for _ in self.adaln_gen(l):
            pass

    def adaln_gen(self, l):
        S = self.S
        par = l % 2
        mod = self.mod[par]
        a12 = self.a12[par]
        S.op("sp", "dma_start", out=self.badaS[:], in_=self.b_adaT[l], writes=[self.b_ada], dma="cad")
        S.op("sp", "dma_start", out=self.gTs[:], in_=self.gT[l], writes=[self.b_ada], dma="cad")
        for j in range(96):
            i = self.ada_i % 2
            self.ada_i += 1
            ad, bad = self.adab[i], self.b_adab[i]
            S.op("pool", "dma_start", out=ad[:], in_=self.w_ada[l][:, j * 128:(j + 1) * 128].rearrange("(k p) n -> p k n", p=128),
                 writes=[bad], dma=f"adab{i}")
            p, bp = self.bankq()
            for k in range(16):
                S.op("pe", "matmul", p[:, 0:2], ad[:, k, :], self.condS[:, k, :], start=(k == 0), stop=(k == 15), reads=[bad, self.b_c], writes=[bp])
            S.op("dve", "tensor_scalar", out=mod[:, j, :], in0=p[:, 0:2], scalar1=self.badaS[:, j:j + 1], scalar2=None, op0=ALU.add,
                 reads=[bp, self.b_ada], writes=[self.b_mod[par]])
            yield
        for i in range(2):
            for w, (sc0, gi) in enumerate([(16, 0), (64, 1)]):
                S.op("dve", "scalar_tensor_tensor", out=a12[:, w, :, i], in0=mod[:, sc0:sc0 + 16, i], scalar=1.0, in1=self.gTs[:, gi, :],
                     op0=ALU.add, op1=ALU.mult, reads=[self.b_mod[par], self.b_ada], writes=[self.b_mod[par]])
        yield

    def mvec(self, l, which, c, ci):
        par = l % 2
        if which == "a1":
            return self.a12[par][:, 0, c, ci:ci + 1]
        if which == "a2":
            return self.a12[par][:, 1, c, ci:ci + 1]
        base = {"shift1": 0, "gate1": 32, "shift2": 48, "gate2": 80}[which]
        return self.mod[par][:, base + c, ci:ci + 1]

    def layer_small(self, l):
        S = self.S
        lam_init = 0.8 - 0.6 * math.exp(-0.3 * l)
        bs = self.b_small
        S.op("sp", "dma_start", out=self.smallS[:], in_=self.smallv[l], writes=[bs], dma="cs")
        S.op("sp", "dma_start", out=self.lamS[:], in_=self.lamv[l], writes=[bs], dma="cs")
        for j in range(2):
            S.op("dve", "tensor_tensor", out=self.lamT[:, j, :], in0=self.lamS[:, 2 * j, :], in1=self.lamS[:, 2 * j + 1, :], op=ALU.mult,
                 reads=[bs], writes=[bs])
            S.op("dve", "reduce_sum", out=self.lamR[:, j:j + 1], in_=self.lamT[:, j, :], axis=AX.X, reads=[bs], writes=[bs])
            S.op("act", "activation", out=self.lamR[:, 2 + j:3 + j], in_=self.lamR[:, j:j + 1], func=AF.Exp, reads=[bs], writes=[bs])
        S.op("dve", "scalar_tensor_tensor", out=self.lamR[:, 4:5], in0=self.lamR[:, 3:4], scalar=-lam_init, in1=self.lamR[:, 2:3],
                                                       op0=ALU.add, op1=ALU.subtract, reads=[bs], writes=[bs])
        S.op("dve", "tensor_scalar", out=self.lamR[:, 5:6], in0=self.smallS[:, 4:5], scalar1=1.0 - lam_init, scalar2=None, op0=ALU.mult,
             reads=[bs], writes=[bs])
        S.op("pool", "dma_start", out=self.wfS[:], in_=self.w_f[l].rearrange("g c d -> c g d"), writes=[self.b_wf], dma="cw")
        for g in range(4):
            for ab in range(2):
                p, bp = self.bankq()
                S.op("pe", "matmul", p[:, 0:128], self.cb[:, 4 + ab, :], self.wfS[:, g, :], start=True, stop=True,
                     reads=[self.b_c, self.b_wf], writes=[bp])
                S.op("act", "activation", out=self.AB[:, g, ab * 128:(ab + 1) * 128], in_=p[:, 0:128], func=AF.Copy,
                                                                     scale=(1.0 if ab == 0 else -1.0), reads=[bp], writes=[self.b_AB])

    def norm(self, st, l, kind):
        S = self.S
        aw, sw = ("a1", "shift1") if kind == 1 else ("a2", "shift2")
        g0 = st * 1280
        for (off, n, ci) in SUBT[st]:
            pq, bq = self.bankq()
            for c in range(16):
                sq, bsq = self.rB.next()
                S.op("act", "activation", out=sq[:, 0:n], in_=self.xs[:, c, off:off + n], func=AF.Square,
                     reads=[self.b_xs[c]], writes=[bsq])
                S.op("pe", "matmul", pq[:, 0:n], self.cb[:, 0, :], sq[:, 0:n], start=(c == 0), stop=(c == 15),
                     reads=[bsq, self.b_c], writes=[bq])
            rinv, brv = self.rRN.next()
            self.rsqrt(rinv, brv, pq, bq, n)
            for c in range(16):
                t, bt = self.rF.next()
                S.op("dve", "scalar_tensor_tensor", out=t[:, 0:n], in0=self.xs[:, c, off:off + n], scalar=self.mvec(l, aw, c, ci),
                                                                                 in1=rinv[:, 0:n], op0=ALU.mult, op1=ALU.mult,
                     reads=[self.b_xs[c], brv, self.b_mod[l % 2]], writes=[bt])
                S.op("act", "activation", out=self.h2[:, c, off:off + n], in_=t[:, 0:n], func=AF.Identity, bias=self.mvec(l, sw, c, ci),
                     reads=[bt, self.b_mod[l % 2]], writes=[self.b_h2[c]])
            if kind == 1:
                S.op("sp", "dma_start", out=self.hT[:, :, g0 + off:g0 + off + n].rearrange("c p n -> p c n"), in_=self.h2[:, :, off:off + n],
                     reads=self.b_h2, dma="h2io")

    def src_rows(self, g, n, sample, prompt):
        return sample[g:g + n, :] if g < TS else prompt[g - TS:g - TS + n, :]

    def stage0(self, st):
        S = self.S
        for b in range(10):
            g = st * 1280 + b * 128
            tm, btm = self.tm[b % 2], self.b_tm[b % 2]
            btmx = [btm] + self.b_h2[4 * (b % 2):4 * (b % 2) + 4]
            S.op("sp", "dma_start", out=tm[:], in_=self.src_rows(g, 128, self.x_s, self.x_p), writes=btmx, dma=f"tm{b%2}")
            for c4 in range(4):
                p, bp = self.bank()
                for cc in range(4):
                    c = c4 * 4 + cc
                    S.op("pe", "transpose", p[:, cc * 128:(cc + 1) * 128], tm[:, c * 128:(c + 1) * 128], self.cf[:, 0, :],
                         reads=btmx + [self.b_c], writes=[bp])
                eng = "act" if c4 % 2 == 0 else "dve"
                if eng == "act":
                    S.op("act", "activation", out=self.xs[:, c4 * 4:c4 * 4 + 4, b * 128:(b + 1) * 128],
                                                                       in_=p[:].rearrange("p (a n) -> p a n", a=4), func=AF.Copy,
                         reads=[bp], writes=self.b_xs[c4 * 4:c4 * 4 + 4])
                else:
                    S.op("dve", "tensor_copy", out=self.xs[:, c4 * 4:c4 * 4 + 4, b * 128:(b + 1) * 128],
                                                                        in_=p[:].rearrange("p (a n) -> p a n", a=4),
                         reads=[bp], writes=self.b_xs[c4 * 4:c4 * 4 + 4])

    def store_xs(self, st):
        g0 = st * 1280
        self.S.op("sp", "dma_start", out=self.xT[:, :, g0:g0 + 1280].rearrange("c p n -> p c n"), in_=self.xs[:], reads=self.b_xs, dma="xsio")

    def final_out(self, st):
        S = self.S
        for b in range(10):
            g = st * 1280 + b * 128
            tm, btm = self.tm[b % 2], self.b_tm[b % 2]
            btmx = [btm] + self.b_h2[4 * (b % 2):4 * (b % 2) + 4]
            for c4 in range(4):
                p, bp = self.bank()
                for cc in range(4):
                    c = c4 * 4 + cc
                    S.op("pe", "transpose", p[:, cc * 128:(cc + 1) * 128], self.xs[:, c, b * 128:(b + 1) * 128], self.cf[:, 0, :],
                         reads=[self.b_xs[c], self.b_c], writes=[bp])
                if c4 % 2 == 0:
                    S.op("act", "activation", out=tm[:, c4 * 512:(c4 + 1) * 512], in_=p[:], func=AF.Copy, reads=[bp], writes=btmx)
                else:
                    S.op("dve", "tensor_copy", out=tm[:, c4 * 512:(c4 + 1) * 512], in_=p[:], reads=[bp], writes=btmx)
            S.op("sp", "dma_start", out=self.src_rows(g, 128, self.y_s, self.y_p), in_=tm[:], reads=btmx, dma=f"tm{b%2}")

    def stage2(self, l, st):
        S = self.S
        g0 = st * 1280
        S.op("sp", "dma_start", out=self.xs[:], in_=self.xT[:, :, g0:g0 + 1280].rearrange("c p n -> p c n"), writes=self.b_xs, dma="xsio")
        S.op("sp", "dma_start", out=self.h2[:], in_=self.mixT[:, :, g0:g0 + 1280].rearrange("c p n -> p c n"), writes=self.b_h2, dma="h2io")
        for sl in range(4):
            big, bb = self.load_big(self.w_out[l][:, sl * 512:(sl + 1) * 512], 16, 512)
            for (off, n, ci) in SUBT[st]:
                for mm in range(4):
                    m = sl * 4 + mm
                    p, bp = self.bank()
                    for k in range(16):
                        S.op("pe", "matmul", p[:, 0:n], big[:, k, mm * 128:(mm + 1) * 128], self.h2[:, k, off:off + n],
                                                                               start=(k == 0), stop=(k == 15),
                             reads=bb + [self.b_h2[k]], writes=[bp])
                    S.op("dve", "scalar_tensor_tensor", out=self.xs[:, m, off:off + n], in0=p[:, 0:n], scalar=self.mvec(l, "gate1", m, ci),
                                                                          in1=self.xs[:, m, off:off + n], op0=ALU.mult, op1=ALU.add,
                         reads=[bp, self.b_xs[m], self.b_mod[l % 2]], writes=[self.b_xs[m]])
        self.norm(st, l, 2)
        nsl = DFF // 256
        subt = SUBT[st]
        wts = {}

        def load1(j):
            wts[j] = self.load_unit(self.w1[l][:, j * 256:(j + 1) * 256], 16)

        def load2(j):
            wts[j] = wts[j] + self.load_unit(self.w2[l][j * 256:(j + 1) * 256, :], 2)

        def load(j):
            load1(j)
            load2(j)

        def w1_group(j, off, n, hc):
            w1u, b1 = wts[j][0], wts[j][1]
            par = j % 2
            p, bp = self.bank2()
            for k in range(16):
                S.op("pe", "matmul", p[:, 0:n], w1u[:, k, hc * 128:(hc + 1) * 128], self.h2[:, k, off:off + n], start=(k == 0), stop=(k == 15),
                     reads=b1 + [self.b_h2[k]], writes=[bp])
            r, br = self.rF.next()
            S.op("act", "activation", out=r[:, 0:n], in_=p[:, 0:n], func=AF.Relu, reads=[bp], writes=[br])
            S.op("act", "activation", out=self.ub[par][:, hc, off:off + n], in_=r[:, 0:n], func=AF.Square, reads=[br], writes=[self.b_u[par][hc]])

        def w2_group(j, off, n, ci, m):
            _, _, w2u, b2 = wts[j]
            par = j % 2
            p, bp = self.bank2()
            for hc in range(2):
                S.op("pe", "matmul", p[:, 0:n], w2u[:, hc, m * 128:(m + 1) * 128], self.ub[par][:, hc, off:off + n], start=(hc == 0), stop=(hc == 1),
                     reads=b2 + [self.b_u[par][hc]], writes=[bp])
            S.op("dve", "scalar_tensor_tensor", out=self.xs[:, m, off:off + n], in0=p[:, 0:n], scalar=self.mvec(l, "gate2", m, ci),
                 in1=self.xs[:, m, off:off + n], op0=ALU.mult, op1=ALU.add, reads=[bp, self.b_xs[m], self.b_mod[l % 2]], writes=[self.b_xs[m]])

        g1 = [(off, n, hc) for (off, n, ci) in subt for hc in range(2)]
        g2 = [(off, n, ci, m) for (off, n, ci) in subt for m in range(16)]
        load(0)
        load(1)
        for a in g1:
            w1_group(0, *a)
        for j in range(nsl):
            if j + 2 < nsl:
                load1(j + 2)
            per = len(g2) // len(g1)
            for gi, a in enumerate(g1):
                if j + 1 < nsl:
                    w1_group(j + 1, *a)
                for b in g2[gi * per:(gi + 1) * per]:
                    w2_group(j, *b)
            wts.pop(j)
            if j + 2 < nsl:
                load2(j + 2)

    def post_qk(self, pz, bpz, n, gcol, ones_idx, rope, out_ap, out_bufs, outf=None):
        for _ in self.post_qk_gen(pz, bpz, n, gcol, ones_idx, rope, out_ap, out_bufs, outf):
            pass

    def post_qk_gen(self, pz, bpz, n, gcol, ones_idx, rope, out_ap, out_bufs, outf=None):
        S = self.S
        sq, bsq = self.rB.next()
        S.op("act", "activation", out=sq[:, 0:n], in_=pz[:, 0:n], func=AF.Square, reads=[bpz], writes=[bsq])
        zg, bzg = self.rF.next()
        S.op("act", "activation", out=zg[:, 0:n], in_=pz[:, 0:n], func=AF.Copy, scale=gcol, reads=[bpz, self.b_small], writes=[bzg])
        yield
        pq, bq = self.bankq()
        S.op("pe", "matmul", pq[:, 0:n], self.cb[:, ones_idx, :], sq[:, 0:n], start=True, stop=True, reads=[bsq, self.b_c], writes=[bq])
        rinv, brv = self.rF.next()
        self.rsqrt(rinv, brv, pq, bq, n)
        if rope is not None:
            ridx, cos, sin, brope = rope
            t1, bt1 = self.rF.next()
            S.op("pool", "tensor_tensor", out=t1[:, 0:n], in0=zg[:, 0:n], in1=cos, op=ALU.mult, reads=[bzg, brope], writes=[bt1])
            yield
            pr, bpr = self.bankq()
            S.op("pe", "matmul", pr[:, 0:n], self.cf[:, ridx, :], zg[:, 0:n], start=True, stop=True, reads=[bzg, self.b_c], writes=[bpr])
            t2, bt2 = self.rF.next()
            S.op("dve", "tensor_tensor", out=t2[:, 0:n], in0=pr[:, 0:n], in1=sin, op=ALU.mult, reads=[bpr, brope], writes=[bt2])
            S.op("dve", "tensor_tensor", out=t1[:, 0:n], in0=t1[:, 0:n], in1=t2[:, 0:n], op=ALU.add, reads=[bt1, bt2], writes=[bt1])
            S.op("dve", "tensor_tensor", out=out_ap, in0=t1[:, 0:n], in1=rinv[:, 0:n], op=ALU.mult, reads=[bt1, brv], writes=out_bufs)
        else:
            S.op("dve", "tensor_tensor", out=out_ap, in0=zg[:, 0:n], in1=rinv[:, 0:n], op=ALU.mult, reads=[bzg, brv], writes=out_bufs)
            if outf is not None:
                of, bof = outf
                S.op("dve", "tensor_tensor", out=of[:, 0:n], in0=zg[:, 0:n], in1=rinv[:, 0:n], op=ALU.mult, reads=[bzg, brv], writes=[bof])
        yield

    def load_rope(self, t):
        i = self.rope_i % 2
        self.rope_i += 1
        rt, br = self.ropeT[i], self.b_rope[i]
        self.S.op("sp", "dma_start", out=rt[:], in_=self.rope[:, :, t * 512:(t + 1) * 512].rearrange("f p n -> p f n"), writes=[br], dma=f"rope{i}")
        return rt, br

    def attention(self, kind, h, qT, bq, q0, n, kbs, dst_chunk, gcols, hook=None, bg=None):
        S = self.S
        ncomp = 1 if kind == "A" else 2
        scale = 128 ** -0.5 if kind == "A" else 64 ** -0.5
        ktc = h // 4 if kind == "A" else h
        vcol = (h // 4) * 128 if kind == "A" else h * 128
        oc = []
        nk = len(kbs)
        for comp in range(ncomp):
            ai = self.acc_i % 2
            self.acc_i += 1
            psO, bO = self.ps[4 + 2 * ai], self.b_ps[4 + 2 * ai]
            psS, bS = self.ps[5 + 2 * ai], self.b_ps[5 + 2 * ai]
            r0, r1 = (0, 128) if kind == "A" else (comp * 64, comp * 64 + 64)

            def pv(item):
                pT, bpT, kb, i = item
                S.op("pe", "matmul", psO[:, 0:n], self.Vs[:, kb, vcol:vcol + 128], pT[:, 0:n], start=(i == 0), stop=(i == nk - 1),
                     reads=[self.b_v, bpT], writes=[bO])
                S.op("pe", "matmul", psS[:, 0:n], self.cb[:, 3, :], pT[:, 0:n], start=(i == 0), stop=(i == nk - 1),
                     reads=[self.b_c, bpT], writes=[bS])

            pend = []
            for i, kb in enumerate(kbs):
                ps_s, bs = self.bank()
                S.op("pe", "matmul", ps_s[:, 0:n], self.KT[r0:r1, ktc, kb * 128:(kb + 1) * 128], qT[r0:r1, q0:q0 + n], start=True, stop=True,
                     reads=[self.b_kt, bq], writes=[bs])
                pT, bpT = self.rB.next()
                S.op("act", "activation", out=pT[:, 0:n], in_=ps_s[:, 0:n], func=AF.Exp, scale=scale, reads=[bs], writes=[bpT])
                pend.append((pT, bpT, kb, i))
                if len(pend) > 2:
                    pv(pend.pop(0))
                if hook is not None and comp == 0 and i in (1, 5, 9):
                    next(hook, None)
                if bg is not None and comp == 0 and i in (12, 16):
                    next(bg, None)
            for item in pend:
                pv(item)
            rs, brs = self.rF.next()
            S.op("dve", "reciprocal", out=rs[:, 0:n], in_=psS[:, 0:n], reads=[bS], writes=[brs])
            if kind == "A":
                ob, bob = self.rOB.next()
                S.op("dve", "tensor_tensor", out=ob[:, 0:n], in0=psO[:, 0:n], in1=rs[:, 0:n], op=ALU.mult, reads=[bO, brs], writes=[bob])
                S.op("sp", "dma_start", out=self.mixT[dst_chunk][:, gcols:gcols + n], in_=ob[:, 0:n], reads=[bob], dma=bob.name)
            else:
                o, bo = self.rF.next()
                S.op("dve", "tensor_tensor", out=o[:, 0:n], in0=psO[:, 0:n], in1=rs[:, 0:n], op=ALU.mult, reads=[bO, brs], writes=[bo])
                oc.append((o, bo))
        if kind == "B":
            (o0, b0), (o1, b1) = oc
            od, bod = self.rF.next()
            S.op("dve", "scalar_tensor_tensor", out=od[:, 0:n], in0=o1[:, 0:n], scalar=self.lamR[:, 4:5], in1=o0[:, 0:n], op0=ALU.mult, op1=ALU.add,
                 reads=[b0, b1, self.b_small], writes=[bod])
            sq, bsq = self.rB.next()
            S.op("act", "activation", out=sq[:, 0:n], in_=od[:, 0:n], func=AF.Square, reads=[bod], writes=[bsq])
            pq, bq2 = self.bankq()
            S.op("pe", "matmul", pq[:, 0:n], self.cb[:, 1, :], sq[:, 0:n], start=True, stop=True, reads=[bsq, self.b_c], writes=[bq2])
            rinv, brv = self.rF.next()
            self.rsqrt(rinv, brv, pq, bq2, n)
            ob, bob = self.rOB.next()
            S.op("dve", "scalar_tensor_tensor", out=ob[:, 0:n], in0=od[:, 0:n], scalar=self.lamR[:, 5:6], in1=rinv[:, 0:n], op0=ALU.mult, op1=ALU.mult,
                 reads=[bod, brv, self.b_small], writes=[bob])
            S.op("sp", "dma_start", out=self.mixT[dst_chunk][:, gcols:gcols + n], in_=ob[:, 0:n], reads=[bob], dma=bob.name)

    def stage1(self, l, grp, bg=None):
        S = self.S
        isS = grp == "S"
        tok0 = 0 if isS else TS
        ntile = 4 if isS else 1
        ntb = 16 if isS else 4
        S.op("sp", "dma_start", out=self.hseq[:, :, 0:ntile * 512], in_=self.hT[:, :, tok0:tok0 + ntile * 512].rearrange("c p n -> p c n"),
             writes=[self.b_hseq], dma="hseq")
        bh = self.b_hseq

        def proj_fm(big, bb, col0, t, n=512):
            p, bp = self.bank()
            for k in range(16):
                S.op("pe", "matmul", p[:, 0:n], big[:, k, col0:col0 + 128], self.hseq[:, k, t * 512:t * 512 + n], start=(k == 0), stop=(k == 15),
                     reads=bb + [bh], writes=[bp])
            return p, bp

        import os
        skip = os.environ.get("K_PSKIP", "") if not isS else ""
        big, bb = self.load_big(self.w_in[l][:, 3072:3584], 16, 512)
        for t in range(ntile if "F" not in skip else 0):
            for g in range(4):
                p, bp = proj_fm(big, bb, g * 128, t)
                ft, bft = self.rB.next()
                S.op("act", "activation", out=ft[:], in_=p[:], func=AF.Copy, reads=[bp], writes=[bft])
                for jb in range(4):
                    px, bpx = self.bank()
                    S.op("pe", "matmul", px[:, 0:256], ft[:, jb * 128:(jb + 1) * 128], self.AB[:, g, :], start=True, stop=True,
                         reads=[bft, self.b_AB], writes=[bpx])
                    tb = t * 4 + jb
                    S.op("dve", "tensor_copy", out=self.XAB[:, tb, :, g * 128:(g + 1) * 128],
                                                                          in_=px[:, 0:256].rearrange("p (a d) -> p a d", a=2),
                         reads=[bpx], writes=[self.b_kt, self.b_v])
        if "F" in skip:
            pass
        elif isS:
            for t in range(4):
                bc, bbc = self.load_big(self.dftS[0][:, t * 512:(t + 1) * 512], 16, 512)
                bsn, bbs = self.load_big(self.dftS[1][:, t * 512:(t + 1) * 512], 16, 512)
                for gd in range(4):
                    p, bp = self.bank()
                    for tb in range(16):
                        S.op("pe", "matmul", p[:], self.XAB[:, tb, 0, gd * 128:(gd + 1) * 128], bc[:, tb, :], start=(tb == 0), stop=False,
                             reads=bbc + [self.b_kt, self.b_v], writes=[bp])
                    for tb in range(16):
                        S.op("pe", "matmul", p[:], self.XAB[:, tb, 1, gd * 128:(gd + 1) * 128], bsn[:, tb, :], start=False, stop=(tb == 15),
                             reads=bbs + [self.b_kt, self.b_v], writes=[bp])
                    ob, bob = self.rOB.next()
                    S.op("act", "activation", out=ob[:], in_=p[:], func=AF.Copy, reads=[bp], writes=[bob])
                    S.op("sp", "dma_start", out=self.mixT[12 + gd][:, t * 512:(t + 1) * 512], in_=ob[:], reads=[bob], dma=bob.name)
        else:
            bc, bbc = self.load_big(self.dftP[0], 2, 256)
            bsn, bbs = self.load_big(self.dftP[1], 2, 256)
            for s in range(2):
                for gd in range(4):
                    p, bp = self.bank()
                    for b2 in range(2):
                        S.op("pe", "matmul", p[:, 0:256], self.XAB[:, s * 2 + b2, 0, gd * 128:(gd + 1) * 128], bc[:, b2, 0:256],
                                                                             start=(b2 == 0), stop=False, reads=bbc + [self.b_kt, self.b_v], writes=[bp])
                    for b2 in range(2):
                        S.op("pe", "matmul", p[:, 0:256], self.XAB[:, s * 2 + b2, 1, gd * 128:(gd + 1) * 128], bsn[:, b2, 0:256],
                                                                             start=False, stop=(b2 == 1), reads=bbs + [self.b_kt, self.b_v], writes=[bp])
                    ob, bob = self.rOB.next()
                    S.op("act", "activation", out=ob[:, 0:256], in_=p[:, 0:256], func=AF.Copy, reads=[bp], writes=[bob])
                    S.op("sp", "dma_start", out=self.mixT[12 + gd][:, TS + s * 256:TS + (s + 1) * 256], in_=ob[:, 0:256],
                         reads=[bob], dma=bob.name)

        for kind in ("A", "B"):
            if kind in skip:
                continue
            if kind == "A":
                kcol, nkc, vcol, nv, qcol, nq = 1024, 2, 1280, 256, 0, 8
                cK, cV, sK, sV = self.cak, self.cav, self.sKA, self.sVA
                gk, gq, ones_idx, ridx, rc = self.smallS[:, 1:2], self.smallS[:, 0:1], 1, 1, 0
            else:
                kcol, nkc, vcol, nv, qcol, nq = 2048, 4, 2560, 512, 1536, 4
                cK, cV, sK, sV = self.cbk, self.cbv, self.sKB, self.sVB
                gk, gq, ones_idx, ridx, rc = self.smallS[:, 3:4], self.smallS[:, 2:3], 2, 2, 2
            kw = nkc * 128
            if kind == "A":
                bigk, bbk = self.load_big(self.w_in[l][:, 1024:1536], 16, 512)
                bigv, bbv, kc0, vc0 = bigk, bbk, 0, 256
            else:
                bigk, bbk = self.load_big(self.w_in[l][:, 2048:2560], 16, 512)
                bigv, bbv = self.load_big(self.w_in[l][:, 2560:3072], 16, 512)
                kc0, vc0 = 0, 0
            if isS:
                S.op("pool", "dma_start", out=self.Vs[:, 16:20, 0:nv], in_=cV[l].rearrange("(b p) n -> p b n", p=128), writes=[self.b_v], dma="vc")
                S.op("sp", "dma_start", out=self.cstage[:, :, 0:kw], in_=cK[l].rearrange("(b p) n -> p b n", p=128), writes=[self.b_cst], dma="rope1")
                for ch in range(nkc):
                    p, bp = self.bank()
                    for b in range(4):
                        S.op("pe", "transpose", p[:, b * 128:(b + 1) * 128], self.cstage[:, b, ch * 128:(ch + 1) * 128], self.cf[:, 0, :],
                             reads=[self.b_cst, self.b_c], writes=[bp])
                    S.op("act", "activation", out=self.KT[:, ch, 2048:2560], in_=p[:], func=AF.Copy, reads=[bp], writes=[self.b_kt])
            for t in range(ntile):
                rope = None
                if isS:
                    rt, brt = self.load_rope(t)
                    rope = (ridx, rt[:, rc, :], rt[:, rc + 1, :], brt)
                for ch in range(nkc):
                    p, bp = proj_fm(bigk, bbk, kc0 + ch * 128, t)
                    outf = None
                    if not isS and "K" not in skip:
                        outf = self.rF.next()
                    self.post_qk(p, bp, 512, gk, ones_idx, rope, self.KT[:, ch, t * 512:(t + 1) * 512], [self.b_kt], outf=outf)
                    if not isS and "K" not in skip:
                        kf, bkf = outf
                        pt, bpt = self.bank()
                        for jb in range(4):
                            S.op("pe", "transpose", pt[:, jb * 128:(jb + 1) * 128], kf[:, jb * 128:(jb + 1) * 128], self.cf[:, 0, :],
                                 reads=[bkf, self.b_c], writes=[bpt])
                        stg, bstg = self.rSO.next()
                        S.op("act", "activation", out=stg[:], in_=pt[:], func=AF.Copy, reads=[bpt], writes=[bstg])
                        for jb in range(4):
                            S.op("sp", "dma_start", out=sK[jb // 2, l, (jb % 2) * 128:(jb % 2) * 128 + 128, ch * 128:(ch + 1) * 128],
                                                                                 in_=stg[:, jb * 128:(jb + 1) * 128], reads=[bstg], dma=bstg.name)
                for jb in range(4):
                    p, bp = self.bank()
                    tok = t * 512 + jb * 128
                    for k in range(16):
                        S.op("pe", "matmul", p[:, 0:nv], self.hseq[:, k, tok:tok + 128], bigv[:, k, vc0:vc0 + nv], start=(k == 0), stop=(k == 15),
                             reads=bbv + [bh], writes=[bp])
                    if isS:
                        S.op("act", "activation", out=self.Vs[:, t * 4 + jb, 0:nv], in_=p[:, 0:nv], func=AF.Copy, reads=[bp], writes=[self.b_v])
                    else:
                        stg, bstg = self.rSO.next()
                        S.op("dve", "tensor_copy", out=stg[:, 0:nv], in_=p[:, 0:nv], reads=[bp], writes=[bstg])
                        S.op("act", "activation", out=self.Vs[:, t * 4 + jb, 0:nv], in_=stg[:, 0:nv], func=AF.Copy, reads=[bstg], writes=[self.b_v])
                        S.op("sp", "dma_start", out=sV[jb // 2, l, (jb % 2) * 128:(jb % 2) * 128 + 128, :], in_=stg[:, 0:nv], reads=[bstg], dma=bstg.name)
            for qs in range(nq // 4):
                bigq, bbq = self.load_big(self.w_in[l][:, qcol + qs * 512:qcol + (qs + 1) * 512], 16, 512)
                ropes = {}

                def unit_gen(t, hh, holder):
                    if isS and t not in ropes:
                        rt, brt = self.load_rope(t)
                        ropes[t] = (ridx, rt[:, rc, :], rt[:, rc + 1, :], brt)
                    p, bp = proj_fm(bigq, bbq, hh * 128, t)
                    qT, bqT = self.rQT.next()
                    holder["q"] = (qT, bqT)
                    yield from self.post_qk_gen(p, bp, 512, gq, ones_idx, ropes.get(t), qT[:], [bqT])

                units = [(t, hh) for t in range(ntile) for hh in range(4)]
                hold = {}
                cur = unit_gen(units[0][0], units[0][1], hold)
                for _ in cur:
                    pass
                for ui, (t, hh) in enumerate(units):
                    h = qs * 4 + hh
                    qT, bqT = hold["q"]
                    nhold = {}
                    nxt = unit_gen(units[ui + 1][0], units[ui + 1][1], nhold) if ui + 1 < len(units) else None
                    dst = h if kind == "A" else 8 + h
                    if isS:
                        self.attention(kind, h, qT, bqT, 0, 512, list(range(20)), dst, t * 512, hook=nxt, bg=bg)
                    else:
                        for s in range(2):
                            self.attention(kind, h, qT, bqT, s * 256, 256, [2 * s, 2 * s + 1], dst, TS + s * 256, hook=None)
                    if nxt is not None:
                        for _ in nxt:
                            pass
                    hold = nhold

    def build(self, stop=None):
        import os
        S = self.S
        stop = int(os.environ.get("K_STOP", "999")) if stop is None else stop
        step = [0]

        def go():
            step[0] += 1
            return step[0] <= stop

        def body():
            if not go(): return
            self.setup()
            if not go(): return
            self.adaln(0)
            for st in range(2):
                if not go(): return
                self.stage0(st)
                if not go(): return
                self.norm(st, 0, 1)
                self.store_xs(st)
            S.barrier()
            for l in range(self.depth):
                if not go(): return
                self.layer_small(l)
                bg = self.adaln_gen(l + 1) if l + 1 < self.depth else None
                for grp in ("S", "P"):
                    if not go(): return
                    self.stage1(l, grp, bg=bg if grp == "S" else None)
                    if bg is not None:
                        for _ in bg:
                            pass
                S.barrier()
                for st in range(2):
                    if not go(): return
                    self.stage2(l, st)
                    if not go(): return
                    if l + 1 < self.depth:
                        self.norm(st, l + 1, 1)
                        self.store_xs(st)
                    else:
                        self.final_out(st)
                S.barrier()

        body()
        S.barrier()
        S.finalize()
        S.run(self.nc)
        return self.nc


def _consts():
    f32 = np.float32
    ident = np.eye(128, dtype=f32)

    def rm(hd):
        R = np.zeros((128, 128), f32)
        q = hd // 4
        for base in range(0, 128, hd):
            for half in range(2):
                b0 = base + half * (hd // 2)
                for i in range(q):
                    R[b0 + q + i, b0 + i] = -1.0
                    R[b0 + i, b0 + q + i] = 1.0
        return R

    onesD = np.full((128, 128), 1.0 / 2048, f32)
    onesA = np.full((128, 128), 1.0 / 128, f32)
    onesB = np.zeros((128, 128), f32)
    onesB[0:64, 0:64] = 1.0 / 64
    onesB[64:128, 64:128] = 1.0 / 64
    ones1 = np.ones((128, 128), f32)
    cd = np.arange(128)
    ang = 2 * np.pi * np.outer(cd, cd) / 128
    Cc = (np.cos(ang) / np.sqrt(128)).astype(f32)
    Sc = (np.sin(ang) / np.sqrt(128)).astype(f32)
    cmat = np.stack([ident, rm(128), rm(64), onesD, onesA, onesB, ones1, Cc, Sc], axis=1)

    def rope_tab(hd):
        t = np.arange(TS)
        row = (t // 64).astype(np.float64)
        col = (t % 64).astype(np.float64)
        nf = hd // 4
        inv = 10000.0 ** (-np.arange(nf, dtype=np.float64) / nf)
        inv = inv.astype(f32).astype(np.float64)
        ang = np.concatenate([row[:, None] * inv, row[:, None] * inv, col[:, None] * inv, col[:, None] * inv], axis=1)
        ang = ang.astype(f32)
        c = np.cos(ang).T.astype(f32)
        s = np.sin(ang).T.astype(f32)
        reps = 128 // hd
        return np.tile(c, (reps, 1)), np.tile(s, (reps, 1))

    cA, sA = rope_tab(128)
    cB, sB = rope_tab(64)
    rope = np.stack([cA, sA, cB, sB], axis=0).astype(f32)

    def dft(T):
        t = np.arange(T, dtype=np.int64)
        m = np.outer(t, t) % T
        a = 2 * np.pi * m / T
        return np.stack([np.cos(a) / np.sqrt(T), np.sin(a) / np.sqrt(T)], axis=0).astype(f32)

    return cmat.astype(f32), rope, dft(TS), dft(TP)


_CACHE = {}


def make_in_maps(inp, ncores=8, depth=4):
    f32 = np.float32
    g = lambda k: np.asarray(inp[k], dtype=f32)
    cmat, rope, dftS, dftP = _consts()
    b_adaT = np.ascontiguousarray(g("b_ada").reshape(4, 96, 128).transpose(0, 2, 1))
    gT = np.ascontiguousarray(np.stack([g("norm_mix_g").reshape(4, 16, 128), g("norm_mlp_g").reshape(4, 16, 128)], axis=1).transpose(0, 3, 1, 2))
    smallv = np.zeros((4, 128, 8), f32)
    smallv[:, :, 0] = g("q_norm_a")
    smallv[:, :, 1] = g("k_norm_a")
    smallv[:, :, 2] = np.tile(g("q_norm_b"), (1, 2))
    smallv[:, :, 3] = np.tile(g("k_norm_b"), (1, 2))
    smallv[:, :, 4] = g("subln_g")
    lam = np.stack([g("lambda_q1"), g("lambda_k1"), g("lambda_q2"), g("lambda_k2")], axis=1)
    lamv = np.ascontiguousarray(np.broadcast_to(lam[:, None], (4, 128, 4, 64)))
    L = depth
    shared = dict(w_ada=g("w_ada")[:L], b_adaT=b_adaT[:L], gT=gT[:L], w_in=g("w_in")[:L], w_out=g("w_out")[:L], w1=g("w_mlp_in")[:L],
                  w2=g("w_mlp_out")[:L], w_f=g("w_fourier")[:L], smallv=smallv[:L], lamv=lamv[:L], cmat=cmat, rope=rope, dftS=dftS, dftP=dftP)
    xs, xp = g("x_sample"), g("x_prompt")
    cak, cav, cbk, cbv = g("cache_attn_k"), g("cache_attn_v"), g("cache_diff_k"), g("cache_diff_v")
    c, cctx = g("c"), g("c_ctx")
    maps = []
    for i in range(ncores):
        cond = np.stack([c[i], cctx], axis=1)
        condT = np.ascontiguousarray(cond.reshape(16, 128, 2).transpose(1, 0, 2))
        m = dict(shared)
        m.update(x_s=np.ascontiguousarray(xs[i]), x_p=np.ascontiguousarray(xp[2 * i:2 * i + 2].reshape(2 * TP, D)),
                 cak=np.ascontiguousarray(cak[i].reshape(4, 512, 256)), cav=np.ascontiguousarray(cav[i].reshape(4, 512, 256)),
                 cbk=np.ascontiguousarray(cbk[i].reshape(4, 512, 512)), cbv=np.ascontiguousarray(cbv[i].reshape(4, 512, 512)),
                 condT=condT)
        maps.append(m)
    return maps


def assemble(results, ncores=8, depth=4):
    f32 = np.float32
    y_p = np.zeros((2 * ncores, TP, D), f32)
    y_s = np.zeros((ncores, TS, D), f32)
    sKA = np.zeros((2 * ncores, depth, TP, 2, 128), f32)
    sVA = np.zeros((2 * ncores, depth, TP, 2, 128), f32)
    sKB = np.zeros((2 * ncores, depth, TP, 4, 2, 64), f32)
    sVB = np.zeros((2 * ncores, depth, TP, 4, 128), f32)
    for i, r in enumerate(results):
        y_s[i] = r["y_s"]
        y_p[2 * i:2 * i + 2] = r["y_p"].reshape(2, TP, D)
        sKA[2 * i:2 * i + 2] = r["sKA"].reshape(2, depth, TP, 2, 128)
        sVA[2 * i:2 * i + 2] = r["sVA"].reshape(2, depth, TP, 2, 128)
        sKB[2 * i:2 * i + 2] = r["sKB"].reshape(2, depth, TP, 4, 2, 64)
        sVB[2 * i:2 * i + 2] = r["sVB"].reshape(2, depth, TP, 4, 128)
    return (y_p, y_s, sKA, sVA, sKB, sVB)


def kernel(**inputs):
    ncores, depth = 8, 4
    if "nc" not in _CACHE:
        _CACHE["nc"] = Prog(depth).build()
    nc = _CACHE["nc"]
    maps = make_in_maps(inputs, ncores, depth)
    res = run_bass_kernel_spmd(nc, maps, core_ids=list(range(ncores)))
    return assemble(res.results, ncores, depth)
```
